# Optimizing a Trainium2 kernel written in Bass

```python
import jax
import jax.numpy as jnp
from jax import lax

D_MODEL = 1024
BATCH = 32
SEQ = 256
DEPTH = 1
DEC_BATCH = 4
DEC_SEQ = 2048
PAST_LEN = 256

GRID_W = 64
D_MIX = D_MODEL
M_HEADS = 4
M_HEAD_DIM = 128
M_WIDTH = M_HEADS * M_HEAD_DIM
M_CHUNK = 64
A_HEADS = 8
A_NOPE = 64
A_ROPE = 32
A_QK = A_NOPE + A_ROPE
A_VDIM = 64
A_WIDTH = A_HEADS * A_VDIM
Q_LORA = 384
KV_LORA = 256
ROPE_BASE = 10000.0
D_FF = 4 * D_MODEL
EPS = 1e-6
Q_BLOCK = 128
IN_SIZES = (M_WIDTH, M_WIDTH, M_WIDTH, M_WIDTH, 4 * M_HEADS, Q_LORA, KV_LORA, A_ROPE)
N_IN = 4 * M_WIDTH + 4 * M_HEADS + Q_LORA + KV_LORA + A_ROPE

kernel_name = "hymba_mlstm_mla_prefix_diffusion_step"


def _rmsnorm(x, g):
    xf = x.astype(jnp.float32)
    y = xf * lax.rsqrt(jnp.mean(xf * xf, axis=-1, keepdims=True) + EPS)
    return (y * g.astype(jnp.float32)).astype(x.dtype)


def _split_cols(a, sizes):
    idx, acc = [], 0
    for s in sizes[:-1]:
        acc += s
        idx.append(acc)
    return jnp.split(a, idx, axis=-1)


def _rope_2d(rows):
    row = jnp.repeat(jnp.arange(rows, dtype=jnp.float32), GRID_W)
    col = jnp.tile(jnp.arange(GRID_W, dtype=jnp.float32), rows)
    half = A_ROPE // 2
    inv = ROPE_BASE ** (-jnp.arange(0, half, 2, dtype=jnp.float32) / half)
    ang = jnp.concatenate([row[:, None] * inv, col[:, None] * inv], axis=-1)
    return jnp.cos(ang), jnp.sin(ang)


def _apply_rope(x, cos, sin):
    x1, x2 = jnp.split(x, 2, axis=-1)
    c = cos[:, None, :].astype(x.dtype)
    s = sin[:, None, :].astype(x.dtype)
    return jnp.concatenate([x1 * c - x2 * s, x1 * s + x2 * c], axis=-1)


def _mlstm_chunkwise(q, k, v, ig, lf, C0, n0, m0):
    B, H, T, Dh = q.shape
    nc = T // M_CHUNK
    L = M_CHUNK

    def to_chunks(a):
        return jnp.moveaxis(a.reshape(B, H, nc, L, *a.shape[3:]), 2, 0)

    tril = jnp.tril(jnp.ones((L, L), dtype=bool))

    def step(carry, inp):
        C, n, m = carry
        qc, kc, vc, ic, fc = inp
        b = jnp.cumsum(fc, axis=-1)
        dmat = b[..., :, None] - b[..., None, :] + ic[..., None, :]
        dmat = jnp.where(tril, dmat, -jnp.inf)
        m_inter = b + m[..., None]
        m_t = jnp.maximum(m_inter, jnp.max(dmat, axis=-1))
        w_intra = jnp.exp(dmat - m_t[..., None])
        w_inter = jnp.exp(m_inter - m_t)
        s = jnp.einsum('bhtk,bhsk->bhts', qc, kc) * w_intra
        num = jnp.einsum('bhts,bhsv->bhtv', s, vc) + w_inter[..., None] * jnp.einsum('bhtk,bhkv->bhtv', qc, C)
        den = jnp.sum(s, axis=-1) + w_inter * jnp.einsum('bhtk,bhk->bht', qc, n)
        h = num / jnp.maximum(jnp.abs(den), jnp.exp(-m_t))[..., None]
        b_last = b[..., -1]
        log_w = b_last[..., None] - b + ic
        m_new = jnp.maximum(b_last + m, jnp.max(log_w, axis=-1))
        w = jnp.exp(log_w - m_new[..., None])
        decay = jnp.exp(b_last + m - m_new)
        C_new = decay[..., None, None] * C + jnp.einsum('bhs,bhsk,bhsv->bhkv', w, kc, vc)
        n_new = decay[..., None] * n + jnp.einsum('bhs,bhsk->bhk', w, kc)
        return (C_new, n_new, m_new), h

    carry0 = (C0.astype(jnp.float32), n0.astype(jnp.float32), m0.astype(jnp.float32))
    (C, n, m), hs = lax.scan(step, carry0, (to_chunks(q), to_chunks(k), to_chunks(v), to_chunks(ig), to_chunks(lf)))
    h = jnp.moveaxis(hs, 0, 2).reshape(B, H, T, Dh)
    return h, (C, n, m)


def _mlstm_bidir(q, k, v, ig, fg, C0, n0, m0):
    qh = jnp.transpose(q.astype(jnp.float32), (0, 2, 1, 3))
    kh = jnp.transpose(k.astype(jnp.float32), (0, 2, 1, 3)) * (M_HEAD_DIM ** -0.5)
    vh = jnp.transpose(v.astype(jnp.float32), (0, 2, 1, 3))
    ig = jnp.transpose(ig, (0, 2, 3, 1))
    lf = jax.nn.log_sigmoid(jnp.transpose(fg, (0, 2, 3, 1)))
    flip = lambda a: jnp.flip(a, axis=2)
    h_f, (Cf, nf, mf) = _mlstm_chunkwise(qh, kh, vh, ig[:, 0], lf[:, 0], C0[:, 0], n0[:, 0], m0[:, 0])
    h_b, (Cb, nb, mb) = _mlstm_chunkwise(flip(qh), flip(kh), flip(vh), flip(ig[:, 1]), flip(lf[:, 1]),
                                         C0[:, 1], n0[:, 1], m0[:, 1])
    h = jnp.transpose(h_f + flip(h_b), (0, 2, 1, 3))
    return h, jnp.stack([Cf, Cb], axis=1), jnp.stack([nf, nb], axis=1), jnp.stack([mf, mb], axis=1)


def _mla_queries(q_lat, q_lora_g, w_q_up, q_head_g, rope):
    B, T, _ = q_lat.shape
    q = (_rmsnorm(q_lat, q_lora_g) @ w_q_up).reshape(B, T, A_HEADS, A_QK)
    q = _rmsnorm(q, q_head_g)
    if rope is not None:
        q = jnp.concatenate([q[..., :A_NOPE], _apply_rope(q[..., A_NOPE:], *rope)], axis=-1)
    return q


def _mla_keys_values(ckv, k_rope, w_kv_up, k_head_g, rope):
    B, T, _ = ckv.shape
    kv = (ckv @ w_kv_up).reshape(B, T, A_HEADS, A_NOPE + A_VDIM)
    k_nope, v = kv[..., :A_NOPE], kv[..., A_NOPE:]
    k = jnp.concatenate([k_nope, jnp.broadcast_to(k_rope[:, :, None, :], (B, T, A_HEADS, A_ROPE))], axis=-1)
    k = _rmsnorm(k, k_head_g)
    if rope is not None:
        k = jnp.concatenate([k[..., :A_NOPE], _apply_rope(k[..., A_NOPE:], *rope)], axis=-1)
    return k, v


def _block_attention(q, k, v):
    B, Tq, H, Dq = q.shape
    nb = Tq // Q_BLOCK
    scale = Dq ** -0.5
    qb = jnp.moveaxis(q.astype(jnp.float32).reshape(B, nb, Q_BLOCK, H, Dq), 1, 0)
    kf = k.astype(jnp.float32)
    vf = v.astype(jnp.float32)

    def one(qi):
        s = jnp.einsum('bqhd,bkhd->bhqk', qi, kf) * scale
        p = jax.nn.softmax(s, axis=-1)
        return jnp.einsum('bhqk,bkhd->bqhd', p, vf)

    out = lax.map(one, qb)
    return jnp.moveaxis(out, 0, 1).reshape(B, Tq, H, v.shape[-1]).astype(q.dtype)


def _layer(x, cond, ctx_C, ctx_n, ctx_m, ctx_ckv, ctx_krope, rope,
           norm1_g, norm2_g, w_ada, b_ada, w_in, mlstm_gate_b, mlstm_norm_g,
           q_lora_g, kv_lora_g, w_q_up, w_kv_up, q_head_g, k_head_g, w_out, w_mlp_up, w_mlp_down):
    B, T, _ = x.shape
    mod = (jax.nn.silu(cond) @ w_ada + b_ada)[..., None, :]
    sh1, sc1, g1, sh2, sc2, g2 = jnp.split(mod, 6, axis=-1)
    h = _rmsnorm(x, norm1_g) * (1 + sc1) + sh1
    mq, mk, mv, mo, gates, q_lat, kv_lat, k_rope = _split_cols(h @ w_in, IN_SIZES)
    gates = (gates + mlstm_gate_b).astype(jnp.float32).reshape(B, T, 2, 2, M_HEADS)
    hm, C_new, n_new, m_new = _mlstm_bidir(mq.reshape(B, T, M_HEADS, M_HEAD_DIM),
                                           mk.reshape(B, T, M_HEADS, M_HEAD_DIM),
                                           mv.reshape(B, T, M_HEADS, M_HEAD_DIM),
                                           gates[:, :, :, 0], gates[:, :, :, 1], ctx_C, ctx_n, ctx_m)
    hm = _rmsnorm(hm, mlstm_norm_g.reshape(M_HEADS, M_HEAD_DIM)).astype(x.dtype)
    hm = (hm * jax.nn.sigmoid(mo.reshape(B, T, M_HEADS, M_HEAD_DIM))).reshape(B, T, M_WIDTH)
    ckv = _rmsnorm(kv_lat, kv_lora_g)
    q = _mla_queries(q_lat, q_lora_g, w_q_up, q_head_g, rope)
    k, v = _mla_keys_values(ckv, k_rope, w_kv_up, k_head_g, rope)
    if ctx_ckv is not None:
        kc, vc = _mla_keys_values(ctx_ckv, ctx_krope, w_kv_up, k_head_g, None)
        k = jnp.concatenate([k, kc.astype(k.dtype)], axis=1)
        v = jnp.concatenate([v, vc.astype(v.dtype)], axis=1)
    ha = _block_attention(q, k, v).reshape(B, T, A_WIDTH)
    x = x + g1 * (jnp.concatenate([hm, ha], axis=-1) @ w_out)
    h2 = _rmsnorm(x, norm2_g) * (1 + sc2) + sh2
    x = x + g2 * (jnp.square(jax.nn.relu(h2 @ w_mlp_up)) @ w_mlp_down)
    return x, (ckv, k_rope, C_new, n_new, m_new)


def setup_inputs(seed: int = 0) -> dict:
    key = jax.random.key(seed)
    ks = jax.random.split(key, 32)
    f32 = jnp.float32

    def nrm(k, shape, s):
        return s * jax.random.normal(k, shape, f32)

    L = DEPTH
    i_bias = nrm(ks[20], (L, 2, 1, M_HEADS), 0.1)
    f_bias = jnp.linspace(3.0, 6.0, M_HEADS, dtype=f32) + nrm(ks[21], (L, 2, 1, M_HEADS), 0.1)
    return {
        "x_prompt": nrm(ks[0], (BATCH, SEQ, D_MODEL), 1.0),
        "x_sample": nrm(ks[1], (DEC_BATCH, DEC_SEQ, D_MODEL), 1.0),
        "cache_mla_ckv": nrm(ks[2], (DEC_BATCH, L, PAST_LEN, KV_LORA), 1.0),
        "cache_mla_krope": nrm(ks[3], (DEC_BATCH, L, PAST_LEN, A_ROPE), 1.0),
        "state_mlstm_C": nrm(ks[4], (DEC_BATCH, L, 2, M_HEADS, M_HEAD_DIM, M_HEAD_DIM), 0.05),
        "state_mlstm_n": nrm(ks[5], (DEC_BATCH, L, 2, M_HEADS, M_HEAD_DIM), 0.5),
        "state_mlstm_m": jax.random.uniform(ks[6], (DEC_BATCH, L, 2, M_HEADS), f32, -1.0, 2.0),
        "c": nrm(ks[7], (DEC_BATCH, D_MODEL), 1.0),
        "c_ctx": nrm(ks[8], (D_MODEL,), 1.0),
        "norm1_g": 1.0 + nrm(ks[9], (L, D_MODEL), 0.02),
        "norm2_g": 1.0 + nrm(ks[10], (L, D_MODEL), 0.02),
        "w_ada": nrm(ks[11], (L, D_MODEL, 6 * D_MODEL), 0.5 * D_MODEL ** -0.5),
        "b_ada": nrm(ks[12], (L, 6 * D_MODEL), 0.02),
        "w_in": nrm(ks[13], (L, D_MODEL, N_IN), D_MODEL ** -0.5),
        "mlstm_gate_b": jnp.concatenate([i_bias, f_bias], axis=2).reshape(L, 4 * M_HEADS),
        "mlstm_norm_g": 1.0 + nrm(ks[14], (L, M_WIDTH), 0.02),
        "q_lora_g": 1.0 + nrm(ks[15], (L, Q_LORA), 0.02),
        "kv_lora_g": 1.0 + nrm(ks[16], (L, KV_LORA), 0.02),
        "w_q_up": nrm(ks[17], (L, Q_LORA, A_HEADS * A_QK), Q_LORA ** -0.5),
        "w_kv_up": nrm(ks[18], (L, KV_LORA, A_HEADS * (A_NOPE + A_VDIM)), KV_LORA ** -0.5),
        "q_head_g": 1.0 + nrm(ks[19], (L, A_QK), 0.02),
        "k_head_g": 1.0 + nrm(ks[22], (L, A_QK), 0.02),
        "w_out": nrm(ks[23], (L, D_MIX, D_MODEL), D_MIX ** -0.5),
        "w_mlp_up": nrm(ks[24], (L, D_MODEL, D_FF), D_MODEL ** -0.5),
        "w_mlp_down": nrm(ks[25], (L, D_FF, D_MODEL), D_FF ** -0.5),
    }


def reference(x_prompt, x_sample, cache_mla_ckv, cache_mla_krope, state_mlstm_C, state_mlstm_n, state_mlstm_m,
              c, c_ctx, norm1_g, norm2_g, w_ada, b_ada, w_in, mlstm_gate_b, mlstm_norm_g,
              q_lora_g, kv_lora_g, w_q_up, w_kv_up, q_head_g, k_head_g, w_out, w_mlp_up, w_mlp_down):
    Bp = x_prompt.shape[0]
    zero_C = jnp.zeros((Bp, 2, M_HEADS, M_HEAD_DIM, M_HEAD_DIM), jnp.float32)
    zero_n = jnp.zeros((Bp, 2, M_HEADS, M_HEAD_DIM), jnp.float32)
    zero_m = jnp.zeros((Bp, 2, M_HEADS), jnp.float32)
    y = x_prompt
    ckvs, kropes, Cs, ns, ms = [], [], [], [], []
    for l in range(DEPTH):
        y, (ckv, krope, Cl, nl, ml) = _layer(
            y, c_ctx, zero_C, zero_n, zero_m, None, None, None,
            norm1_g[l], norm2_g[l], w_ada[l], b_ada[l], w_in[l], mlstm_gate_b[l], mlstm_norm_g[l],
            q_lora_g[l], kv_lora_g[l], w_q_up[l], w_kv_up[l], q_head_g[l], k_head_g[l],
            w_out[l], w_mlp_up[l], w_mlp_down[l])
        ckvs.append(ckv)
        kropes.append(krope)
        Cs.append(Cl)
        ns.append(nl)
        ms.append(ml)
    rows = x_sample.shape[1] // GRID_W
    rope = _rope_2d(rows)
    z = x_sample
    for l in range(DEPTH):
        z, _ = _layer(
            z, c, state_mlstm_C[:, l], state_mlstm_n[:, l], state_mlstm_m[:, l],
            cache_mla_ckv[:, l], cache_mla_krope[:, l], rope,
            norm1_g[l], norm2_g[l], w_ada[l], b_ada[l], w_in[l], mlstm_gate_b[l], mlstm_norm_g[l],
            q_lora_g[l], kv_lora_g[l], w_q_up[l], w_kv_up[l], q_head_g[l], k_head_g[l],
            w_out[l], w_mlp_up[l], w_mlp_down[l])
    return (y, z, jnp.stack(ckvs, axis=1), jnp.stack(kropes, axis=1), jnp.stack(Cs, axis=1),
            jnp.stack(ns, axis=1), jnp.stack(ms, axis=1))
```

```python
import numpy as np
from contextlib import ExitStack, contextmanager
import concourse.bass as bass
import concourse.mybir as mybir
from concourse.bass_utils import run_bass_kernel_spmd

F32 = mybir.dt.float32
BF16 = mybir.dt.bfloat16
AF = mybir.ActivationFunctionType
ALU = mybir.AluOpType
AX = mybir.AxisListType
EPS = 1e-6
NEG = -30000.0


class T:
    def __init__(self, name):
        self.name = name
        self.w = []
        self.r = []
        self.dsem = None
        self.dcnt = 0


class FW:
    def __init__(self, nc, es):
        self.nc = nc
        self.es = es
        self.eng = {'pe': nc.tensor, 'act': nc.scalar, 'dve': nc.vector, 'pool': nc.gpsimd, 'sp': nc.sync}
        self.sem = {k: es.enter_context(nc.semaphore("s_" + k)) for k in ('pe', 'act', 'dve', 'pool')}
        self.cnt = {k: 0 for k in self.sem}
        self.waited = {k: {} for k in self.eng}
        self.dma_ts = []
        self.dsem_pool = []
        self.dead = False
        self.limit = None
        self.count2 = None
        self.sem_total = {}

    def _wait(self, e, deps):
        need = {}
        for (s, v) in deps:
            if v > need.get(id(s), (s, 0))[1]:
                need[id(s)] = (s, v)
        for (s, v) in need.values():
            if e == 'pe' and s is self.sem['pe']:
                continue
            if self.waited[e].get(id(s), 0) >= v:
                continue
            self.eng[e].wait_ge(s, v)
            self.waited[e][id(s)] = v

    def op(self, e, fn, reads=(), writes=()):
        if self.count2 is not None and self.limit is not None:
            self.count2 += 1
            if self.count2 > self.limit:
                self.dead = True
        if self.dead:
            return None
        deps = []
        for t in reads:
            deps += t.w
        for t in writes:
            deps += t.w
            deps += t.r
        self._wait(e, deps)
        ins = fn()
        self.cnt[e] += 1
        ins.then_inc(self.sem[e], 1)
        tok = (self.sem[e], self.cnt[e])
        for t in reads:
            t.r.append(tok)
            if len(t.r) > 24:
                t.r = self._compact(t.r)
        for t in writes:
            t.w = [tok]
            t.r = []
        return ins

    @staticmethod
    def _compact(lst):
        m = {}
        for (s, v) in lst:
            if v > m.get(id(s), (s, 0))[1]:
                m[id(s)] = (s, v)
        return list(m.values())

    def dma(self, q, out, in_, reads=(), writes=(), **kw):
        if self.dead:
            return None
        deps = []
        for t in reads:
            deps += t.w
        for t in writes:
            deps += t.w
            deps += t.r
        self._wait(q, deps)
        own = (writes[0] if writes else reads[0])
        if own.dsem is not None and own.dq != ('sw' if q == 'pool' else 'hw'):
            raise RuntimeError('mixed dma queues on tile ' + own.name)
        if own.dsem is None:
            own.dq = 'sw' if q == 'pool' else 'hw'
            pl = [x for x in self.dsem_pool if x[2] == own.dq]
            if pl:
                ent = pl[-1]
                self.dsem_pool.remove(ent)
                own.dsem, own.dcnt = ent[0], ent[1]
            else:
                own.dsem = self.es.enter_context(self.nc.semaphore("d%d" % len(self.dma_ts)))
                own.dcnt = 0
                self.dma_ts.append(own.dsem)
        if own.dcnt > 0:
            self._wait(q, [(own.dsem, own.dcnt)])
        own.dcnt += 16
        self.sem_total[id(own.dsem)] = (own.dsem, own.dcnt)
        ins = self.eng[q].dma_start(out=out, in_=in_, **kw)
        ins.then_inc(own.dsem, 16)
        tok = (own.dsem, own.dcnt)
        for t in reads:
            t.r.append(tok)
        for t in writes:
            t.w = [tok]
            t.r = []
        return ins

    def barrier(self):
        if self.dead:
            return
        toks = [(self.sem[e], self.cnt[e]) for e in self.sem if self.cnt[e] > 0] + list(self.sem_total.values())
        for e in ('pe', 'act', 'dve', 'pool', 'sp'):
            self._wait(e, toks)

    def release(self, t):
        if t.dsem is not None:
            self.dsem_pool.append((t.dsem, t.dcnt, t.dq))
            t.dsem = None

    def finish(self):
        for (sem, cnt) in self.sem_total.values():
            self.eng['sp'].wait_ge(sem, cnt)


class Buf:
    def __init__(self, es, nc, name, shape, dt, ntr=1, psum=False, fw=None):
        self.T = [T(name + str(i)) for i in range(ntr)]
        if fw is not None:
            for t in self.T:
                es.callback(fw.release, t)
        if psum:
            self.t = es.enter_context(nc.psum_tensor(name, shape, dt))
        else:
            self.t = es.enter_context(nc.sbuf_tensor(name, shape, dt))

    def __getitem__(self, k):
        return self.t[k]


class _Stop(Exception):
    pass


def build_program(stage=99):
    nc = bass.Bass("TRN2", target_bir_lowering=False)

    fwbox = [None]

    def chk(k):
        if stage <= k:
            fwbox[0].dead = True

    def din(name, shape):
        return nc.dram_tensor(name, shape, F32, kind="ExternalInput").ap()

    def dout(name, shape):
        return nc.dram_tensor(name, shape, F32, kind="ExternalOutput").ap()

    xp = din("xp", [1024, 1024])
    xs = din("xs", [2048, 1024])
    cvec = din("cvec", [2, 1024])
    w_ada = din("w_ada", [1024, 6144])
    b_ada = din("b_ada", [1, 6144])
    norm_g = din("norm_g", [2, 1024])
    w_in = din("w_in", [1024, 2720])
    w_gate = din("w_gate", [1024, 40])
    gate_b = din("gate_b", [40, 1])
    rope_cs = din("rope_cs", [2048, 64])
    ctx_ckv = din("ctx_ckv", [256, 256])
    ctx_kr = din("ctx_kr", [256, 32])
    st_C = din("st_C", [2, 4, 128, 128])
    st_n = din("st_n", [2, 4, 128])
    st_m = din("st_m", [2, 4])
    g_q = din("g_q", [1, 384])
    g_kv = din("g_kv", [1, 256])
    g_qh = din("g_qh", [1, 96])
    g_kh = din("g_kh", [1, 96])
    g_mn = din("g_mn", [1, 512])
    w_q_up = din("w_q_up", [384, 768])
    w_kv_up = din("w_kv_up", [256, 1024])
    w_out = din("w_out", [1024, 1024])
    w_up = din("w_up", [1024, 4096])
    w_dn = din("w_dn", [4096, 1024])
    c_ident = din("c_ident", [128, 128])
    c_mask = din("c_mask", [2, 128, 128])
    c_sel = din("c_sel", [4, 512])
    c_nsel = din("c_nsel", [4, 512])

    yp = dout("yp", [1024, 1024])
    ys = dout("ys", [1024, 1024])
    o_ckv = dout("o_ckv", [1024, 256])
    o_kr = dout("o_kr", [1024, 32])
    o_C = dout("o_C", [4, 2, 4, 128, 128])
    o_n = dout("o_n", [4, 2, 4, 128])
    o_m = dout("o_m", [4, 2, 4])

    with ExitStack() as es0:
        fw = FW(nc, es0)
        fwbox[0] = fw

        @contextmanager
        def scope():
            with ExitStack() as es_:
                yield es_
            fw.barrier()

        sfx = [""]

        def B(es, name, shape, dt, ntr=1):
            return Buf(es, nc, name + sfx[0], shape, dt, ntr, fw=fw)

        def P(es, name, shape, dt):
            return Buf(es, nc, name, shape, dt, 1, psum=True)

        def act(fn, r, w):
            return fw.op('act', fn, r, w)

        def dve(fn, r, w):
            return fw.op('dve', fn, r, w)

        def pool(fn, r, w):
            return fw.op('pool', fn, r, w)

        def pe(fn, r, w):
            return fw.op('pe', fn, r, w)

        def mm(out, lhsT, rhs, start, stop, r, w):
            return fw.op('pe', lambda: nc.tensor.matmul(out, lhsT=lhsT, rhs=rhs, start=start, stop=stop), r, w)

        def tr(out, in_, ident, r, w):
            return fw.op('pe', lambda: nc.tensor.transpose(out, in_, ident), r, w)

        pbig, ptr, psm = [], [], []
        rot = {'pbig': 0, 'ptr': 0, 'psm': 0}
        psum_es = [None]
        pgen = [0]

        def set_psum(nbig, ntr, nsm):
            assert 2 * nbig + ntr + nsm <= 8
            if psum_es[0] is not None:
                fw.barrier()
                psum_es[0].close()
            psum_es[0] = ExitStack()
            pgen[0] += 1
            g = pgen[0]
            pbig[:] = [P(psum_es[0], "pbig%d_%d" % (i, g), [128, 1024], F32) for i in range(nbig)]
            ptr[:] = [P(psum_es[0], "ptr%d_%d" % (i, g), [128, 1024], BF16) for i in range(ntr)]
            psm[:] = [P(psum_es[0], "psm%d_%d" % (i, g), [128, 512], F32) for i in range(nsm)]

        def nxt(kind):
            lst = {'pbig': pbig, 'ptr': ptr, 'psm': psm}[kind]
            rot[kind] = (rot[kind] + 1) % len(lst)
            return lst[rot[kind]]

        set_psum(2, 2, 2)

        identf = B(es0, "identf", [128, 128], F32)
        identb = B(es0, "identb", [128, 128], BF16)
        maskc = B(es0, "maskc", [128, 2, 128], F32)
        sel4 = B(es0, "sel4", [4, 512], F32)
        nsel4 = B(es0, "nsel4", [4, 512], F32)
        onesf = B(es0, "onesf", [128, 512], F32)
        junk = B(es0, "junk", [128, 1024], BF16)
        gqbc = B(es0, "gqbc", [128, 384], F32)
        gkvbc = B(es0, "gkvbc", [128, 256], F32)
        gqhbc = B(es0, "gqhbc", [128, 96], F32)
        gkhbc = B(es0, "gkhbc", [128, 96], F32)
        gmnbc = B(es0, "gmnbc", [128, 512], F32)
        gbias = B(es0, "gbias", [40, 2], F32)
        g1bc = B(es0, "g1bc", [128, 2, 1024], F32)
        g2bc = B(es0, "g2bc", [128, 2, 1024], F32)
        ab = B(es0, "ab", [128, 2, 4, 8], F32)

        fw.dma('sp', identf[:], c_ident, writes=identf.T)
        fw.dma('pool', identb[:], c_ident, writes=identb.T)
        fw.dma('sp', maskc[:], c_mask.rearrange("d s t -> s d t"), writes=maskc.T)
        fw.dma('sp', sel4[:], c_sel, writes=sel4.T)
        fw.dma('sp', nsel4[:], c_nsel, writes=nsel4.T)
        dve(lambda: nc.vector.memset(onesf[:], 1.0), [], onesf.T)
        fw.dma('sp', gqbc[:], g_q.partition_broadcast(128), writes=gqbc.T)
        fw.dma('sp', gkvbc[:], g_kv.partition_broadcast(128), writes=gkvbc.T)
        fw.dma('sp', gqhbc[:], g_qh.partition_broadcast(128), writes=gqhbc.T)
        fw.dma('sp', gkhbc[:], g_kh.partition_broadcast(128), writes=gkhbc.T)
        fw.dma('sp', gmnbc[:], g_mn.partition_broadcast(128), writes=gmnbc.T)
        fw.dma('sp', gbias[:, 0:1], gate_b, writes=gbias.T)
        dve(lambda: nc.vector.tensor_scalar(out=gbias[:, 1:2], in0=gbias[:, 0:1], scalar1=-1.0, scalar2=None, op0=ALU.mult), gbias.T, gbias.T)
        dve(lambda: nc.vector.tensor_scalar(out=gqhbc[:], in0=gqhbc[:], scalar1=float(96 ** -0.5), scalar2=None, op0=ALU.mult), gqhbc.T, gqhbc.T)

        with scope() as es:
            cT = B(es, "cT", [128, 2, 8], F32)
            ngT = B(es, "ngT", [128, 2, 8], F32)
            ce = B(es, "ce", [128, 2, 8], F32)
            scT = B(es, "scT", [128, 8, 33], BF16)
            bada = B(es, "bada", [33, 6144], F32)
            modrow = B(es, "modrow", [33, 6144], F32)
            wab = [B(es, "wab%d" % i, [128, 8, 512], BF16) for i in range(2)]
            for r in range(2):
                fw.dma('sp', cT[:, r, :], cvec[r].rearrange("(c p) -> p c", p=128), writes=cT.T, allow_slow_non_contiguous=True)
                fw.dma('sp', ngT[:, r, :], norm_g[r].rearrange("(c p) -> p c", p=128), writes=ngT.T, allow_slow_non_contiguous=True)
            dve(lambda: nc.vector.memset(bada[:], 0.0), [], bada.T)
            fw.dma('sp', bada[0:1, :], b_ada, writes=bada.T)
            fw.dma('sp', bada[32:33, :], b_ada, writes=bada.T)
            act(lambda: nc.scalar.activation(out=ce[:], in_=cT[:], func=AF.Exp, scale=-1.0), cT.T, ce.T)
            dve(lambda: nc.vector.tensor_scalar(out=ce[:], in0=ce[:], scalar1=1.0, scalar2=None, op0=ALU.add), ce.T, ce.T)
            dve(lambda: nc.vector.reciprocal(out=ce[:], in_=ce[:]), ce.T, ce.T)
            dve(lambda: nc.vector.memset(scT[:], 0.0), [], scT.T)
            dve(lambda: nc.vector.tensor_tensor(out=scT[:, :, 0], in0=cT[:, 0, :], in1=ce[:, 0, :], op=ALU.mult), cT.T + ce.T, scT.T)
            dve(lambda: nc.vector.tensor_tensor(out=scT[:, :, 32], in0=cT[:, 1, :], in1=ce[:, 1, :], op=ALU.mult), cT.T + ce.T, scT.T)
            wa_v = w_ada.rearrange("(c p) n -> p c n", p=128)
            fw.dma('pool', wab[0][:], wa_v[:, :, 0:512], writes=wab[0].T)
            for blk in range(12):
                wb = wab[blk % 2]
                if blk + 1 < 12:
                    fw.dma('pool', wab[(blk + 1) % 2][:], wa_v[:, :, (blk + 1) * 512:(blk + 2) * 512], writes=wab[(blk + 1) % 2].T)
                ps = nxt('psm')
                for c in range(8):
                    mm(ps[0:33, :], scT[:, c, :], wb[:, c, :], c == 0, c == 7, scT.T + wb.T, ps.T)
                dve(lambda: nc.vector.tensor_tensor(out=modrow[:, blk * 512:(blk + 1) * 512], in0=ps[0:33, :], in1=bada[:, blk * 512:(blk + 1) * 512], op=ALU.add), ps.T + bada.T, modrow.T)
            for cond in range(2):
                r = 32 * cond
                for (dst, off) in ((g1bc, 2048), (g2bc, 5120)):
                    for hh in range(2):
                        ps = nxt('psm')
                        mm(ps[:, :], onesf[r:r + 1, 0:128], modrow[r:r + 1, off + hh * 512: off + (hh + 1) * 512], True, True, onesf.T + modrow.T, ps.T)
                        act(lambda: nc.scalar.copy(out=dst[:, cond, hh * 512:(hh + 1) * 512], in_=ps[:, :]), ps.T, dst.T)
                ps = nxt('psm')
                for vi, off in enumerate((0, 1024, 3072, 4096)):
                    for c in range(8):
                        mm(ps[:, vi * 8 + c: vi * 8 + c + 1], modrow[r:r + 1, off + c * 128: off + (c + 1) * 128], onesf[r:r + 1, 0:1], True, True, onesf.T + modrow.T, ps.T)
                dve(lambda: nc.vector.scalar_tensor_tensor(out=ab[:, cond, 0, :], in0=ps[:, 8:16], scalar=1.0, in1=ngT[:, 0, :], op0=ALU.add, op1=ALU.mult), ps.T + ngT.T, ab.T)
                dve(lambda: nc.vector.tensor_copy(out=ab[:, cond, 1, :], in_=ps[:, 0:8]), ps.T, ab.T)
                dve(lambda: nc.vector.scalar_tensor_tensor(out=ab[:, cond, 2, :], in0=ps[:, 24:32], scalar=1.0, in1=ngT[:, 1, :], op0=ALU.add, op1=ALU.mult), ps.T + ngT.T, ab.T)
                dve(lambda: nc.vector.tensor_copy(out=ab[:, cond, 3, :], in_=ps[:, 16:24]), ps.T, ab.T)

        def rstd_from_mean(ss_ap, ss_T):
            act(lambda: nc.scalar.activation(out=ss_ap, in_=ss_ap, func=AF.Ln, bias=EPS, scale=1.0), ss_T, ss_T)
            act(lambda: nc.scalar.activation(out=ss_ap, in_=ss_ap, func=AF.Exp, scale=-0.5), ss_T, ss_T)

        def norm_to_hT(es_unused, xt_ap, xt_T, cond, which, dst_ap, dst_T, wk):
            ss, xn = wk
            act(lambda: nc.scalar.activation(out=junk[:], in_=xt_ap, func=AF.Square, scale=float(1024 ** -0.5), accum_out=ss[:, 0:1]), xt_T, junk.T + ss.T)
            rstd_from_mean(ss[:, 0:1], ss.T)
            dve(lambda: nc.vector.tensor_scalar(out=xn[:], in0=xt_ap, scalar1=ss[:, 0:1], scalar2=None, op0=ALU.mult), xt_T + ss.T, xn.T)
            pt = nxt('ptr')
            for c in range(8):
                tr(pt[:, c * 128:(c + 1) * 128], xn[:, c * 128:(c + 1) * 128], identb[:], xn.T + identb.T, pt.T)
            ptv = pt[:, :].rearrange("p (c t) -> p c t", c=8)
            a_ap = ab[:, cond, 2 * which, :].unsqueeze(2).to_broadcast([128, 8, 128])
            b_ap = ab[:, cond, 2 * which + 1, :].unsqueeze(2).to_broadcast([128, 8, 128])
            dve(lambda: nc.vector.tensor_tensor(out=dst_ap, in0=ptv, in1=a_ap, op=ALU.mult), pt.T + ab.T, dst_T)
            dve(lambda: nc.vector.tensor_tensor(out=dst_ap, in0=dst_ap, in1=b_ap, op=ALU.add), dst_T + ab.T, dst_T)

        try:
          for sj in ('S', 'P'):
              chk(0 if sj == 'S' else 10)
              isS = sj == 'S'
              sfx[0] = "_" + sj
              cond = 1 if isS else 0
              xin = xs if isS else xp
              yout = ys if isS else yp
              NT = 16 if isS else 8
              NOWN = 8
              NK = 18 if isS else 8
              if isS:
                  seqs = [dict(tiles=list(range(16)), own=list(range(8)), ktiles=list(range(18)))]
              else:
                  seqs = [dict(tiles=[2 * i, 2 * i + 1], own=[2 * i, 2 * i + 1], ktiles=[2 * i, 2 * i + 1]) for i in range(4)]
              with scope() as esj:
                  hmaT = B(esj, "hmaT", [128, 8, 1024], BF16, 8)
                  with scope() as es2:
                      qT = B(es2, "qT", [128, 4, 1024], BF16, 8)
                      kT = B(es2, "kT", [128, 4, 1024], BF16, 8)
                      ktok = B(es2, "ktok", [128, NT, 512], BF16, NT)
                      v1 = B(es2, "v1", [128, NT, 4, 130], BF16, NT)
                      motok = B(es2, "motok", [128, 8, 512], BF16, 8)
                      gall = B(es2, "gall", [40, NT * 128], F32, NT // 4)
                      dve(lambda: nc.vector.memset(v1[:, :, :, 128:129], 1.0), [], v1.T)
                      with scope() as es3:
                          qlnT = B(es3, "qlnT", [128, 3, 1024], BF16, 8)
                          ckvT = B(es3, "ckvT", [128, 2, NK * 128], BF16, NK)
                          kr = B(es3, "kr", [128, NK, 32], F32, NK)
                          with scope() as es4:
                              wi = B(es4, "wi", [128, 8, 2720], BF16, 8)
                              wg = B(es4, "wg", [128, 8, 40], BF16)
                              wi_v = w_in.rearrange("(c p) n -> p c n", p=128)
                              for c in range(8):
                                  fw.dma('pool', wi[:, c, :], wi_v[:, c, :], writes=[wi.T[c]])
                              fw.dma('pool', wg[:], w_gate.rearrange("(c p) n -> p c n", p=128), writes=wg.T)
                              NXB = 1 if isS else 2
                              xt = [B(es4, "xt%d" % i, [128, 1024], F32) for i in range(NXB)]
                              xnb = [B(es4, "xnb%d" % i, [128, 1024], BF16) for i in range(NXB)]
                              ssb = [B(es4, "ssb%d" % i, [128, 4], F32) for i in range(2)]
                              hTg = [B(es4, "hTg%d" % i, [128, 8, 512], BF16, 4) for i in range(2)]
                              latf = [B(es4, "latf%d" % i, [128, 672], F32) for i in range(NXB)]
                              lss = [B(es4, "lss%d" % i, [128, 2], F32) for i in range(2)]
                              qlnb = [B(es4, "qlnb%d" % i, [128, 384], BF16) for i in range(2)]
                              ckvf = [B(es4, "ckvf%d" % i, [128, 256], F32) for i in range(2)]
                              ckvb = [B(es4, "ckvb%d" % i, [128, 256], BF16) for i in range(2)]
                              NHT = 2

                              def norms(grp):
                                  hg_ = hTg[grp % NHT]
                                  for tt in range(4):
                                      ti = grp * 4 + tt
                                      k2 = ti % NXB
                                      fw.dma('sp', xt[k2][:], xin[ti * 128:(ti + 1) * 128, :], writes=xt[k2].T)
                                      norm_to_hT(None, xt[k2][:], xt[k2].T, cond, 0, hg_[:, :, tt * 128:(tt + 1) * 128], [hg_.T[tt]], (ssb[ti % 2], xnb[k2]))

                              def projs(grp):
                                  hg_ = hTg[grp % NHT]
                                  own_grp = (grp * 4) < NOWN
                                  if own_grp:
                                      for (dst, cb) in ((qT, 0), (kT, 512)):
                                          for h in range(4):
                                              ps = nxt('psm')
                                              for c in range(8):
                                                  mm(ps[:, :], wi[:, c, cb + h * 128: cb + (h + 1) * 128], hg_[:, c, :], c == 0, c == 7, [wi.T[c]] + hg_.T, ps.T)
                                              act(lambda: nc.scalar.mul(out=dst[:, h, grp * 512:(grp + 1) * 512], in_=ps[:, :], mul=(float(128 ** -0.5) if cb == 512 else 1.0)), ps.T, dst.T[grp * 4:(grp + 1) * 4])
                                  ps = nxt('psm')
                                  for c in range(8):
                                      mm(ps[0:40, :], wg[:, c, :], hg_[:, c, :], c == 0, c == 7, wg.T + hg_.T, ps.T)
                                  gs = slice(grp * 512, (grp + 1) * 512)
                                  act(lambda: nc.scalar.activation(out=gall[0:8, gs], in_=ps[0:8, :], func=AF.Identity, bias=gbias[0:8, 0:1], scale=1.0), ps.T + gbias.T, [gall.T[grp]])
                                  act(lambda: nc.scalar.activation(out=gall[32:40, gs], in_=ps[32:40, :], func=AF.Exp, bias=gbias[32:40, 1:2], scale=-1.0), ps.T + gbias.T, [gall.T[grp]])
                                  act(lambda: nc.scalar.activation(out=gall[32:40, gs], in_=gall[32:40, gs], func=AF.Ln, bias=1.0, scale=1.0), [gall.T[grp]], [gall.T[grp]])
                                  for tt in range(4):
                                      ti = grp * 4 + tt
                                      k2 = ti % 2
                                      lh = lambda c: hg_[:, c, tt * 128:(tt + 1) * 128]
                                      pb = nxt('pbig')
                                      for nb, cb in enumerate((512, 1024)):
                                          for c in range(8):
                                              mm(pb[:, nb * 512:(nb + 1) * 512], lh(c), wi[:, c, cb:cb + 512], c == 0, c == 7, [wi.T[c], hg_.T[tt]], pb.T)
                                      act(lambda: nc.scalar.mul(out=ktok[:, ti, :], in_=pb[:, 0:512], mul=float(128 ** -0.5)), pb.T, [ktok.T[ti]])
                                      act(lambda: nc.scalar.copy(out=v1[:, ti, :, 0:128], in_=pb[:, 512:1024].rearrange("p (h d) -> p h d", h=4)), pb.T, [v1.T[ti]])
                                      pb = nxt('pbig')
                                      if own_grp:
                                          for c in range(8):
                                              mm(pb[:, 0:512], lh(c), wi[:, c, 1536:2048], c == 0, c == 7, [wi.T[c], hg_.T[tt]], pb.T)
                                          act(lambda: nc.scalar.copy(out=motok[:, ti, :], in_=pb[:, 0:512]), pb.T, [motok.T[ti]])
                                          pb = nxt('pbig')
                                          for (o0, o1, cb) in ((0, 512, 2048), (512, 672, 2560)):
                                              for c in range(8):
                                                  mm(pb[:, o0:o1], lh(c), wi[:, c, cb:cb + (o1 - o0)], c == 0, c == 7, [wi.T[c], hg_.T[tt]], pb.T)
                                          dve(lambda: nc.vector.tensor_copy(out=latf[k2 % NXB][:], in_=pb[:, 0:672]), pb.T, latf[k2 % NXB].T)
                                      else:
                                          for c in range(8):
                                              mm(pb[:, 512:800], lh(c), wi[:, c, 2432:2720], c == 0, c == 7, [wi.T[c], hg_.T[tt]], pb.T)
                                          dve(lambda: nc.vector.tensor_copy(out=latf[k2 % NXB][:, 384:672], in_=pb[:, 512:800]), pb.T, latf[k2 % NXB].T)
                                      lf_ = latf[k2 % NXB]
                                      ls = lss[k2]
                                      if own_grp:
                                          act(lambda: nc.scalar.activation(out=junk[:, 0:384], in_=lf_[:, 0:384], func=AF.Square, scale=float(384 ** -0.5), accum_out=ls[:, 0:1]), lf_.T, junk.T + ls.T)
                                      else:
                                          dve(lambda: nc.vector.memset(ls[:, 0:1], 1.0), [], ls.T)
                                      act(lambda: nc.scalar.activation(out=junk[:, 0:256], in_=lf_[:, 384:640], func=AF.Square, scale=float(256 ** -0.5), accum_out=ls[:, 1:2]), lf_.T, junk.T + ls.T)
                                      rstd_from_mean(ls[:, 0:2], ls.T)
                                      if own_grp:
                                          dve(lambda: nc.vector.scalar_tensor_tensor(out=qlnb[k2][:], in0=lf_[:, 0:384], scalar=ls[:, 0:1], in1=gqbc[:], op0=ALU.mult, op1=ALU.mult), lf_.T + ls.T + gqbc.T, qlnb[k2].T)
                                          pt = nxt('ptr')
                                          for c in range(3):
                                              tr(pt[:, c * 128:(c + 1) * 128], qlnb[k2][:, c * 128:(c + 1) * 128], identb[:], qlnb[k2].T + identb.T, pt.T)
                                          act(lambda: nc.scalar.copy(out=qlnT[:, :, ti * 128:(ti + 1) * 128], in_=pt[:, 0:384].rearrange("p (c t) -> p c t", c=3)), pt.T, [qlnT.T[ti]])
                                      dve(lambda: nc.vector.scalar_tensor_tensor(out=ckvf[k2][:], in0=lf_[:, 384:640], scalar=ls[:, 1:2], in1=gkvbc[:], op0=ALU.mult, op1=ALU.mult), lf_.T + ls.T + gkvbc.T, ckvf[k2].T)
                                      pool(lambda: nc.gpsimd.tensor_copy(out=ckvb[k2][:], in_=ckvf[k2][:]), ckvf[k2].T, ckvb[k2].T)
                                      pool(lambda: nc.gpsimd.tensor_copy(out=kr[:, ti, :], in_=lf_[:, 640:672]), lf_.T, [kr.T[ti]])
                                      if not isS:
                                          fw.dma('sp', o_ckv[ti * 128:(ti + 1) * 128, :], ckvf[k2][:], reads=ckvf[k2].T)
                                          fw.dma('sp', o_kr[ti * 128:(ti + 1) * 128, :], lf_[:, 640:672], reads=lf_.T)
                                      pt = nxt('ptr')
                                      for c in range(2):
                                          tr(pt[:, c * 128:(c + 1) * 128], ckvb[k2][:, c * 128:(c + 1) * 128], identb[:], ckvb[k2].T + identb.T, pt.T)
                                      act(lambda: nc.scalar.copy(out=ckvT[:, :, ti * 128:(ti + 1) * 128], in_=pt[:, 0:256].rearrange("p (c t) -> p c t", c=2)), pt.T, [ckvT.T[ti]])
                              NG = NT // 4
                              if NHT == 2:
                                  norms(0)
                                  for grp in range(NG):
                                      if grp + 1 < NG:
                                          norms(grp + 1)
                                      projs(grp)
                              else:
                                  for grp in range(NG):
                                      norms(grp)
                                      projs(grp)
                              if isS:
                                  for j in range(2):
                                      ti = 16 + j
                                      fw.dma('sp', ckvf[j][:], ctx_ckv[j * 128:(j + 1) * 128, :], writes=ckvf[j].T)
                                      pool(lambda: nc.gpsimd.tensor_copy(out=ckvb[j][:], in_=ckvf[j][:]), ckvf[j].T, ckvb[j].T)
                                      fw.dma('sp', kr[:, ti, :], ctx_kr[j * 128:(j + 1) * 128, :], writes=[kr.T[ti]])
                                      pt = nxt('ptr')
                                      for c in range(2):
                                          tr(pt[:, c * 128:(c + 1) * 128], ckvb[j][:, c * 128:(c + 1) * 128], identb[:], ckvb[j].T + identb.T, pt.T)
                                      act(lambda: nc.scalar.copy(out=ckvT[:, :, ti * 128:(ti + 1) * 128], in_=pt[:, 0:256].rearrange("p (c t) -> p c t", c=2)), pt.T, [ckvT.T[ti]])

                          chk(1 + (0 if isS else 10))
                          set_psum(1, 1, 5)
                          with scope() as es5:
                              wq = B(es5, "wq", [128, 3, 768], BF16)
                              wkv = B(es5, "wkv", [128, 2, 1024], BF16)
                              fw.dma('pool', wq[:], w_q_up.rearrange("(c p) n -> p c n", p=128), writes=wq.T)
                              fw.dma('pool', wkv[:], w_kv_up.rearrange("(c p) n -> p c n", p=128), writes=wkv.T)
                              KT = B(es5, "KT", [128, 4, NK * 128], BF16, NK)
                              V1 = B(es5, "V1", [128, NK, 4, 128], BF16, NK)
                              ob = B(es5, "ob", [128, 2, 128], BF16)
                              rcpb = B(es5, "rcpb", [128, 512], F32)
                              QT = B(es5, "QT", [128, 4, 512], BF16, 4)
                              cs = B(es5, "cs", [128, 16, 64], F32)
                              kf = [B(es5, "kf%d" % i, [128, 2, 4, 96], F32) for i in range(2)]
                              sqf = B(es5, "sqf", [128, 2, 4, 96], F32)
                              knb = [B(es5, "knb%d" % i, [128, 2, 4, 128], BF16) for i in range(2)]
                              for kb_ in knb:
                                  dve(lambda: nc.vector.memset(kb_[:], 0.0), [], kb_.T)
                              hss = [B(es5, "hss%d" % i, [128, 8], F32) for i in range(2)]
                              rt = B(es5, "rt", [128, 2, 2, 4, 32], F32)
                              Eb = [B(es5, "Eb%d" % i, [128, 512], BF16) for i in range(3)]
                              dve(lambda: nc.vector.memset(V1[:], 0.0), [], V1.T)
                              dve(lambda: nc.vector.memset(ob[:], 0.0), [], ob.T)
                              dve(lambda: nc.vector.memset(ob[:, 0, 0:64], 1.0), [], ob.T)
                              dve(lambda: nc.vector.memset(ob[:, 1, 64:128], 1.0), [], ob.T)
                              if isS:
                                  fw.dma('sp', cs[:], rope_cs.rearrange("(j p) d -> p j d", p=128), writes=cs.T)
                              ecnt = [0]
                              stc = [0]
                              gq2 = B(es5, "gq2", [128, 96], F32)
                              dve(lambda: nc.vector.tensor_copy(out=gq2[:], in_=gqhbc[:]), gqhbc.T, gq2.T)
                              dve(lambda: nc.vector.tensor_tensor(out=gq2[:, 0:64], in0=gq2[:, 0:64], in1=gkhbc[:, 0:64], op=ALU.mult), gq2.T + gkhbc.T, gq2.T)

                              def norm_a(psv, src_T, kr_t0):
                                  i2 = ecnt[0] % 2
                                  ecnt[0] += 1
                                  f, s, ss_ = kf[i2], sqf, hss[i2]
                                  if kr_t0 is not None:
                                      dve(lambda: nc.vector.tensor_copy(out=f[:, :, :, 0:64], in_=psv[:, :, :, 0:64]), src_T, f.T)
                                      dve(lambda: nc.vector.tensor_copy(out=f[:, :, :, 64:96], in_=kr[:, kr_t0:kr_t0 + 2, :].unsqueeze(2).to_broadcast([128, 2, 4, 32])), [kr.T[kr_t0], kr.T[kr_t0 + 1]] + f.T, f.T)
                                  else:
                                      dve(lambda: nc.vector.tensor_copy(out=f[:], in_=psv), src_T, f.T)
                                  dve(lambda: nc.vector.tensor_tensor(out=s[:], in0=f[:], in1=f[:], op=ALU.mult), f.T, s.T)
                                  dve(lambda: nc.vector.tensor_reduce(out=ss_[:], in_=s[:].rearrange("p t h d -> p (t h) d"), axis=AX.X, op=ALU.add), s.T, ss_.T)
                                  act(lambda: nc.scalar.activation(out=ss_[:], in_=ss_[:], func=AF.Ln, bias=EPS, scale=1.0 / 96), ss_.T, ss_.T)
                                  act(lambda: nc.scalar.activation(out=ss_[:], in_=ss_[:], func=AF.Exp, scale=-0.5), ss_.T, ss_.T)
                                  return i2

                              def norm_b(i2, kind, rope_t0):
                                  f, o, ss_, r_ = kf[i2], knb[i2], hss[i2], rt
                                  f8 = f[:].rearrange("p t h d -> p (t h) d")
                                  o8 = o[:].rearrange("p t h d -> p (t h) d")
                                  tb = lambda a, b: cs[:, rope_t0:rope_t0 + 2, a:b].unsqueeze(2).to_broadcast([128, 2, 4, b - a])
                                  if kind == 'k':
                                      dve(lambda: nc.vector.tensor_tensor(out=o8[:, :, 0:64], in0=f8[:, :, 0:64], in1=ss_[:].unsqueeze(2).to_broadcast([128, 8, 64]), op=ALU.mult), f.T + ss_.T, o.T)
                                      dve(lambda: nc.vector.tensor_tensor(out=f8[:, :, 64:96], in0=f8[:, :, 64:96], in1=ss_[:].unsqueeze(2).to_broadcast([128, 8, 32]), op=ALU.mult), f.T + ss_.T, f.T)
                                      gkr = gkhbc[:, 64:96].unsqueeze(1).to_broadcast([128, 8, 32])
                                      if rope_t0 is None:
                                          dve(lambda: nc.vector.tensor_tensor(out=o8[:, :, 64:96], in0=f8[:, :, 64:96], in1=gkr, op=ALU.mult), f.T + gkhbc.T, o.T)
                                          return o
                                      dve(lambda: nc.vector.tensor_tensor(out=f8[:, :, 64:96], in0=f8[:, :, 64:96], in1=gkr, op=ALU.mult), f.T + gkhbc.T, f.T)
                                  else:
                                      dve(lambda: nc.vector.tensor_tensor(out=f8, in0=f8, in1=ss_[:].unsqueeze(2).to_broadcast([128, 8, 96]), op=ALU.mult), f.T + ss_.T, f.T)
                                      gb = gq2[:].unsqueeze(1).to_broadcast([128, 8, 96])
                                      if rope_t0 is None:
                                          dve(lambda: nc.vector.tensor_tensor(out=o8[:, :, 0:96], in0=f8, in1=gb, op=ALU.mult), f.T + gq2.T, o.T)
                                          return o
                                      dve(lambda: nc.vector.tensor_tensor(out=o8[:, :, 0:64], in0=f8[:, :, 0:64], in1=gq2[:, 0:64].unsqueeze(1).to_broadcast([128, 8, 64]), op=ALU.mult), f.T + gq2.T, o.T)
                                      dve(lambda: nc.vector.tensor_tensor(out=f8[:, :, 64:96], in0=f8[:, :, 64:96], in1=gq2[:, 64:96].unsqueeze(1).to_broadcast([128, 8, 32]), op=ALU.mult), f.T + gq2.T, f.T)
                                  dve(lambda: nc.vector.tensor_tensor(out=r_[:, 0], in0=f[:, :, :, 64:96], in1=tb(0, 32), op=ALU.mult), f.T + cs.T, r_.T)
                                  dve(lambda: nc.vector.tensor_tensor(out=r_[:, 1, :, :, 0:16], in0=f[:, :, :, 80:96], in1=tb(32, 48), op=ALU.mult), f.T + cs.T, r_.T)
                                  dve(lambda: nc.vector.tensor_tensor(out=r_[:, 1, :, :, 16:32], in0=f[:, :, :, 64:80], in1=tb(48, 64), op=ALU.mult), f.T + cs.T, r_.T)
                                  dve(lambda: nc.vector.tensor_tensor(out=o[:, :, :, 64:96], in0=r_[:, 0], in1=r_[:, 1], op=ALU.add), r_.T, o.T)
                                  return o

                              def to_T(o, dst_ap, dst_T):
                                  pt = nxt('ptr')
                                  for t in range(2):
                                      for h in range(4):
                                          j = t * 4 + h
                                          tr(pt[:, j * 128:(j + 1) * 128], o[:, t, h, :], identb[:], o.T + identb.T, pt.T)
                                  act(lambda: nc.scalar.copy(out=dst_ap, in_=pt[:, :].rearrange("p (t h k) -> p t h k", t=2, h=4)), pt.T, dst_T)

                              for hg in range(2):
                                  def k_a(kp):
                                      t0 = 2 * kp
                                      pb = nxt('pbig')
                                      for t in range(2):
                                          for c in range(2):
                                              mm(pb[:, t * 512:(t + 1) * 512], ckvT[:, c, (t0 + t) * 128:(t0 + t + 1) * 128], wkv[:, c, hg * 512:(hg + 1) * 512], c == 0, c == 1, [ckvT.T[t0 + t]] + wkv.T, pb.T)
                                      psv = pb[:, :].rearrange("p (t h d) -> p t h d", t=2, h=4)
                                      dve(lambda: nc.vector.tensor_copy(out=V1[:, t0:t0 + 2, 0::2, 0:64], in_=psv[:, :, 0::2, 64:128]), pb.T, [V1.T[t0], V1.T[t0 + 1]])
                                      dve(lambda: nc.vector.tensor_copy(out=V1[:, t0:t0 + 2, 1::2, 64:128], in_=psv[:, :, 1::2, 64:128]), pb.T, [V1.T[t0], V1.T[t0 + 1]])
                                      return norm_a(psv, pb.T, t0)

                                  def k_b(kp, i2):
                                      t0 = 2 * kp
                                      o = norm_b(i2, 'k', (t0 if (isS and t0 < 16) else None))
                                      to_T(o, KT[:, :, t0 * 128:(t0 + 2) * 128].rearrange("p h (t k) -> p t h k", t=2), [KT.T[t0], KT.T[t0 + 1]])

                                  chk(1.1 + (0 if isS else 10))
                                  st_ = k_a(0)
                                  for kp in range(NK // 2):
                                      nst = k_a(kp + 1) if kp + 1 < NK // 2 else None
                                      k_b(kp, st_)
                                      st_ = nst
                                  for qb in range(2):
                                      chk(1.4 + (0 if isS else 10))
                                      def q_a(tp):
                                          ti0 = qb * 4 + tp * 2
                                          pb = nxt('pbig')
                                          for t in range(2):
                                              for c in range(3):
                                                  mm(pb[:, t * 512:t * 512 + 384], qlnT[:, c, (ti0 + t) * 128:(ti0 + t + 1) * 128], wq[:, c, hg * 384:(hg + 1) * 384], c == 0, c == 2, [qlnT.T[ti0 + t]] + wq.T, pb.T)
                                          psv = pb[:, :].rearrange("p (t x) -> p t x", t=2)[:, :, 0:384].rearrange("p t (h d) -> p t h d", h=4)
                                          return norm_a(psv, pb.T, None)

                                      def q_b(tp, i2):
                                          ti0 = qb * 4 + tp * 2
                                          o = norm_b(i2, 'q', (ti0 if isS else None))
                                          to_T(o, QT[:, :, tp * 256:(tp + 1) * 256].rearrange("p h (t k) -> p t h k", t=2), [QT.T[tp * 2], QT.T[tp * 2 + 1]])

                                      s0 = q_a(0)
                                      s1 = q_a(1)
                                      q_b(0, s0)
                                      q_b(1, s1)
                                      if isS:
                                          sgroups = [(list(range(4)), list(range(18)))]
                                      else:
                                          sgroups = [([0, 1], [4 * qb, 4 * qb + 1]), ([2, 3], [4 * qb + 2, 4 * qb + 3])]
                                      chk(1.6 + (0 if isS else 10))
                                      stb = psm[0:3]
                                      pO, pDn = psm[3], psm[4]
                                      for c2 in range(2):
                                          for (qts, kts) in sgroups:
                                              q0, q1 = qts[0] * 128, (qts[-1] + 1) * 128
                                              items = [(hh, kt) for hh in (2 * c2, 2 * c2 + 1) for kt in kts]

                                              def emit_st(i_):
                                                  hh_, kt_ = items[i_]
                                                  ps_ = stb[stc[0] % 3]
                                                  stc[0] += 1
                                                  mm(ps_[:, q0:q1], KT[:, hh_, kt_ * 128:(kt_ + 1) * 128], QT[:, hh_, q0:q1], True, True, [KT.T[kt_]] + [QT.T[q] for q in qts], ps_.T)
                                                  return ps_
                                              DEPTH = 2
                                              pend = [emit_st(i_) for i_ in range(min(DEPTH, len(items)))]
                                              for i_, (hh, kt) in enumerate(items):
                                                  ps = pend.pop(0)
                                                  E = Eb[ecnt[0] % 3]
                                                  ecnt[0] += 1
                                                  act(lambda: nc.scalar.activation(out=E[:, q0:q1], in_=ps[:, q0:q1], func=AF.Exp), ps.T, E.T)
                                                  if i_ + DEPTH < len(items):
                                                      pend.append(emit_st(i_ + DEPTH))
                                                  fst, lst = (i_ == 0), (i_ == len(items) - 1)
                                                  mm(pO[:, q0:q1], V1[:, kt, hh, :], E[:, q0:q1], fst, lst, E.T + [V1.T[kt]], pO.T)
                                                  mm(pDn[:, q0:q1], ob[:, hh % 2, :], E[:, q0:q1], fst, lst, E.T + ob.T, pDn.T)
                                              act(lambda: nc.scalar.activation(out=rcpb[:, q0:q1], in_=pDn[:, q0:q1], func=AF.Ln), pDn.T, rcpb.T)
                                              act(lambda: nc.scalar.activation(out=rcpb[:, q0:q1], in_=rcpb[:, q0:q1], func=AF.Exp, scale=-1.0), rcpb.T, rcpb.T)
                                              tq0, tq1 = qb * 512 + q0, qb * 512 + q1
                                              dve(lambda: nc.vector.tensor_tensor(out=hmaT[:, 4 + 2 * hg + c2, tq0:tq1], in0=pO[:, q0:q1], in1=rcpb[:, q0:q1], op=ALU.mult), pO.T + rcpb.T, [hmaT.T[qb * 4 + q] for q in qts])
                          set_psum(2, 2, 2)
                      chk(2 + (0 if isS else 10))
                      set_psum(0, 1, 7)
                      with scope() as es6:
                          if isS:
                              chains = [dict(d=0, units=[(t, True) for t in range(8)], seq=0),
                                        dict(d=1, units=[(t, t < 8) for t in range(15, -1, -1)], seq=0)]
                          else:
                              chains = []
                              for i in range(4):
                                  chains.append(dict(d=0, units=[(2 * i, True), (2 * i + 1, True)], seq=i))
                                  chains.append(dict(d=1, units=[(2 * i + 1, True), (2 * i, True)], seq=i))
                          NCH = len(chains)
                          NB, NR = 3, 3
                          Cst = [B(es6, "Cst%d" % i, [128, 4, 129], F32) for i in range(NCH)]
                          Cbs = [B(es6, "Cbs%d" % i, [128, 4, 129], BF16) for i in range(NCH)]
                          mrow = [[B(es6, "mrow%d_%d" % (i, j), [4, 1], F32) for j in range(3)] for i in range(NCH)]
                          rowsM = B(es6, "rowsM", [4, 16, 128], F32, 16)
                          rowsG = B(es6, "rowsG", [4, 16, 128], F32, 16)
                          TSall = B(es6, "TSall", [128, 2 * NT, 16], F32, 2 * NT)
                          gi = [B(es6, "gi%d" % i, [4, 128], F32) for i in range(NR)]
                          gl = [B(es6, "gl%d" % i, [4, 128], F32) for i in range(NR)]
                          Ar = [B(es6, "Ar%d" % i, [4, 128], F32) for i in range(NR)]
                          gsc = [B(es6, "gsc%d" % i, [4, 128], F32) for i in range(NR)]
                          Msc = [B(es6, "Msc%d" % i, [4, 128], F32) for i in range(NR)]
                          R3 = [B(es6, "R3%d" % i, [4, 3, 128], F32) for i in range(NR)]
                          sm = [B(es6, "sm%d" % i, [4, 8], F32) for i in range(NR)]
                          dd = [B(es6, "dd%d" % i, [4, 4], F32) for i in range(NR)]
                          Yd = [B(es6, "Yd%d" % i, [4, 512], F32) for i in range(NB)]
                          Dm = [B(es6, "Dm%d" % i, [128, 512], F32) for i in range(NB)]
                          Pm = [B(es6, "Pm%d" % i, [128, 512], BF16) for i in range(NB)]
                          kw = [B(es6, "kw%d" % i, [128, 512], BF16) for i in range(NB)]
                          tmpn = [B(es6, "tmpn%d" % i, [128, 4, 129], F32) for i in range(NB)]
                          dn = [B(es6, "dn%d" % i, [128, 8], F32) for i in range(NB)]
                          hsum = B(es6, "hsum", [128, 8, 512], F32, 8)
                          htmp = B(es6, "htmp", [128, 512], F32)
                          hmtok = B(es6, "hmtok", [128, 512], BF16)
                          ones4 = B(es6, "ones4", [4, 128], F32)
                          dve(lambda: nc.vector.memset(ones4[:], 1.0), [], ones4.T)
                          rcnt = [0]
                          hcnt = [0]

                          def rows_a(ci, tile, full, cur):
                              d = chains[ci]['d']
                              k = rcnt[0] % NR
                              rcnt[0] += 1
                              mold, mnew = mrow[ci][cur], mrow[ci][(cur + 1) % 3]
                              rev = (d == 1)
                              r0 = d * 4
                              cs_ = slice(tile * 128, (tile + 1) * 128)
                              gT = [gall.T[tile // 4]]
                              fw.dma('sp', gi[k][:], gall[r0:r0 + 4, cs_], reads=gT, writes=gi[k].T)
                              fw.dma('sp', gl[k][:], gall[32 + r0:32 + r0 + 4, cs_], reads=gT, writes=gl[k].T)
                              V = (lambda ap: ap[:, ::-1]) if rev else (lambda ap: ap)
                              last = 0 if rev else 127
                              A_, R_, s_ = Ar[k], R3[k], sm[k]
                              if full:
                                  slot = d * 8 + tile
                                  g_ap, g_T = rowsG[:, slot, :], [rowsG.T[slot]]
                                  M_ap, M_T = rowsM[:, slot, :], [rowsM.T[slot]]
                              else:
                                  g_ap, g_T = gsc[k][:], gsc[k].T
                                  M_ap, M_T = Msc[k][:], Msc[k].T
                              dve(lambda: nc.vector.tensor_tensor_scan(out=V(A_[:]), data0=V(ones4[:]), data1=V(gl[k][:]), initial=0.0, op0=ALU.mult, op1=ALU.add), gl[k].T + ones4.T, A_.T)
                              dve(lambda: nc.vector.tensor_tensor(out=g_ap, in0=gi[k][:], in1=A_[:], op=ALU.add), gi[k].T + A_.T, g_T)
                              dve(lambda: nc.vector.tensor_tensor_scan(out=V(M_ap), data0=V(g_ap), data1=V(g_ap), initial=mold[:, 0:1], op0=ALU.max, op1=ALU.max), g_T + mold.T, M_T)
                              dve(lambda: nc.vector.tensor_tensor(out=mnew[:, 0:1], in0=M_ap[:, last:last + 1], in1=A_[:, last:last + 1], op=ALU.subtract), M_T + A_.T, mnew.T)
                              dve(lambda: nc.vector.tensor_scalar(out=s_[:, 0:1], in0=M_ap[:, last:last + 1], scalar1=-1.0, scalar2=None, op0=ALU.mult), M_T, s_.T)
                              return (ci, tile, full, cur, k, g_ap, g_T, M_ap, M_T)

                          def rows_b(stt):
                              ci, tile, full, cur, k, g_ap, g_T, M_ap, M_T = stt
                              d = chains[ci]['d']
                              mold = mrow[ci][cur]
                              A_, R_, s_ = Ar[k], R3[k], sm[k]
                              act(lambda: nc.scalar.activation(out=s_[:, 1:2], in_=mold[:, 0:1], func=AF.Exp, bias=s_[:, 0:1], scale=1.0), mold.T + s_.T, s_.T)
                              dve(lambda: nc.vector.tensor_tensor(out=R_[:, 0, :], in0=A_[:], in1=M_ap, op=ALU.subtract), A_.T + M_T, R_.T)
                              act(lambda: nc.scalar.activation(out=R_[:, 0, :], in_=R_[:, 0, :], func=AF.Exp), R_.T, R_.T)
                              act(lambda: nc.scalar.activation(out=R_[:, 1, :], in_=M_ap, func=AF.Exp, bias=mold[:, 0:1], scale=-1.0), M_T + mold.T, R_.T)
                              act(lambda: nc.scalar.activation(out=R_[:, 2, :], in_=g_ap, func=AF.Exp, bias=s_[:, 0:1], scale=1.0), g_T + s_.T, R_.T)
                              ps = nxt('psm')
                              for kk in range(3):
                                  tr(ps[:, kk * 4:(kk + 1) * 4], R_[:, kk, :], identf[0:4, 0:4], R_.T + identf.T, ps.T)
                              dve(lambda: nc.vector.tensor_scalar(out=dd[k][:], in0=identf[0:4, 0:4], scalar1=s_[:, 1:2], scalar2=None, op0=ALU.mult), identf.T + s_.T, dd[k].T)
                              mm(ps[:, 12:16], ones4[:, :], dd[k][:], True, True, ones4.T + dd[k].T, ps.T)
                              ti_ = d * NT + tile
                              dve(lambda: nc.vector.tensor_copy(out=TSall[:, ti_, :], in_=ps[:, 0:16]), ps.T, [TSall.T[ti_]])

                          PB = psm

                          def fa1(u):
                              ci, tile, full, seq_first, k = u
                              if not full:
                                  return
                              d = chains[ci]['d']
                              cs_ = slice(tile * 128, (tile + 1) * 128)
                              slot = d * 8 + tile
                              M_ap, M_T = rowsM[:, slot, :], [rowsM.T[slot]]
                              g_ap, g_T = rowsG[:, slot, :], [rowsG.T[slot]]
                              pD, pS = PB[0], PB[1]
                              dve(lambda: nc.vector.tensor_tensor(out=Yd[k][:].rearrange("k (h t) -> k h t", h=4), in0=nsel4[:].rearrange("k (h t) -> k h t", h=4), in1=M_ap.unsqueeze(1).to_broadcast([4, 4, 128]), op=ALU.mult), nsel4.T + M_T, Yd[k].T)
                              mm(pD[:, :], ones4[:, :], Yd[k][:], True, False, ones4.T + Yd[k].T, pD.T)
                              mm(pD[:, :], g_ap, sel4[:], False, True, g_T + sel4.T, pD.T)
                              for h in range(4):
                                  mm(pS[:, h * 128:(h + 1) * 128], kT[:, h, cs_], qT[:, h, cs_], True, True, [kT.T[tile], qT.T[tile]], pS.T)
                              dve(lambda: nc.vector.tensor_tensor(out=Dm[k][:].rearrange("p (h t) -> p h t", h=4), in0=pD[:, :].rearrange("p (h t) -> p h t", h=4), in1=maskc[:, d, :].unsqueeze(1).to_broadcast([128, 4, 128]), op=ALU.add), pD.T + maskc.T, Dm[k].T)
                              act(lambda: nc.scalar.activation(out=Dm[k][:], in_=Dm[k][:], func=AF.Exp), Dm[k].T, Dm[k].T)
                              dve(lambda: nc.vector.tensor_tensor(out=Pm[k][:], in0=pS[:, :], in1=Dm[k][:], op=ALU.mult), pS.T + Dm[k].T, Pm[k].T)

                          def fa2(u):
                              ci, tile, full, seq_first, k = u
                              if not full:
                                  return
                              pI = [PB[0], PB[2]]
                              for h in range(4):
                                  pq = pI[h // 2]
                                  mm(pq[:, (h % 2) * 129:(h % 2) * 129 + 129], Pm[k][:, h * 128:(h + 1) * 128], v1[:, tile, h, 0:129], True, True, Pm[k].T + [v1.T[tile]], pq.T)
                              for j in range(2):
                                  act(lambda: nc.scalar.copy(out=tmpn[k][:, 2 * j:2 * j + 2, :], in_=pI[j][:, 0:258].rearrange("p (h c) -> p h c", h=2)), pI[j].T, tmpn[k].T)

                          def back(u):
                              ci, tile, full, seq_first, k = u
                              d = chains[ci]['d']
                              ti_ = d * NT + tile
                              tsT = [TSall.T[ti_]]
                              ts = lambda a, b: TSall[:, ti_, a:b]
                              C_, Cb_ = Cst[ci], Cbs[ci]
                              cs_ = slice(tile * 128, (tile + 1) * 128)
                              pX = [PB[3], PB[4]]
                              pU = [PB[5], PB[6]]
                              if full:
                                  for h in range(4):
                                      pq = pX[h // 2]
                                      mm(pq[:, (h % 2) * 129:(h % 2) * 129 + 129], qT[:, h, cs_], Cb_[:, h, :], True, True, [qT.T[tile]] + Cb_.T, pq.T)
                                  for h in range(4):
                                      pq = pX[h // 2]
                                      dve(lambda: nc.vector.scalar_tensor_tensor(out=tmpn[k][:, h, :], in0=pq[:, (h % 2) * 129:(h % 2) * 129 + 129], scalar=ts(4 + h, 5 + h), in1=tmpn[k][:, h, :], op0=ALU.mult, op1=ALU.add), pq.T + tsT + tmpn[k].T, tmpn[k].T)
                                  den = tmpn[k][:, :, 128]
                                  dve(lambda: nc.vector.scalar_tensor_tensor(out=dn[k][:, 0:4], in0=den, scalar=-1.0, in1=den, op0=ALU.mult, op1=ALU.max), tmpn[k].T, dn[k].T)
                                  dve(lambda: nc.vector.tensor_tensor(out=dn[k][:, 0:4], in0=dn[k][:, 0:4], in1=ts(0, 4), op=ALU.max), dn[k].T + tsT, dn[k].T)
                                  dve(lambda: nc.vector.reciprocal(out=dn[k][:, 4:8], in_=dn[k][:, 0:4]), dn[k].T, dn[k].T)
                                  rb = dn[k][:, 4:8].unsqueeze(2).to_broadcast([128, 4, 128])
                                  hv = hsum[:, tile, :].rearrange("p (h c) -> p h c", h=4)
                                  if seq_first:
                                      dve(lambda: nc.vector.tensor_tensor(out=hv, in0=tmpn[k][:, :, 0:128], in1=rb, op=ALU.mult), tmpn[k].T + dn[k].T, [hsum.T[tile]])
                                  else:
                                      dve(lambda: nc.vector.tensor_tensor(out=htmp[:].rearrange("p (h c) -> p h c", h=4), in0=tmpn[k][:, :, 0:128], in1=rb, op=ALU.mult), tmpn[k].T + dn[k].T, htmp.T)
                                      dve(lambda: nc.vector.tensor_tensor(out=hsum[:, tile, :], in0=hsum[:, tile, :], in1=htmp[:], op=ALU.add), htmp.T + [hsum.T[tile]], [hsum.T[tile]])
                              pool(lambda: nc.gpsimd.tensor_tensor(out=kw[k][:].rearrange("p (h c) -> p h c", h=4), in0=ktok[:, tile, :].rearrange("p (h c) -> p h c", h=4), in1=ts(8, 12).unsqueeze(2).to_broadcast([128, 4, 128]), op=ALU.mult), [ktok.T[tile]] + tsT, kw[k].T)
                              for h in range(4):
                                  pq = pU[h // 2]
                                  mm(pq[:, (h % 2) * 129:(h % 2) * 129 + 129], kw[k][:, h * 128:(h + 1) * 128], v1[:, tile, h, 0:129], True, True, kw[k].T + [v1.T[tile]], pq.T)
                              for h in range(4):
                                  pq = pU[h // 2]
                                  dve(lambda: nc.vector.scalar_tensor_tensor(out=C_[:, h, :], in0=C_[:, h, :], scalar=ts(12 + h, 13 + h), in1=pq[:, (h % 2) * 129:(h % 2) * 129 + 129], op0=ALU.mult, op1=ALU.add), C_.T + tsT + pq.T, C_.T)
                              act(lambda: nc.scalar.copy(out=Cb_[:], in_=C_[:]), C_.T, Cb_.T)

                          fsq = [B(es6, "fsq%d" % i, [128, 512], F32) for i in range(2)]
                          sigs = [B(es6, "sigs%d" % i, [128, 512], F32) for i in range(3)]
                          hs2s = [B(es6, "hs2s%d" % i, [128, 4], F32) for i in range(2)]

                          def fz0(tile, j):
                              pool(lambda: nc.gpsimd.tensor_tensor(out=fsq[j % 2][:], in0=hsum[:, tile, :], in1=hsum[:, tile, :], op=ALU.mult), [hsum.T[tile]], fsq[j % 2].T)
                              sg = sigs[j % 3]
                              act(lambda: nc.scalar.activation(out=sg[:], in_=motok[:, tile, :], func=AF.Exp, scale=-1.0), [motok.T[tile]], sg.T)
                              act(lambda: nc.scalar.activation(out=sg[:], in_=sg[:], func=AF.Ln, bias=1.0, scale=1.0), sg.T, sg.T)
                              act(lambda: nc.scalar.activation(out=sg[:], in_=sg[:], func=AF.Exp, scale=-1.0), sg.T, sg.T)

                          def fz1(tile, j):
                              h2 = hs2s[j % 2]
                              hv = hsum[:, tile, :].rearrange("p (h c) -> p h c", h=4)
                              dve(lambda: nc.vector.tensor_reduce(out=h2[:], in_=fsq[j % 2][:].rearrange("p (h c) -> p h c", h=4), axis=AX.X, op=ALU.add), fsq[j % 2].T, h2.T)
                              act(lambda: nc.scalar.activation(out=h2[:], in_=h2[:], func=AF.Ln, bias=EPS, scale=1.0 / 128), h2.T, h2.T)
                              act(lambda: nc.scalar.activation(out=h2[:], in_=h2[:], func=AF.Exp, scale=-0.5), h2.T, h2.T)
                              pool(lambda: nc.gpsimd.tensor_tensor(out=hv, in0=hv, in1=h2[:].unsqueeze(2).to_broadcast([128, 4, 128]), op=ALU.mult), [hsum.T[tile]] + h2.T, [hsum.T[tile]])
                              pool(lambda: nc.gpsimd.tensor_tensor(out=hsum[:, tile, :], in0=hsum[:, tile, :], in1=gmnbc[:], op=ALU.mult), [hsum.T[tile]] + gmnbc.T, [hsum.T[tile]])

                          def fz2(tile, j):
                              sg = sigs[j % 3]
                              dve(lambda: nc.vector.tensor_tensor(out=hmtok[:], in0=hsum[:, tile, :], in1=sg[:], op=ALU.mult), [hsum.T[tile]] + sg.T, hmtok.T)
                              pt = nxt('ptr')
                              for c in range(4):
                                  tr(pt[:, c * 128:(c + 1) * 128], hmtok[:, c * 128:(c + 1) * 128], identb[:], hmtok.T + identb.T, pt.T)
                              act(lambda: nc.scalar.copy(out=hmaT[:, 0:4, tile * 128:(tile + 1) * 128], in_=pt[:, 0:512].rearrange("p (c t) -> p c t", c=4)), pt.T, [hmaT.T[tile]])

                          fq = []
                          fcnt = [0]

                          def advance_fq():
                              nq = []
                              for (stg, tile_, j_) in fq:
                                  if stg == 1:
                                      fz1(tile_, j_)
                                      nq.append((2, tile_, j_))
                                  else:
                                      fz2(tile_, j_)
                              fq[:] = nq

                          for ci, ch in enumerate(chains):
                              d = ch['d']
                              Cc, mc = Cst[ci], mrow[ci][0]
                              if isS:
                                  fw.dma('sp', Cc[:, :, 0:128], st_C[d].rearrange("h k v -> k h v"), writes=Cc.T)
                                  fw.dma('sp', Cc[:, :, 128], st_n[d].rearrange("h k -> k h"), writes=Cc.T, allow_slow_non_contiguous=True)
                                  fw.dma('sp', mc[:, 0:1], st_m[d].unsqueeze(1), writes=mc.T)
                              else:
                                  dve(lambda: nc.vector.memset(Cc[:], 0.0), [], Cc.T)
                                  dve(lambda: nc.vector.memset(mc[:], 0.0), [], mc.T)
                              act(lambda: nc.scalar.copy(out=Cbs[ci][:], in_=Cc[:]), Cc.T, Cbs[ci].T)
                          chk(2.2 + (0 if isS else 10))
                          steps = max(len(ch['units']) for ch in chains)
                          curm = [0] * NCH
                          pend_r = [None]
                          for st in range(steps):
                              for ci, ch in enumerate(chains):
                                  if st < len(ch['units']):
                                      tile, full = ch['units'][st]
                                      stt_ = rows_a(ci, tile, full, curm[ci])
                                      curm[ci] = (curm[ci] + 1) % 3
                                      if pend_r[0] is not None:
                                          rows_b(pend_r[0])
                                      pend_r[0] = stt_
                          if pend_r[0] is not None:
                              rows_b(pend_r[0])
                          chk(2.5 + (0 if isS else 10))
                          visited = set()
                          ulist = []
                          pref, rest = [], []
                          for ch in chains:
                              us = ch['units']
                              j_ = 0
                              while j_ < len(us) and not us[j_][1]:
                                  j_ += 1
                              pref.append(us[:j_])
                              rest.append(us[j_:])
                          order = []
                          for ci in range(NCH):
                              order += [(ci, t_, f_) for (t_, f_) in pref[ci]]
                          for st in range(max(len(r_) for r_ in rest)):
                              for ci in range(NCH):
                                  if st < len(rest[ci]):
                                      order.append((ci,) + tuple(rest[ci][st]))
                          for (ci, tile, full) in order:
                              first = tile not in visited
                              if full and first:
                                  visited.add(tile)
                              ulist.append((ci, tile, full, first, len(ulist) % NB))
                          fa1(ulist[0])
                          fa2(ulist[0])
                          for i_, u in enumerate(ulist):
                              if i_ + 1 < len(ulist):
                                  fa1(ulist[i_ + 1])
                              back(u)
                              if i_ + 1 < len(ulist):
                                  fa2(ulist[i_ + 1])
                              advance_fq()
                              if u[2] and not u[3]:
                                  fz0(u[1], fcnt[0])
                                  fq.append((1, u[1], fcnt[0]))
                                  fcnt[0] += 1
                          while fq:
                              advance_fq()
                          if not isS:
                              for ci, ch in enumerate(chains):
                                  d, si = ch['d'], ch['seq']
                                  Cf, mf = Cst[ci], mrow[ci][curm[ci]]
                                  fw.dma('sp', o_C[si, d].rearrange("h k v -> k h v"), Cf[:, :, 0:128], reads=Cf.T)
                                  fw.dma('sp', o_n[si, d].rearrange("h k -> k h"), Cf[:, :, 128], reads=Cf.T, allow_slow_non_contiguous=True)
                                  fw.dma('sp', o_m[si, d].unsqueeze(1), mf[:, 0:1], reads=mf.T)
                      set_psum(2, 2, 2)

                  chk(3 + (0 if isS else 10))
                  xmid = B(esj, "xmid", [128, 8, 1024], F32, 8)
                  h2T = B(esj, "h2T", [128, 8, 1024], BF16, 8)
                  wu = [B(esj, "wu%d" % i, [128, 8, 1024], BF16) for i in range(2)]
                  wd = [B(esj, "wd%d" % i, [128, 8, 1024], BF16) for i in range(2)]
                  wu_v = w_up.rearrange("(c p) n -> p c n", p=128)
                  wd_v = w_dn.rearrange("(f p) n -> p f n", p=128)

                  def load_q(q):
                      fw.dma('pool', wu[q % 2][:], wu_v[:, :, q * 1024:(q + 1) * 1024], writes=wu[q % 2].T)
                      fw.dma('pool', wd[q % 2][:], wd_v[:, q * 8:(q + 1) * 8, :], writes=wd[q % 2].T)
                  with scope() as es7:
                      wo = B(es7, "wo", [128, 8, 1024], BF16)
                      fw.dma('pool', wo[:], w_out.rearrange("(c p) n -> p c n", p=128), writes=wo.T)
                      load_q(0)
                      load_q(1)
                      tmpx = [B(es7, "tmpx%d" % i, [128, 1024], F32) for i in range(2)]
                      ssb2 = [B(es7, "ssc%d" % i, [128, 4], F32) for i in range(2)]
                      xnb2 = [B(es7, "xnc%d" % i, [128, 1024], BF16) for i in range(2)]
                      def stageA(ti):
                          k2 = ti % 2
                          fw.dma('sp', xmid[:, ti, :], xin[ti * 128:(ti + 1) * 128, :], writes=[xmid.T[ti]])
                          pb = nxt('pbig')
                          for nb in range(2):
                              for c in range(8):
                                  mm(pb[:, nb * 512:(nb + 1) * 512], hmaT[:, c, ti * 128:(ti + 1) * 128], wo[:, c, nb * 512:(nb + 1) * 512], c == 0, c == 7, [hmaT.T[ti]] + wo.T, pb.T)
                          dve(lambda: nc.vector.tensor_tensor(out=tmpx[k2][:], in0=pb[:, :], in1=g1bc[:, cond, :], op=ALU.mult), pb.T + g1bc.T, tmpx[k2].T)
                          pool(lambda: nc.gpsimd.tensor_tensor(out=xmid[:, ti, :], in0=xmid[:, ti, :], in1=tmpx[k2][:], op=ALU.add), tmpx[k2].T + [xmid.T[ti]], [xmid.T[ti]])

                      def stageB(ti):
                          k2 = ti % 2
                          norm_to_hT(None, xmid[:, ti, :], [xmid.T[ti]], cond, 1, h2T[:, :, ti * 128:(ti + 1) * 128], [h2T.T[ti]], (ssb2[k2], xnb2[k2]))

                      stageA(0)
                      for ti in range(8):
                          if ti + 1 < 8:
                              stageA(ti + 1)
                          stageB(ti)

                  chk(4 + (0 if isS else 10))
                  with scope() as es8:
                      ub = [B(es8, "ub%d" % i, [128, 8, 512], BF16, 8) for i in range(2)]
                      rl = [B(es8, "rl%d" % i, [128, 512], BF16) for i in range(2)]
                      tmpy = [B(es8, "tmpy%d" % i, [128, 1024], F32) for i in range(2)]
                      ucnt = 0
                      for q in range(4):
                          wuq, wdq = wu[q % 2], wd[q % 2]
                          if 1 <= q and q + 1 < 4:
                              load_q(q + 1)
                          for g in range(2):
                              u = ub[ucnt % 2]
                              ucnt += 1
                              for fb in range(8):
                                  ps = nxt('psm')
                                  for c in range(8):
                                      mm(ps[:, :], wuq[:, c, fb * 128:(fb + 1) * 128], h2T[:, c, g * 512:(g + 1) * 512], c == 0, c == 7, wuq.T + h2T.T[g * 4:(g + 1) * 4], ps.T)
                                  r_ = rl[fb % 2]
                                  act(lambda: nc.scalar.activation(out=r_[:], in_=ps[:, :], func=AF.Relu), ps.T, r_.T)
                                  dve(lambda: nc.vector.tensor_tensor(out=u[:, fb, :], in0=r_[:], in1=r_[:], op=ALU.mult), r_.T, [u.T[fb]])
                              for tt in range(4):
                                  ti = g * 4 + tt
                                  k2 = ti % 2
                                  pb = nxt('pbig')
                                  for nb in range(2):
                                      for fb in range(8):
                                          mm(pb[:, nb * 512:(nb + 1) * 512], u[:, fb, tt * 128:(tt + 1) * 128], wdq[:, fb, nb * 512:(nb + 1) * 512], fb == 0, fb == 7, [u.T[fb]] + wdq.T, pb.T)
                                  dve(lambda: nc.vector.tensor_tensor(out=tmpy[k2][:], in0=pb[:, :], in1=g2bc[:, cond, :], op=ALU.mult), pb.T + g2bc.T, tmpy[k2].T)
                                  pool(lambda: nc.gpsimd.tensor_tensor(out=xmid[:, ti, :], in0=xmid[:, ti, :], in1=tmpy[k2][:], op=ALU.add), tmpy[k2].T + [xmid.T[ti]], [xmid.T[ti]])
                                  if q == 3:
                                      fw.dma('sp', yout[ti * 128:(ti + 1) * 128, :], xmid[:, ti, :], reads=[xmid.T[ti]])

        except _Stop:
            pass
        fw.finish()
        psum_es[0].close()
    return nc


_PROG = [None]
_DEBUG_HOOK = [None]


def _rope_tables():
    rows = 2048 // 64
    row = np.repeat(np.arange(rows, dtype=np.float32), 64)
    col = np.tile(np.arange(64, dtype=np.float32), rows)
    half = 16
    inv = (np.float32(10000.0) ** (-np.arange(0, half, 2, dtype=np.float32) / np.float32(half))).astype(np.float32)
    ang = np.concatenate([row[:, None] * inv, col[:, None] * inv], axis=-1).astype(np.float32)
    cs_, sn_ = np.cos(ang), np.sin(ang)
    return np.concatenate([cs_, cs_, -sn_, sn_], axis=-1).astype(np.float32)


def kernel(x_prompt, x_sample, cache_mla_ckv, cache_mla_krope, state_mlstm_C, state_mlstm_n, state_mlstm_m,
           c, c_ctx, norm1_g, norm2_g, w_ada, b_ada, w_in, mlstm_gate_b, mlstm_norm_g,
           q_lora_g, kv_lora_g, w_q_up, w_kv_up, q_head_g, k_head_g, w_out, w_mlp_up, w_mlp_down):
    f = lambda a: np.ascontiguousarray(np.asarray(a, dtype=np.float32))
    x_prompt, x_sample = f(x_prompt), f(x_sample)
    w_in0 = f(w_in)[0]
    if _PROG[0] is None:
        _PROG[0] = build_program()
    nc = _PROG[0]
    rope = _rope_tables()
    ident = np.eye(128, dtype=np.float32)
    s_idx = np.arange(128)[:, None]
    t_idx = np.arange(128)[None, :]
    mask = np.stack([np.where(s_idx <= t_idx, 0.0, NEG), np.where(s_idx >= t_idx, 0.0, NEG)]).astype(np.float32)
    sel = np.zeros((4, 4, 128), np.float32)
    for k in range(4):
        sel[k, k, :] = 1.0
    sel = sel.reshape(4, 512)
    w_main = np.ascontiguousarray(np.concatenate([w_in0[:, 0:2048], w_in0[:, 2064:2736]], axis=1))
    gcols = w_in0[:, 2048:2064]
    gb = f(mlstm_gate_b)[0]
    in_maps = []
    for core in range(8):
        s, p = core // 2, core % 2
        flip = (p == 1)
        xp_ = x_prompt[4 * core:4 * core + 4]
        xs_ = x_sample[s]
        stC, stn, stm = f(state_mlstm_C)[s, 0], f(state_mlstm_n)[s, 0], f(state_mlstm_m)[s, 0]
        rp = rope
        gc, gbb = gcols, gb
        if flip:
            xp_ = xp_[:, ::-1]
            xs_ = xs_[::-1]
            rp = rope[::-1]
            stC, stn, stm = stC[::-1], stn[::-1], stm[::-1]
            gc = np.concatenate([gcols[:, 8:16], gcols[:, 0:8]], axis=1)
            gbb = np.concatenate([gb[8:16], gb[0:8]])
        gc40 = np.zeros((1024, 40), np.float32)
        gb40 = np.zeros((40, 1), np.float32)
        for d_ in range(2):
            gc40[:, d_ * 4:(d_ + 1) * 4] = gc[:, d_ * 8:d_ * 8 + 4]
            gc40[:, 32 + d_ * 4:32 + (d_ + 1) * 4] = gc[:, d_ * 8 + 4:d_ * 8 + 8]
            gb40[d_ * 4:(d_ + 1) * 4, 0] = gbb[d_ * 8:d_ * 8 + 4]
            gb40[32 + d_ * 4:32 + (d_ + 1) * 4, 0] = gbb[d_ * 8 + 4:d_ * 8 + 8]
        in_maps.append({
            "xp": np.ascontiguousarray(xp_.reshape(1024, 1024)),
            "xs": np.ascontiguousarray(xs_),
            "cvec": np.ascontiguousarray(np.stack([f(c_ctx), f(c)[s]])),
            "w_ada": f(w_ada)[0], "b_ada": f(b_ada)[0][None, :],
            "norm_g": np.ascontiguousarray(np.stack([f(norm1_g)[0], f(norm2_g)[0]])),
            "w_in": w_main, "w_gate": gc40, "gate_b": gb40,
            "rope_cs": np.ascontiguousarray(rp),
            "ctx_ckv": f(cache_mla_ckv)[s, 0], "ctx_kr": f(cache_mla_krope)[s, 0],
            "st_C": np.ascontiguousarray(stC), "st_n": np.ascontiguousarray(stn), "st_m": np.ascontiguousarray(stm),
            "g_q": f(q_lora_g), "g_kv": f(kv_lora_g), "g_qh": f(q_head_g), "g_kh": f(k_head_g), "g_mn": f(mlstm_norm_g),
            "w_q_up": f(w_q_up)[0], "w_kv_up": f(w_kv_up)[0], "w_out": f(w_out)[0],
            "w_up": f(w_mlp_up)[0], "w_dn": f(w_mlp_down)[0],
            "c_ident": ident, "c_mask": mask, "c_sel": sel, "c_nsel": np.ascontiguousarray(-sel),
        })
    if _DEBUG_HOOK[0] is not None:
        return _DEBUG_HOOK[0](in_maps)
    res = run_bass_kernel_spmd(nc, in_maps, core_ids=list(range(8)))
    R = res.results
    y_p = np.zeros((32, 256, 1024), np.float32)
    y_s = np.zeros((4, 2048, 1024), np.float32)
    o_ckv = np.zeros((32, 1, 256, 256), np.float32)
    o_kr = np.zeros((32, 1, 256, 32), np.float32)
    o_C = np.zeros((32, 1, 2, 4, 128, 128), np.float32)
    o_n = np.zeros((32, 1, 2, 4, 128), np.float32)
    o_m = np.zeros((32, 1, 2, 4), np.float32)
    for core in range(8):
        s, p = core // 2, core % 2
        r = R[core]
        yp_ = r["yp"].reshape(4, 256, 1024)
        ck = r["o_ckv"].reshape(4, 256, 256)
        kr_ = r["o_kr"].reshape(4, 256, 32)
        C_, n_, m_ = r["o_C"], r["o_n"], r["o_m"]
        ys_ = r["ys"]
        if p == 1:
            yp_, ck, kr_ = yp_[:, ::-1], ck[:, ::-1], kr_[:, ::-1]
            C_, n_, m_ = C_[:, ::-1], n_[:, ::-1], m_[:, ::-1]
            y_s[s, 1024:2048] = ys_[::-1]
        else:
            y_s[s, 0:1024] = ys_
        y_p[4 * core:4 * core + 4] = yp_
        o_ckv[4 * core:4 * core + 4, 0] = ck
        o_kr[4 * core:4 * core + 4, 0] = kr_
        o_C[4 * core:4 * core + 4, 0] = C_
        o_n[4 * core:4 * core + 4, 0] = n_
        o_m[4 * core:4 * core + 4, 0] = m_
    return (y_p, y_s, o_ckv, o_kr, o_C, o_n, o_m)
```

```python
import numpy as np
from contextlib import ExitStack, contextmanager
import concourse.bass as bass
import concourse.mybir as mybir
from concourse.bass_utils import run_bass_kernel_spmd

F32 = mybir.dt.float32
BF16 = mybir.dt.bfloat16
AF = mybir.ActivationFunctionType
ALU = mybir.AluOpType
AX = mybir.AxisListType
EPS = 1e-6
NEG = -30000.0


class T:
    def __init__(self, name):
        self.name = name
        self.w = []
        self.r = []
        self.dsem = None
        self.dcnt = 0


class FW:
    def __init__(self, nc, es):
        self.nc = nc
        self.es = es
        self.eng = {'pe': nc.tensor, 'act': nc.scalar, 'dve': nc.vector, 'pool': nc.gpsimd, 'sp': nc.sync}
        self.sem = {k: es.enter_context(nc.semaphore("s_" + k)) for k in ('pe', 'act', 'dve', 'pool')}
        self.cnt = {k: 0 for k in self.sem}
        self.waited = {k: {} for k in self.eng}
        self.dma_ts = []
        self.dsem_pool = []
        self.dead = False
        self.limit = None
        self.count2 = None
        self.sem_total = {}

    def _wait(self, e, deps):
        need = {}
        for (s, v) in deps:
            if v > need.get(id(s), (s, 0))[1]:
                need[id(s)] = (s, v)
        for (s, v) in need.values():
            if e == 'pe' and s is self.sem['pe']:
                continue
            if self.waited[e].get(id(s), 0) >= v:
                continue
            self.eng[e].wait_ge(s, v)
            self.waited[e][id(s)] = v

    def op(self, e, fn, reads=(), writes=()):
        if self.count2 is not None and self.limit is not None:
            self.count2 += 1
            if self.count2 > self.limit:
                self.dead = True
        if self.dead:
            return None
        deps = []
        for t in reads:
            deps += t.w
        for t in writes:
            deps += t.w
            deps += t.r
        self._wait(e, deps)
        ins = fn()
        self.cnt[e] += 1
        ins.then_inc(self.sem[e], 1)
        tok = (self.sem[e], self.cnt[e])
        for t in reads:
            t.r.append(tok)
            if len(t.r) > 24:
                t.r = self._compact(t.r)
        for t in writes:
            t.w = [tok]
            t.r = []
        return ins

    @staticmethod
    def _compact(lst):
        m = {}
        for (s, v) in lst:
            if v > m.get(id(s), (s, 0))[1]:
                m[id(s)] = (s, v)
        return list(m.values())

    def dma(self, q, out, in_, reads=(), writes=(), **kw):
        if self.dead:
            return None
        deps = []
        for t in reads:
            deps += t.w
        for t in writes:
            deps += t.w
            deps += t.r
        self._wait(q, deps)
        own = (writes[0] if writes else reads[0])
        if own.dsem is not None and own.dq != ('sw' if q == 'pool' else 'hw'):
            raise RuntimeError('mixed dma queues on tile ' + own.name)
        if own.dsem is None:
            own.dq = 'sw' if q == 'pool' else 'hw'
            pl = [x for x in self.dsem_pool if x[2] == own.dq]
            if pl:
                ent = pl[-1]
                self.dsem_pool.remove(ent)
                own.dsem, own.dcnt = ent[0], ent[1]
            else:
                own.dsem = self.es.enter_context(self.nc.semaphore("d%d" % len(self.dma_ts)))
                own.dcnt = 0
                self.dma_ts.append(own.dsem)
        if own.dcnt > 0:
            self._wait(q, [(own.dsem, own.dcnt)])
        own.dcnt += 16
        self.sem_total[id(own.dsem)] = (own.dsem, own.dcnt)
        ins = self.eng[q].dma_start(out=out, in_=in_, **kw)
        ins.then_inc(own.dsem, 16)
        tok = (own.dsem, own.dcnt)
        for t in reads:
            t.r.append(tok)
        for t in writes:
            t.w = [tok]
            t.r = []
        return ins

    def barrier(self):
        if self.dead:
            return
        toks = [(self.sem[e], self.cnt[e]) for e in self.sem if self.cnt[e] > 0] + list(self.sem_total.values())
        for e in ('pe', 'act', 'dve', 'pool', 'sp'):
            self._wait(e, toks)

    def release(self, t):
        if t.dsem is not None:
            self.dsem_pool.append((t.dsem, t.dcnt, t.dq))
            t.dsem = None

    def finish(self):
        for (sem, cnt) in self.sem_total.values():
            self.eng['sp'].wait_ge(sem, cnt)


class Buf:
    def __init__(self, es, nc, name, shape, dt, ntr=1, psum=False, fw=None):
        self.T = [T(name + str(i)) for i in range(ntr)]
        if fw is not None:
            for t in self.T:
                es.callback(fw.release, t)
        if psum:
            self.t = es.enter_context(nc.psum_tensor(name, shape, dt))
        else:
            self.t = es.enter_context(nc.sbuf_tensor(name, shape, dt))

    def __getitem__(self, k):
        return self.t[k]


class _Stop(Exception):
    pass


def build_program(stage=99):
    nc = bass.Bass("TRN2", target_bir_lowering=False)

    fwbox = [None]

    def chk(k):
        if stage <= k:
            fwbox[0].dead = True

    def din(name, shape):
        return nc.dram_tensor(name, shape, F32, kind="ExternalInput").ap()

    def dout(name, shape):
        return nc.dram_tensor(name, shape, F32, kind="ExternalOutput").ap()

    xp = din("xp", [1024, 1024])
    xs = din("xs", [2048, 1024])
    cvec = din("cvec", [2, 1024])
    w_ada = din("w_ada", [1024, 6144])
    b_ada = din("b_ada", [1, 6144])
    norm_g = din("norm_g", [2, 1024])
    w_in = din("w_in", [1024, 2720])
    w_gate = din("w_gate", [1024, 40])
    gate_b = din("gate_b", [40, 1])
    rope_cs = din("rope_cs", [2048, 64])
    ctx_ckv = din("ctx_ckv", [256, 256])
    ctx_kr = din("ctx_kr", [256, 32])
    st_C = din("st_C", [2, 4, 128, 128])
    st_n = din("st_n", [2, 4, 128])
    st_m = din("st_m", [2, 4])
    g_q = din("g_q", [1, 384])
    g_kv = din("g_kv", [1, 256])
    g_qh = din("g_qh", [1, 96])
    g_kh = din("g_kh", [1, 96])
    g_mn = din("g_mn", [1, 512])
    w_q_up = din("w_q_up", [384, 768])
    w_kv_up = din("w_kv_up", [256, 1024])
    w_out = din("w_out", [1024, 1024])
    w_up = din("w_up", [1024, 4096])
    w_dn = din("w_dn", [4096, 1024])
    c_ident = din("c_ident", [128, 128])
    c_mask = din("c_mask", [2, 128, 128])
    c_sel = din("c_sel", [4, 512])
    c_nsel = din("c_nsel", [4, 512])

    yp = dout("yp", [1024, 1024])
    ys = dout("ys", [1024, 1024])
    o_ckv = dout("o_ckv", [1024, 256])
    o_kr = dout("o_kr", [1024, 32])
    o_C = dout("o_C", [4, 2, 4, 128, 128])
    o_n = dout("o_n", [4, 2, 4, 128])
    o_m = dout("o_m", [4, 2, 4])

    with ExitStack() as es0:
        fw = FW(nc, es0)
        fwbox[0] = fw

        @contextmanager
        def scope():
            with ExitStack() as es_:
                yield es_
            fw.barrier()

        sfx = [""]

        def B(es, name, shape, dt, ntr=1):
            return Buf(es, nc, name + sfx[0], shape, dt, ntr, fw=fw)

        def P(es, name, shape, dt):
            return Buf(es, nc, name, shape, dt, 1, psum=True)

        def act(fn, r, w):
            return fw.op('act', fn, r, w)

        def dve(fn, r, w):
            return fw.op('dve', fn, r, w)

        def pool(fn, r, w):
            return fw.op('pool', fn, r, w)

        def pe(fn, r, w):
            return fw.op('pe', fn, r, w)

        def mm(out, lhsT, rhs, start, stop, r, w):
            return fw.op('pe', lambda: nc.tensor.matmul(out, lhsT=lhsT, rhs=rhs, start=start, stop=stop), r, w)

        def tr(out, in_, ident, r, w):
            return fw.op('pe', lambda: nc.tensor.transpose(out, in_, ident), r, w)

        pbig, ptr, psm = [], [], []
        rot = {'pbig': 0, 'ptr': 0, 'psm': 0}
        psum_es = [None]
        pgen = [0]

        def set_psum(nbig, ntr, nsm):
            assert 2 * nbig + ntr + nsm <= 8
            if psum_es[0] is not None:
                fw.barrier()
                psum_es[0].close()
            psum_es[0] = ExitStack()
            pgen[0] += 1
            g = pgen[0]
            pbig[:] = [P(psum_es[0], "pbig%d_%d" % (i, g), [128, 1024], F32) for i in range(nbig)]
            ptr[:] = [P(psum_es[0], "ptr%d_%d" % (i, g), [128, 1024], BF16) for i in range(ntr)]
            psm[:] = [P(psum_es[0], "psm%d_%d" % (i, g), [128, 512], F32) for i in range(nsm)]

        def nxt(kind):
            lst = {'pbig': pbig, 'ptr': ptr, 'psm': psm}[kind]
            rot[kind] = (rot[kind] + 1) % len(lst)
            return lst[rot[kind]]

        set_psum(2, 2, 2)

        identf = B(es0, "identf", [128, 128], F32)
        identb = B(es0, "identb", [128, 128], BF16)
        maskc = B(es0, "maskc", [128, 2, 128], F32)
        sel4 = B(es0, "sel4", [4, 512], F32)
        nsel4 = B(es0, "nsel4", [4, 512], F32)
        onesf = B(es0, "onesf", [128, 512], F32)
        junk = B(es0, "junk", [128, 1024], BF16)
        gqbc = B(es0, "gqbc", [128, 384], F32)
        gkvbc = B(es0, "gkvbc", [128, 256], F32)
        gqhbc = B(es0, "gqhbc", [128, 96], F32)
        gkhbc = B(es0, "gkhbc", [128, 96], F32)
        gmnbc = B(es0, "gmnbc", [128, 512], F32)
        gbias = B(es0, "gbias", [40, 2], F32)
        g1bc = B(es0, "g1bc", [128, 2, 1024], F32)
        g2bc = B(es0, "g2bc", [128, 2, 1024], F32)
        ab = B(es0, "ab", [128, 2, 4, 8], F32)

        fw.dma('sp', identf[:], c_ident, writes=identf.T)
        fw.dma('pool', identb[:], c_ident, writes=identb.T)
        fw.dma('sp', maskc[:], c_mask.rearrange("d s t -> s d t"), writes=maskc.T)
        fw.dma('sp', sel4[:], c_sel, writes=sel4.T)
        fw.dma('sp', nsel4[:], c_nsel, writes=nsel4.T)
        dve(lambda: nc.vector.memset(onesf[:], 1.0), [], onesf.T)
        fw.dma('sp', gqbc[:], g_q.partition_broadcast(128), writes=gqbc.T)
        fw.dma('sp', gkvbc[:], g_kv.partition_broadcast(128), writes=gkvbc.T)
        fw.dma('sp', gqhbc[:], g_qh.partition_broadcast(128), writes=gqhbc.T)
        fw.dma('sp', gkhbc[:], g_kh.partition_broadcast(128), writes=gkhbc.T)
        fw.dma('sp', gmnbc[:], g_mn.partition_broadcast(128), writes=gmnbc.T)
        fw.dma('sp', gbias[:, 0:1], gate_b, writes=gbias.T)
        dve(lambda: nc.vector.tensor_scalar(out=gbias[:, 1:2], in0=gbias[:, 0:1], scalar1=-1.0, scalar2=None, op0=ALU.mult), gbias.T, gbias.T)
        dve(lambda: nc.vector.tensor_scalar(out=gqhbc[:], in0=gqhbc[:], scalar1=float(96 ** -0.5), scalar2=None, op0=ALU.mult), gqhbc.T, gqhbc.T)

        with scope() as es:
            cT = B(es, "cT", [128, 2, 8], F32)
            ngT = B(es, "ngT", [128, 2, 8], F32)
            ce = B(es, "ce", [128, 2, 8], F32)
            scT = B(es, "scT", [128, 8, 33], BF16)
            bada = B(es, "bada", [33, 6144], F32)
            modrow = B(es, "modrow", [33, 6144], F32)
            wab = [B(es, "wab%d" % i, [128, 8, 512], BF16) for i in range(2)]
            for r in range(2):
                fw.dma('sp', cT[:, r, :], cvec[r].rearrange("(c p) -> p c", p=128), writes=cT.T, allow_slow_non_contiguous=True)
                fw.dma('sp', ngT[:, r, :], norm_g[r].rearrange("(c p) -> p c", p=128), writes=ngT.T, allow_slow_non_contiguous=True)
            dve(lambda: nc.vector.memset(bada[:], 0.0), [], bada.T)
            fw.dma('sp', bada[0:1, :], b_ada, writes=bada.T)
            fw.dma('sp', bada[32:33, :], b_ada, writes=bada.T)
            act(lambda: nc.scalar.activation(out=ce[:], in_=cT[:], func=AF.Exp, scale=-1.0), cT.T, ce.T)
            dve(lambda: nc.vector.tensor_scalar(out=ce[:], in0=ce[:], scalar1=1.0, scalar2=None, op0=ALU.add), ce.T, ce.T)
            dve(lambda: nc.vector.reciprocal(out=ce[:], in_=ce[:]), ce.T, ce.T)
            dve(lambda: nc.vector.memset(scT[:], 0.0), [], scT.T)
            dve(lambda: nc.vector.tensor_tensor(out=scT[:, :, 0], in0=cT[:, 0, :], in1=ce[:, 0, :], op=ALU.mult), cT.T + ce.T, scT.T)
            dve(lambda: nc.vector.tensor_tensor(out=scT[:, :, 32], in0=cT[:, 1, :], in1=ce[:, 1, :], op=ALU.mult), cT.T + ce.T, scT.T)
            wa_v = w_ada.rearrange("(c p) n -> p c n", p=128)
            fw.dma('pool', wab[0][:], wa_v[:, :, 0:512], writes=wab[0].T)
            for blk in range(12):
                wb = wab[blk % 2]
                if blk + 1 < 12:
                    fw.dma('pool', wab[(blk + 1) % 2][:], wa_v[:, :, (blk + 1) * 512:(blk + 2) * 512], writes=wab[(blk + 1) % 2].T)
                ps = nxt('psm')
                for c in range(8):
                    mm(ps[0:33, :], scT[:, c, :], wb[:, c, :], c == 0, c == 7, scT.T + wb.T, ps.T)
                dve(lambda: nc.vector.tensor_tensor(out=modrow[:, blk * 512:(blk + 1) * 512], in0=ps[0:33, :], in1=bada[:, blk * 512:(blk + 1) * 512], op=ALU.add), ps.T + bada.T, modrow.T)
            for cond in range(2):
                r = 32 * cond
                for (dst, off) in ((g1bc, 2048), (g2bc, 5120)):
                    for hh in range(2):
                        ps = nxt('psm')
                        mm(ps[:, :], onesf[r:r + 1, 0:128], modrow[r:r + 1, off + hh * 512: off + (hh + 1) * 512], True, True, onesf.T + modrow.T, ps.T)
                        act(lambda: nc.scalar.copy(out=dst[:, cond, hh * 512:(hh + 1) * 512], in_=ps[:, :]), ps.T, dst.T)
                ps = nxt('psm')
                for vi, off in enumerate((0, 1024, 3072, 4096)):
                    for c in range(8):
                        mm(ps[:, vi * 8 + c: vi * 8 + c + 1], modrow[r:r + 1, off + c * 128: off + (c + 1) * 128], onesf[r:r + 1, 0:1], True, True, onesf.T + modrow.T, ps.T)
                dve(lambda: nc.vector.scalar_tensor_tensor(out=ab[:, cond, 0, :], in0=ps[:, 8:16], scalar=1.0, in1=ngT[:, 0, :], op0=ALU.add, op1=ALU.mult), ps.T + ngT.T, ab.T)
                dve(lambda: nc.vector.tensor_copy(out=ab[:, cond, 1, :], in_=ps[:, 0:8]), ps.T, ab.T)
                dve(lambda: nc.vector.scalar_tensor_tensor(out=ab[:, cond, 2, :], in0=ps[:, 24:32], scalar=1.0, in1=ngT[:, 1, :], op0=ALU.add, op1=ALU.mult), ps.T + ngT.T, ab.T)
                dve(lambda: nc.vector.tensor_copy(out=ab[:, cond, 3, :], in_=ps[:, 16:24]), ps.T, ab.T)

        def rstd_from_mean(ss_ap, ss_T):
            act(lambda: nc.scalar.activation(out=ss_ap, in_=ss_ap, func=AF.Ln, bias=EPS, scale=1.0), ss_T, ss_T)
            act(lambda: nc.scalar.activation(out=ss_ap, in_=ss_ap, func=AF.Exp, scale=-0.5), ss_T, ss_T)

        def norm_to_hT(es_unused, xt_ap, xt_T, cond, which, dst_ap, dst_T, wk):
            ss, xn = wk
            act(lambda: nc.scalar.activation(out=junk[:], in_=xt_ap, func=AF.Square, scale=float(1024 ** -0.5), accum_out=ss[:, 0:1]), xt_T, junk.T + ss.T)
            rstd_from_mean(ss[:, 0:1], ss.T)
            dve(lambda: nc.vector.tensor_scalar(out=xn[:], in0=xt_ap, scalar1=ss[:, 0:1], scalar2=None, op0=ALU.mult), xt_T + ss.T, xn.T)
            pt = nxt('ptr')
            for c in range(8):
                tr(pt[:, c * 128:(c + 1) * 128], xn[:, c * 128:(c + 1) * 128], identb[:], xn.T + identb.T, pt.T)
            ptv = pt[:, :].rearrange("p (c t) -> p c t", c=8)
            a_ap = ab[:, cond, 2 * which, :].unsqueeze(2).to_broadcast([128, 8, 128])
            b_ap = ab[:, cond, 2 * which + 1, :].unsqueeze(2).to_broadcast([128, 8, 128])
            dve(lambda: nc.vector.tensor_tensor(out=dst_ap, in0=ptv, in1=a_ap, op=ALU.mult), pt.T + ab.T, dst_T)
            dve(lambda: nc.vector.tensor_tensor(out=dst_ap, in0=dst_ap, in1=b_ap, op=ALU.add), dst_T + ab.T, dst_T)

        try:
          for sj in ('S', 'P'):
              chk(0 if sj == 'S' else 10)
              isS = sj == 'S'
              sfx[0] = "_" + sj
              cond = 1 if isS else 0
              xin = xs if isS else xp
              yout = ys if isS else yp
              NT = 16 if isS else 8
              NOWN = 8
              NK = 18 if isS else 8
              if isS:
                  seqs = [dict(tiles=list(range(16)), own=list(range(8)), ktiles=list(range(18)))]
              else:
                  seqs = [dict(tiles=[2 * i, 2 * i + 1], own=[2 * i, 2 * i + 1], ktiles=[2 * i, 2 * i + 1]) for i in range(4)]
              with scope() as esj:
                  hmaT = B(esj, "hmaT", [128, 8, 1024], BF16, 8)
                  with scope() as es2:
                      qT = B(es2, "qT", [128, 4, 1024], BF16, 8)
                      kT = B(es2, "kT", [128, 4, 1024], BF16, 8)
                      ktok = B(es2, "ktok", [128, NT, 512], BF16, NT)
                      v1 = B(es2, "v1", [128, NT, 4, 130], BF16, NT)
                      motok = B(es2, "motok", [128, 8, 512], BF16, 8)
                      gall = B(es2, "gall", [40, NT * 128], F32, NT // 4)
                      dve(lambda: nc.vector.memset(v1[:, :, :, 128:129], 1.0), [], v1.T)
                      with scope() as es3:
                          qlnT = B(es3, "qlnT", [128, 3, 1024], BF16, 8)
                          ckvT = B(es3, "ckvT", [128, 2, NK * 128], BF16, NK)
                          kr = B(es3, "kr", [128, NK, 32], F32, NK)
                          with scope() as es4:
                              wi = B(es4, "wi", [128, 8, 2720], BF16, 8)
                              wg = B(es4, "wg", [128, 8, 40], BF16)
                              wi_v = w_in.rearrange("(c p) n -> p c n", p=128)
                              for c in range(8):
                                  fw.dma('pool', wi[:, c, :], wi_v[:, c, :], writes=[wi.T[c]])
                              fw.dma('pool', wg[:], w_gate.rearrange("(c p) n -> p c n", p=128), writes=wg.T)
                              NXB = 1 if isS else 2
                              xt = [B(es4, "xt%d" % i, [128, 1024], F32) for i in range(NXB)]
                              xnb = [B(es4, "xnb%d" % i, [128, 1024], BF16) for i in range(NXB)]
                              ssb = [B(es4, "ssb%d" % i, [128, 4], F32) for i in range(2)]
                              hTg = [B(es4, "hTg%d" % i, [128, 8, 512], BF16, 4) for i in range(2)]
                              latf = [B(es4, "latf%d" % i, [128, 672], F32) for i in range(NXB)]
                              lss = [B(es4, "lss%d" % i, [128, 2], F32) for i in range(2)]
                              qlnb = [B(es4, "qlnb%d" % i, [128, 384], BF16) for i in range(2)]
                              ckvf = [B(es4, "ckvf%d" % i, [128, 256], F32) for i in range(2)]
                              ckvb = [B(es4, "ckvb%d" % i, [128, 256], BF16) for i in range(2)]
                              NHT = 2

                              def norms(grp):
                                  hg_ = hTg[grp % NHT]
                                  for tt in range(4):
                                      ti = grp * 4 + tt
                                      k2 = ti % NXB
                                      fw.dma('sp', xt[k2][:], xin[ti * 128:(ti + 1) * 128, :], writes=xt[k2].T)
                                      norm_to_hT(None, xt[k2][:], xt[k2].T, cond, 0, hg_[:, :, tt * 128:(tt + 1) * 128], [hg_.T[tt]], (ssb[ti % 2], xnb[k2]))

                              def projs(grp):
                                  hg_ = hTg[grp % NHT]
                                  own_grp = (grp * 4) < NOWN
                                  if own_grp:
                                      for (dst, cb) in ((qT, 0), (kT, 512)):
                                          for h in range(4):
                                              ps = nxt('psm')
                                              for c in range(8):
                                                  mm(ps[:, :], wi[:, c, cb + h * 128: cb + (h + 1) * 128], hg_[:, c, :], c == 0, c == 7, [wi.T[c]] + hg_.T, ps.T)
                                              act(lambda: nc.scalar.mul(out=dst[:, h, grp * 512:(grp + 1) * 512], in_=ps[:, :], mul=(float(128 ** -0.5) if cb == 512 else 1.0)), ps.T, dst.T[grp * 4:(grp + 1) * 4])
                                  ps = nxt('psm')
                                  for c in range(8):
                                      mm(ps[0:40, :], wg[:, c, :], hg_[:, c, :], c == 0, c == 7, wg.T + hg_.T, ps.T)
                                  gs = slice(grp * 512, (grp + 1) * 512)
                                  act(lambda: nc.scalar.activation(out=gall[0:8, gs], in_=ps[0:8, :], func=AF.Identity, bias=gbias[0:8, 0:1], scale=1.0), ps.T + gbias.T, [gall.T[grp]])
                                  act(lambda: nc.scalar.activation(out=gall[32:40, gs], in_=ps[32:40, :], func=AF.Exp, bias=gbias[32:40, 1:2], scale=-1.0), ps.T + gbias.T, [gall.T[grp]])
                                  act(lambda: nc.scalar.activation(out=gall[32:40, gs], in_=gall[32:40, gs], func=AF.Ln, bias=1.0, scale=1.0), [gall.T[grp]], [gall.T[grp]])
                                  for tt in range(4):
                                      ti = grp * 4 + tt
                                      k2 = ti % 2
                                      lh = lambda c: hg_[:, c, tt * 128:(tt + 1) * 128]
                                      pb = nxt('pbig')
                                      for nb, cb in enumerate((512, 1024)):
                                          for c in range(8):
                                              mm(pb[:, nb * 512:(nb + 1) * 512], lh(c), wi[:, c, cb:cb + 512], c == 0, c == 7, [wi.T[c], hg_.T[tt]], pb.T)
                                      act(lambda: nc.scalar.mul(out=ktok[:, ti, :], in_=pb[:, 0:512], mul=float(128 ** -0.5)), pb.T, [ktok.T[ti]])
                                      act(lambda: nc.scalar.copy(out=v1[:, ti, :, 0:128], in_=pb[:, 512:1024].rearrange("p (h d) -> p h d", h=4)), pb.T, [v1.T[ti]])
                                      pb = nxt('pbig')
                                      if own_grp:
                                          for c in range(8):
                                              mm(pb[:, 0:512], lh(c), wi[:, c, 1536:2048], c == 0, c == 7, [wi.T[c], hg_.T[tt]], pb.T)
                                          act(lambda: nc.scalar.copy(out=motok[:, ti, :], in_=pb[:, 0:512]), pb.T, [motok.T[ti]])
                                          pb = nxt('pbig')
                                          for (o0, o1, cb) in ((0, 512, 2048), (512, 672, 2560)):
                                              for c in range(8):
                                                  mm(pb[:, o0:o1], lh(c), wi[:, c, cb:cb + (o1 - o0)], c == 0, c == 7, [wi.T[c], hg_.T[tt]], pb.T)
                                          dve(lambda: nc.vector.tensor_copy(out=latf[k2 % NXB][:], in_=pb[:, 0:672]), pb.T, latf[k2 % NXB].T)
                                      else:
                                          for c in range(8):
                                              mm(pb[:, 512:800], lh(c), wi[:, c, 2432:2720], c == 0, c == 7, [wi.T[c], hg_.T[tt]], pb.T)
                                          dve(lambda: nc.vector.tensor_copy(out=latf[k2 % NXB][:, 384:672], in_=pb[:, 512:800]), pb.T, latf[k2 % NXB].T)
                                      lf_ = latf[k2 % NXB]
                                      ls = lss[k2]
                                      if own_grp:
                                          act(lambda: nc.scalar.activation(out=junk[:, 0:384], in_=lf_[:, 0:384], func=AF.Square, scale=float(384 ** -0.5), accum_out=ls[:, 0:1]), lf_.T, junk.T + ls.T)
                                      else:
                                          dve(lambda: nc.vector.memset(ls[:, 0:1], 1.0), [], ls.T)
                                      act(lambda: nc.scalar.activation(out=junk[:, 0:256], in_=lf_[:, 384:640], func=AF.Square, scale=float(256 ** -0.5), accum_out=ls[:, 1:2]), lf_.T, junk.T + ls.T)
                                      rstd_from_mean(ls[:, 0:2], ls.T)
                                      if own_grp:
                                          dve(lambda: nc.vector.scalar_tensor_tensor(out=qlnb[k2][:], in0=lf_[:, 0:384], scalar=ls[:, 0:1], in1=gqbc[:], op0=ALU.mult, op1=ALU.mult), lf_.T + ls.T + gqbc.T, qlnb[k2].T)
                                          pt = nxt('ptr')
                                          for c in range(3):
                                              tr(pt[:, c * 128:(c + 1) * 128], qlnb[k2][:, c * 128:(c + 1) * 128], identb[:], qlnb[k2].T + identb.T, pt.T)
                                          act(lambda: nc.scalar.copy(out=qlnT[:, :, ti * 128:(ti + 1) * 128], in_=pt[:, 0:384].rearrange("p (c t) -> p c t", c=3)), pt.T, [qlnT.T[ti]])
                                      dve(lambda: nc.vector.scalar_tensor_tensor(out=ckvf[k2][:], in0=lf_[:, 384:640], scalar=ls[:, 1:2], in1=gkvbc[:], op0=ALU.mult, op1=ALU.mult), lf_.T + ls.T + gkvbc.T, ckvf[k2].T)
                                      pool(lambda: nc.gpsimd.tensor_copy(out=ckvb[k2][:], in_=ckvf[k2][:]), ckvf[k2].T, ckvb[k2].T)
                                      pool(lambda: nc.gpsimd.tensor_copy(out=kr[:, ti, :], in_=lf_[:, 640:672]), lf_.T, [kr.T[ti]])
                                      if not isS:
                                          fw.dma('sp', o_ckv[ti * 128:(ti + 1) * 128, :], ckvf[k2][:], reads=ckvf[k2].T)
                                          fw.dma('sp', o_kr[ti * 128:(ti + 1) * 128, :], lf_[:, 640:672], reads=lf_.T)
                                      pt = nxt('ptr')
                                      for c in range(2):
                                          tr(pt[:, c * 128:(c + 1) * 128], ckvb[k2][:, c * 128:(c + 1) * 128], identb[:], ckvb[k2].T + identb.T, pt.T)
                                      act(lambda: nc.scalar.copy(out=ckvT[:, :, ti * 128:(ti + 1) * 128], in_=pt[:, 0:256].rearrange("p (c t) -> p c t", c=2)), pt.T, [ckvT.T[ti]])
                              NG = NT // 4
                              if NHT == 2:
                                  norms(0)
                                  for grp in range(NG):
                                      if grp + 1 < NG:
                                          norms(grp + 1)
                                      projs(grp)
                              else:
                                  for grp in range(NG):
                                      norms(grp)
                                      projs(grp)
                              if isS:
                                  for j in range(2):
                                      ti = 16 + j
                                      fw.dma('sp', ckvf[j][:], ctx_ckv[j * 128:(j + 1) * 128, :], writes=ckvf[j].T)
                                      pool(lambda: nc.gpsimd.tensor_copy(out=ckvb[j][:], in_=ckvf[j][:]), ckvf[j].T, ckvb[j].T)
                                      fw.dma('sp', kr[:, ti, :], ctx_kr[j * 128:(j + 1) * 128, :], writes=[kr.T[ti]])
                                      pt = nxt('ptr')
                                      for c in range(2):
                                          tr(pt[:, c * 128:(c + 1) * 128], ckvb[j][:, c * 128:(c + 1) * 128], identb[:], ckvb[j].T + identb.T, pt.T)
                                      act(lambda: nc.scalar.copy(out=ckvT[:, :, ti * 128:(ti + 1) * 128], in_=pt[:, 0:256].rearrange("p (c t) -> p c t", c=2)), pt.T, [ckvT.T[ti]])

                          chk(1 + (0 if isS else 10))
                          set_psum(1, 1, 5)
                          with scope() as es5:
                              wq = B(es5, "wq", [128, 3, 768], BF16)
                              wkv = B(es5, "wkv", [128, 2, 1024], BF16)
                              fw.dma('pool', wq[:], w_q_up.rearrange("(c p) n -> p c n", p=128), writes=wq.T)
                              fw.dma('pool', wkv[:], w_kv_up.rearrange("(c p) n -> p c n", p=128), writes=wkv.T)
                              KT = B(es5, "KT", [128, 4, NK * 128], BF16, NK)
                              V1 = B(es5, "V1", [128, NK, 4, 128], BF16, NK)
                              ob = B(es5, "ob", [128, 2, 128], BF16)
                              rcpb = B(es5, "rcpb", [128, 512], F32)
                              QT = B(es5, "QT", [128, 4, 512], BF16, 4)
                              cs = B(es5, "cs", [128, 16, 64], F32)
                              kf = [B(es5, "kf%d" % i, [128, 2, 4, 96], F32) for i in range(2)]
                              sqf = B(es5, "sqf", [128, 2, 4, 96], F32)
                              knb = [B(es5, "knb%d" % i, [128, 2, 4, 128], BF16) for i in range(2)]
                              for kb_ in knb:
                                  dve(lambda: nc.vector.memset(kb_[:], 0.0), [], kb_.T)
                              hss = [B(es5, "hss%d" % i, [128, 8], F32) for i in range(2)]
                              rt = B(es5, "rt", [128, 2, 2, 4, 32], F32)
                              Eb = [B(es5, "Eb%d" % i, [128, 512], BF16) for i in range(3)]
                              dve(lambda: nc.vector.memset(V1[:], 0.0), [], V1.T)
                              dve(lambda: nc.vector.memset(ob[:], 0.0), [], ob.T)
                              dve(lambda: nc.vector.memset(ob[:, 0, 0:64], 1.0), [], ob.T)
                              dve(lambda: nc.vector.memset(ob[:, 1, 64:128], 1.0), [], ob.T)
                              if isS:
                                  fw.dma('sp', cs[:], rope_cs.rearrange("(j p) d -> p j d", p=128), writes=cs.T)
                              ecnt = [0]
                              stc = [0]
                              gq2 = B(es5, "gq2", [128, 96], F32)
                              dve(lambda: nc.vector.tensor_copy(out=gq2[:], in_=gqhbc[:]), gqhbc.T, gq2.T)
                              dve(lambda: nc.vector.tensor_tensor(out=gq2[:, 0:64], in0=gq2[:, 0:64], in1=gkhbc[:, 0:64], op=ALU.mult), gq2.T + gkhbc.T, gq2.T)

                              def norm_a(psv, src_T, kr_t0):
                                  i2 = ecnt[0] % 2
                                  ecnt[0] += 1
                                  f, s, ss_ = kf[i2], sqf, hss[i2]
                                  if kr_t0 is not None:
                                      dve(lambda: nc.vector.tensor_copy(out=f[:, :, :, 0:64], in_=psv[:, :, :, 0:64]), src_T, f.T)
                                      dve(lambda: nc.vector.tensor_copy(out=f[:, :, :, 64:96], in_=kr[:, kr_t0:kr_t0 + 2, :].unsqueeze(2).to_broadcast([128, 2, 4, 32])), [kr.T[kr_t0], kr.T[kr_t0 + 1]] + f.T, f.T)
                                  else:
                                      dve(lambda: nc.vector.tensor_copy(out=f[:], in_=psv), src_T, f.T)
                                  act(lambda: nc.scalar.activation(out=s[:], in_=f[:], func=AF.Square), f.T, s.T)
                                  dve(lambda: nc.vector.tensor_reduce(out=ss_[:], in_=s[:].rearrange("p t h d -> p (t h) d"), axis=AX.X, op=ALU.add), s.T, ss_.T)
                                  act(lambda: nc.scalar.activation(out=ss_[:], in_=ss_[:], func=AF.Ln, bias=EPS, scale=1.0 / 96), ss_.T, ss_.T)
                                  act(lambda: nc.scalar.activation(out=ss_[:], in_=ss_[:], func=AF.Exp, scale=-0.5), ss_.T, ss_.T)
                                  return i2

                              def norm_b(i2, kind, rope_t0):
                                  f, o, ss_, r_ = kf[i2], knb[i2], hss[i2], rt
                                  f8 = f[:].rearrange("p t h d -> p (t h) d")
                                  o8 = o[:].rearrange("p t h d -> p (t h) d")
                                  tb = lambda a, b: cs[:, rope_t0:rope_t0 + 2, a:b].unsqueeze(2).to_broadcast([128, 2, 4, b - a])
                                  if kind == 'k':
                                      dve(lambda: nc.vector.tensor_tensor(out=o8[:, :, 0:64], in0=f8[:, :, 0:64], in1=ss_[:].unsqueeze(2).to_broadcast([128, 8, 64]), op=ALU.mult), f.T + ss_.T, o.T)
                                      dve(lambda: nc.vector.tensor_tensor(out=f8[:, :, 64:96], in0=f8[:, :, 64:96], in1=ss_[:].unsqueeze(2).to_broadcast([128, 8, 32]), op=ALU.mult), f.T + ss_.T, f.T)
                                      gkr = gkhbc[:, 64:96].unsqueeze(1).to_broadcast([128, 8, 32])
                                      if rope_t0 is None:
                                          dve(lambda: nc.vector.tensor_tensor(out=o8[:, :, 64:96], in0=f8[:, :, 64:96], in1=gkr, op=ALU.mult), f.T + gkhbc.T, o.T)
                                          return o
                                      dve(lambda: nc.vector.tensor_tensor(out=f8[:, :, 64:96], in0=f8[:, :, 64:96], in1=gkr, op=ALU.mult), f.T + gkhbc.T, f.T)
                                  else:
                                      dve(lambda: nc.vector.tensor_tensor(out=f8, in0=f8, in1=ss_[:].unsqueeze(2).to_broadcast([128, 8, 96]), op=ALU.mult), f.T + ss_.T, f.T)
                                      gb = gq2[:].unsqueeze(1).to_broadcast([128, 8, 96])
                                      if rope_t0 is None:
                                          dve(lambda: nc.vector.tensor_tensor(out=o8[:, :, 0:96], in0=f8, in1=gb, op=ALU.mult), f.T + gq2.T, o.T)
                                          return o
                                      dve(lambda: nc.vector.tensor_tensor(out=o8[:, :, 0:64], in0=f8[:, :, 0:64], in1=gq2[:, 0:64].unsqueeze(1).to_broadcast([128, 8, 64]), op=ALU.mult), f.T + gq2.T, o.T)
                                      dve(lambda: nc.vector.tensor_tensor(out=f8[:, :, 64:96], in0=f8[:, :, 64:96], in1=gq2[:, 64:96].unsqueeze(1).to_broadcast([128, 8, 32]), op=ALU.mult), f.T + gq2.T, f.T)
                                  dve(lambda: nc.vector.tensor_tensor(out=r_[:, 0], in0=f[:, :, :, 64:96], in1=tb(0, 32), op=ALU.mult), f.T + cs.T, r_.T)
                                  dve(lambda: nc.vector.tensor_tensor(out=r_[:, 1, :, :, 0:16], in0=f[:, :, :, 80:96], in1=tb(32, 48), op=ALU.mult), f.T + cs.T, r_.T)
                                  dve(lambda: nc.vector.tensor_tensor(out=r_[:, 1, :, :, 16:32], in0=f[:, :, :, 64:80], in1=tb(48, 64), op=ALU.mult), f.T + cs.T, r_.T)
                                  dve(lambda: nc.vector.tensor_tensor(out=o[:, :, :, 64:96], in0=r_[:, 0], in1=r_[:, 1], op=ALU.add), r_.T, o.T)
                                  return o

                              def to_T(o, dst_ap, dst_T):
                                  pt = nxt('ptr')
                                  for t in range(2):
                                      for h in range(4):
                                          j = t * 4 + h
                                          tr(pt[:, j * 128:(j + 1) * 128], o[:, t, h, :], identb[:], o.T + identb.T, pt.T)
                                  act(lambda: nc.scalar.copy(out=dst_ap, in_=pt[:, :].rearrange("p (t h k) -> p t h k", t=2, h=4)), pt.T, dst_T)

                              for hg in range(2):
                                  def k_a(kp):
                                      t0 = 2 * kp
                                      pb = nxt('pbig')
                                      for t in range(2):
                                          for c in range(2):
                                              mm(pb[:, t * 512:(t + 1) * 512], ckvT[:, c, (t0 + t) * 128:(t0 + t + 1) * 128], wkv[:, c, hg * 512:(hg + 1) * 512], c == 0, c == 1, [ckvT.T[t0 + t]] + wkv.T, pb.T)
                                      psv = pb[:, :].rearrange("p (t h d) -> p t h d", t=2, h=4)
                                      dve(lambda: nc.vector.tensor_copy(out=V1[:, t0:t0 + 2, 0::2, 0:64], in_=psv[:, :, 0::2, 64:128]), pb.T, [V1.T[t0], V1.T[t0 + 1]])
                                      dve(lambda: nc.vector.tensor_copy(out=V1[:, t0:t0 + 2, 1::2, 64:128], in_=psv[:, :, 1::2, 64:128]), pb.T, [V1.T[t0], V1.T[t0 + 1]])
                                      return norm_a(psv, pb.T, t0)

                                  def k_b(kp, i2):
                                      t0 = 2 * kp
                                      o = norm_b(i2, 'k', (t0 if (isS and t0 < 16) else None))
                                      to_T(o, KT[:, :, t0 * 128:(t0 + 2) * 128].rearrange("p h (t k) -> p t h k", t=2), [KT.T[t0], KT.T[t0 + 1]])

                                  chk(1.1 + (0 if isS else 10))
                                  st_ = k_a(0)
                                  for kp in range(NK // 2):
                                      nst = k_a(kp + 1) if kp + 1 < NK // 2 else None
                                      k_b(kp, st_)
                                      st_ = nst
                                  for qb in range(2):
                                      chk(1.4 + (0 if isS else 10))
                                      def q_a(tp):
                                          ti0 = qb * 4 + tp * 2
                                          pb = nxt('pbig')
                                          for t in range(2):
                                              for c in range(3):
                                                  mm(pb[:, t * 512:t * 512 + 384], qlnT[:, c, (ti0 + t) * 128:(ti0 + t + 1) * 128], wq[:, c, hg * 384:(hg + 1) * 384], c == 0, c == 2, [qlnT.T[ti0 + t]] + wq.T, pb.T)
                                          psv = pb[:, :].rearrange("p (t x) -> p t x", t=2)[:, :, 0:384].rearrange("p t (h d) -> p t h d", h=4)
                                          return norm_a(psv, pb.T, None)

                                      def q_b(tp, i2):
                                          ti0 = qb * 4 + tp * 2
                                          o = norm_b(i2, 'q', (ti0 if isS else None))
                                          to_T(o, QT[:, :, tp * 256:(tp + 1) * 256].rearrange("p h (t k) -> p t h k", t=2), [QT.T[tp * 2], QT.T[tp * 2 + 1]])

                                      s0 = q_a(0)
                                      s1 = q_a(1)
                                      q_b(0, s0)
                                      q_b(1, s1)
                                      if isS:
                                          sgroups = [(list(range(4)), list(range(18)))]
                                      else:
                                          sgroups = [([0, 1], [4 * qb, 4 * qb + 1]), ([2, 3], [4 * qb + 2, 4 * qb + 3])]
                                      chk(1.6 + (0 if isS else 10))
                                      stb = psm[0:3]
                                      pO, pDn = psm[3], psm[4]
                                      for c2 in range(2):
                                          for (qts, kts) in sgroups:
                                              q0, q1 = qts[0] * 128, (qts[-1] + 1) * 128
                                              items = [(hh, kt) for hh in (2 * c2, 2 * c2 + 1) for kt in kts]

                                              def emit_st(i_):
                                                  hh_, kt_ = items[i_]
                                                  ps_ = stb[stc[0] % 3]
                                                  stc[0] += 1
                                                  mm(ps_[:, q0:q1], KT[:, hh_, kt_ * 128:(kt_ + 1) * 128], QT[:, hh_, q0:q1], True, True, [KT.T[kt_]] + [QT.T[q] for q in qts], ps_.T)
                                                  return ps_
                                              DEPTH = 2
                                              pend = [emit_st(i_) for i_ in range(min(DEPTH, len(items)))]
                                              for i_, (hh, kt) in enumerate(items):
                                                  ps = pend.pop(0)
                                                  E = Eb[ecnt[0] % 3]
                                                  ecnt[0] += 1
                                                  act(lambda: nc.scalar.activation(out=E[:, q0:q1], in_=ps[:, q0:q1], func=AF.Exp), ps.T, E.T)
                                                  if i_ + DEPTH < len(items):
                                                      pend.append(emit_st(i_ + DEPTH))
                                                  fst, lst = (i_ == 0), (i_ == len(items) - 1)
                                                  mm(pO[:, q0:q1], V1[:, kt, hh, :], E[:, q0:q1], fst, lst, E.T + [V1.T[kt]], pO.T)
                                                  mm(pDn[:, q0:q1], ob[:, hh % 2, :], E[:, q0:q1], fst, lst, E.T + ob.T, pDn.T)
                                              act(lambda: nc.scalar.activation(out=rcpb[:, q0:q1], in_=pDn[:, q0:q1], func=AF.Ln), pDn.T, rcpb.T)
                                              act(lambda: nc.scalar.activation(out=rcpb[:, q0:q1], in_=rcpb[:, q0:q1], func=AF.Exp, scale=-1.0), rcpb.T, rcpb.T)
                                              tq0, tq1 = qb * 512 + q0, qb * 512 + q1
                                              dve(lambda: nc.vector.tensor_tensor(out=hmaT[:, 4 + 2 * hg + c2, tq0:tq1], in0=pO[:, q0:q1], in1=rcpb[:, q0:q1], op=ALU.mult), pO.T + rcpb.T, [hmaT.T[qb * 4 + q] for q in qts])
                          set_psum(2, 2, 2)
                      chk(2 + (0 if isS else 10))
                      set_psum(0, 1, 7)
                      with scope() as es6:
                          if isS:
                              chains = [dict(d=0, units=[(t, True) for t in range(8)], seq=0),
                                        dict(d=1, units=[(t, t < 8) for t in range(15, -1, -1)], seq=0)]
                          else:
                              chains = []
                              for i in range(4):
                                  chains.append(dict(d=0, units=[(2 * i, True), (2 * i + 1, True)], seq=i))
                                  chains.append(dict(d=1, units=[(2 * i + 1, True), (2 * i, True)], seq=i))
                          NCH = len(chains)
                          NB, NR = 3, 3
                          Cst = [B(es6, "Cst%d" % i, [128, 4, 129], F32) for i in range(NCH)]
                          Cbs = [B(es6, "Cbs%d" % i, [128, 4, 129], BF16) for i in range(NCH)]
                          mrow = [[B(es6, "mrow%d_%d" % (i, j), [4, 1], F32) for j in range(3)] for i in range(NCH)]
                          rowsM = B(es6, "rowsM", [4, 16, 128], F32, 16)
                          rowsG = B(es6, "rowsG", [4, 16, 128], F32, 16)
                          TSall = B(es6, "TSall", [128, 2 * NT, 16], F32, 2 * NT)
                          gi = [B(es6, "gi%d" % i, [4, 128], F32) for i in range(NR)]
                          gl = [B(es6, "gl%d" % i, [4, 128], F32) for i in range(NR)]
                          Ar = [B(es6, "Ar%d" % i, [4, 128], F32) for i in range(NR)]
                          gsc = [B(es6, "gsc%d" % i, [4, 128], F32) for i in range(NR)]
                          Msc = [B(es6, "Msc%d" % i, [4, 128], F32) for i in range(NR)]
                          R3 = [B(es6, "R3%d" % i, [4, 3, 128], F32) for i in range(NR)]
                          sm = [B(es6, "sm%d" % i, [4, 8], F32) for i in range(NR)]
                          dd = [B(es6, "dd%d" % i, [4, 4], F32) for i in range(NR)]
                          Yd = [B(es6, "Yd%d" % i, [4, 512], F32) for i in range(NB)]
                          Dm = [B(es6, "Dm%d" % i, [128, 512], F32) for i in range(NB)]
                          Pm = [B(es6, "Pm%d" % i, [128, 512], BF16) for i in range(NB)]
                          kw = [B(es6, "kw%d" % i, [128, 512], BF16) for i in range(NB)]
                          tmpn = [B(es6, "tmpn%d" % i, [128, 4, 129], F32) for i in range(NB)]
                          dn = [B(es6, "dn%d" % i, [128, 8], F32) for i in range(NB)]
                          hsum = B(es6, "hsum", [128, 8, 512], F32, 8)
                          htmp = B(es6, "htmp", [128, 512], F32)
                          hmtok = B(es6, "hmtok", [128, 512], BF16)
                          ones4 = B(es6, "ones4", [4, 128], F32)
                          dve(lambda: nc.vector.memset(ones4[:], 1.0), [], ones4.T)
                          rcnt = [0]
                          hcnt = [0]

                          def rows_a(ci, tile, full, cur):
                              d = chains[ci]['d']
                              k = rcnt[0] % NR
                              rcnt[0] += 1
                              mold, mnew = mrow[ci][cur], mrow[ci][(cur + 1) % 3]
                              rev = (d == 1)
                              r0 = d * 4
                              cs_ = slice(tile * 128, (tile + 1) * 128)
                              gT = [gall.T[tile // 4]]
                              fw.dma('sp', gi[k][:], gall[r0:r0 + 4, cs_], reads=gT, writes=gi[k].T)
                              fw.dma('sp', gl[k][:], gall[32 + r0:32 + r0 + 4, cs_], reads=gT, writes=gl[k].T)
                              V = (lambda ap: ap[:, ::-1]) if rev else (lambda ap: ap)
                              last = 0 if rev else 127
                              A_, R_, s_ = Ar[k], R3[k], sm[k]
                              if full:
                                  slot = d * 8 + tile
                                  g_ap, g_T = rowsG[:, slot, :], [rowsG.T[slot]]
                                  M_ap, M_T = rowsM[:, slot, :], [rowsM.T[slot]]
                              else:
                                  g_ap, g_T = gsc[k][:], gsc[k].T
                                  M_ap, M_T = Msc[k][:], Msc[k].T
                              dve(lambda: nc.vector.tensor_tensor_scan(out=V(A_[:]), data0=V(ones4[:]), data1=V(gl[k][:]), initial=0.0, op0=ALU.mult, op1=ALU.add), gl[k].T + ones4.T, A_.T)
                              dve(lambda: nc.vector.tensor_tensor(out=g_ap, in0=gi[k][:], in1=A_[:], op=ALU.add), gi[k].T + A_.T, g_T)
                              dve(lambda: nc.vector.tensor_tensor_scan(out=V(M_ap), data0=V(g_ap), data1=V(g_ap), initial=mold[:, 0:1], op0=ALU.max, op1=ALU.max), g_T + mold.T, M_T)
                              dve(lambda: nc.vector.tensor_tensor(out=mnew[:, 0:1], in0=M_ap[:, last:last + 1], in1=A_[:, last:last + 1], op=ALU.subtract), M_T + A_.T, mnew.T)
                              dve(lambda: nc.vector.tensor_scalar(out=s_[:, 0:1], in0=M_ap[:, last:last + 1], scalar1=-1.0, scalar2=None, op0=ALU.mult), M_T, s_.T)
                              return (ci, tile, full, cur, k, g_ap, g_T, M_ap, M_T)

                          def rows_b(stt):
                              ci, tile, full, cur, k, g_ap, g_T, M_ap, M_T = stt
                              d = chains[ci]['d']
                              mold = mrow[ci][cur]
                              A_, R_, s_ = Ar[k], R3[k], sm[k]
                              act(lambda: nc.scalar.activation(out=s_[:, 1:2], in_=mold[:, 0:1], func=AF.Exp, bias=s_[:, 0:1], scale=1.0), mold.T + s_.T, s_.T)
                              dve(lambda: nc.vector.tensor_tensor(out=R_[:, 0, :], in0=A_[:], in1=M_ap, op=ALU.subtract), A_.T + M_T, R_.T)
                              act(lambda: nc.scalar.activation(out=R_[:, 0, :], in_=R_[:, 0, :], func=AF.Exp), R_.T, R_.T)
                              act(lambda: nc.scalar.activation(out=R_[:, 1, :], in_=M_ap, func=AF.Exp, bias=mold[:, 0:1], scale=-1.0), M_T + mold.T, R_.T)
                              act(lambda: nc.scalar.activation(out=R_[:, 2, :], in_=g_ap, func=AF.Exp, bias=s_[:, 0:1], scale=1.0), g_T + s_.T, R_.T)
                              ps = nxt('psm')
                              for kk in range(3):
                                  tr(ps[:, kk * 4:(kk + 1) * 4], R_[:, kk, :], identf[0:4, 0:4], R_.T + identf.T, ps.T)
                              dve(lambda: nc.vector.tensor_scalar(out=dd[k][:], in0=identf[0:4, 0:4], scalar1=s_[:, 1:2], scalar2=None, op0=ALU.mult), identf.T + s_.T, dd[k].T)
                              mm(ps[:, 12:16], ones4[:, :], dd[k][:], True, True, ones4.T + dd[k].T, ps.T)
                              ti_ = d * NT + tile
                              dve(lambda: nc.vector.tensor_copy(out=TSall[:, ti_, :], in_=ps[:, 0:16]), ps.T, [TSall.T[ti_]])

                          PB = psm

                          def fa1(u):
                              ci, tile, full, seq_first, k = u
                              if not full:
                                  return
                              d = chains[ci]['d']
                              cs_ = slice(tile * 128, (tile + 1) * 128)
                              slot = d * 8 + tile
                              M_ap, M_T = rowsM[:, slot, :], [rowsM.T[slot]]
                              g_ap, g_T = rowsG[:, slot, :], [rowsG.T[slot]]
                              pD, pS = PB[0], PB[1]
                              dve(lambda: nc.vector.tensor_tensor(out=Yd[k][:].rearrange("k (h t) -> k h t", h=4), in0=nsel4[:].rearrange("k (h t) -> k h t", h=4), in1=M_ap.unsqueeze(1).to_broadcast([4, 4, 128]), op=ALU.mult), nsel4.T + M_T, Yd[k].T)
                              mm(pD[:, :], ones4[:, :], Yd[k][:], True, False, ones4.T + Yd[k].T, pD.T)
                              mm(pD[:, :], g_ap, sel4[:], False, True, g_T + sel4.T, pD.T)
                              for h in range(4):
                                  mm(pS[:, h * 128:(h + 1) * 128], kT[:, h, cs_], qT[:, h, cs_], True, True, [kT.T[tile], qT.T[tile]], pS.T)
                              dve(lambda: nc.vector.tensor_tensor(out=Dm[k][:].rearrange("p (h t) -> p h t", h=4), in0=pD[:, :].rearrange("p (h t) -> p h t", h=4), in1=maskc[:, d, :].unsqueeze(1).to_broadcast([128, 4, 128]), op=ALU.add), pD.T + maskc.T, Dm[k].T)
                              act(lambda: nc.scalar.activation(out=Dm[k][:], in_=Dm[k][:], func=AF.Exp), Dm[k].T, Dm[k].T)
                              dve(lambda: nc.vector.tensor_tensor(out=Pm[k][:], in0=pS[:, :], in1=Dm[k][:], op=ALU.mult), pS.T + Dm[k].T, Pm[k].T)

                          def fa2(u):
                              ci, tile, full, seq_first, k = u
                              if not full:
                                  return
                              pI = [PB[0], PB[2]]
                              for h in range(4):
                                  pq = pI[h // 2]
                                  mm(pq[:, (h % 2) * 129:(h % 2) * 129 + 129], Pm[k][:, h * 128:(h + 1) * 128], v1[:, tile, h, 0:129], True, True, Pm[k].T + [v1.T[tile]], pq.T)
                              for j in range(2):
                                  act(lambda: nc.scalar.copy(out=tmpn[k][:, 2 * j:2 * j + 2, :], in_=pI[j][:, 0:258].rearrange("p (h c) -> p h c", h=2)), pI[j].T, tmpn[k].T)

                          def back(u):
                              ci, tile, full, seq_first, k = u
                              d = chains[ci]['d']
                              ti_ = d * NT + tile
                              tsT = [TSall.T[ti_]]
                              ts = lambda a, b: TSall[:, ti_, a:b]
                              C_, Cb_ = Cst[ci], Cbs[ci]
                              cs_ = slice(tile * 128, (tile + 1) * 128)
                              pX = [PB[3], PB[4]]
                              pU = [PB[5], PB[6]]
                              if full:
                                  for h in range(4):
                                      pq = pX[h // 2]
                                      mm(pq[:, (h % 2) * 129:(h % 2) * 129 + 129], qT[:, h, cs_], Cb_[:, h, :], True, True, [qT.T[tile]] + Cb_.T, pq.T)
                                  for h in range(4):
                                      pq = pX[h // 2]
                                      dve(lambda: nc.vector.scalar_tensor_tensor(out=tmpn[k][:, h, :], in0=pq[:, (h % 2) * 129:(h % 2) * 129 + 129], scalar=ts(4 + h, 5 + h), in1=tmpn[k][:, h, :], op0=ALU.mult, op1=ALU.add), pq.T + tsT + tmpn[k].T, tmpn[k].T)
                                  den = tmpn[k][:, :, 128]
                                  dve(lambda: nc.vector.scalar_tensor_tensor(out=dn[k][:, 0:4], in0=den, scalar=-1.0, in1=den, op0=ALU.mult, op1=ALU.max), tmpn[k].T, dn[k].T)
                                  dve(lambda: nc.vector.tensor_tensor(out=dn[k][:, 0:4], in0=dn[k][:, 0:4], in1=ts(0, 4), op=ALU.max), dn[k].T + tsT, dn[k].T)
                                  dve(lambda: nc.vector.reciprocal(out=dn[k][:, 4:8], in_=dn[k][:, 0:4]), dn[k].T, dn[k].T)
                                  rb = dn[k][:, 4:8].unsqueeze(2).to_broadcast([128, 4, 128])
                                  hv = hsum[:, tile, :].rearrange("p (h c) -> p h c", h=4)
                                  if seq_first:
                                      dve(lambda: nc.vector.tensor_tensor(out=hv, in0=tmpn[k][:, :, 0:128], in1=rb, op=ALU.mult), tmpn[k].T + dn[k].T, [hsum.T[tile]])
                                  else:
                                      dve(lambda: nc.vector.tensor_tensor(out=htmp[:].rearrange("p (h c) -> p h c", h=4), in0=tmpn[k][:, :, 0:128], in1=rb, op=ALU.mult), tmpn[k].T + dn[k].T, htmp.T)
                                      dve(lambda: nc.vector.tensor_tensor(out=hsum[:, tile, :], in0=hsum[:, tile, :], in1=htmp[:], op=ALU.add), htmp.T + [hsum.T[tile]], [hsum.T[tile]])
                              pool(lambda: nc.gpsimd.tensor_tensor(out=kw[k][:].rearrange("p (h c) -> p h c", h=4), in0=ktok[:, tile, :].rearrange("p (h c) -> p h c", h=4), in1=ts(8, 12).unsqueeze(2).to_broadcast([128, 4, 128]), op=ALU.mult), [ktok.T[tile]] + tsT, kw[k].T)
                              for h in range(4):
                                  pq = pU[h // 2]
                                  mm(pq[:, (h % 2) * 129:(h % 2) * 129 + 129], kw[k][:, h * 128:(h + 1) * 128], v1[:, tile, h, 0:129], True, True, kw[k].T + [v1.T[tile]], pq.T)
                              for h in range(4):
                                  pq = pU[h // 2]
                                  dve(lambda: nc.vector.scalar_tensor_tensor(out=C_[:, h, :], in0=C_[:, h, :], scalar=ts(12 + h, 13 + h), in1=pq[:, (h % 2) * 129:(h % 2) * 129 + 129], op0=ALU.mult, op1=ALU.add), C_.T + tsT + pq.T, C_.T)
                              act(lambda: nc.scalar.copy(out=Cb_[:], in_=C_[:]), C_.T, Cb_.T)

                          fsq = [B(es6, "fsq%d" % i, [128, 512], F32) for i in range(2)]
                          sigs = [B(es6, "sigs%d" % i, [128, 512], F32) for i in range(3)]
                          hs2s = [B(es6, "hs2s%d" % i, [128, 4], F32) for i in range(2)]

                          def fz0(tile, j):
                              pool(lambda: nc.gpsimd.tensor_tensor(out=fsq[j % 2][:], in0=hsum[:, tile, :], in1=hsum[:, tile, :], op=ALU.mult), [hsum.T[tile]], fsq[j % 2].T)
                              sg = sigs[j % 3]
                              act(lambda: nc.scalar.activation(out=sg[:], in_=motok[:, tile, :], func=AF.Exp, scale=-1.0), [motok.T[tile]], sg.T)
                              act(lambda: nc.scalar.activation(out=sg[:], in_=sg[:], func=AF.Ln, bias=1.0, scale=1.0), sg.T, sg.T)
                              act(lambda: nc.scalar.activation(out=sg[:], in_=sg[:], func=AF.Exp, scale=-1.0), sg.T, sg.T)

                          def fz1(tile, j):
                              h2 = hs2s[j % 2]
                              hv = hsum[:, tile, :].rearrange("p (h c) -> p h c", h=4)
                              dve(lambda: nc.vector.tensor_reduce(out=h2[:], in_=fsq[j % 2][:].rearrange("p (h c) -> p h c", h=4), axis=AX.X, op=ALU.add), fsq[j % 2].T, h2.T)
                              act(lambda: nc.scalar.activation(out=h2[:], in_=h2[:], func=AF.Ln, bias=EPS, scale=1.0 / 128), h2.T, h2.T)
                              act(lambda: nc.scalar.activation(out=h2[:], in_=h2[:], func=AF.Exp, scale=-0.5), h2.T, h2.T)
                              pool(lambda: nc.gpsimd.tensor_tensor(out=hv, in0=hv, in1=h2[:].unsqueeze(2).to_broadcast([128, 4, 128]), op=ALU.mult), [hsum.T[tile]] + h2.T, [hsum.T[tile]])
                              pool(lambda: nc.gpsimd.tensor_tensor(out=hsum[:, tile, :], in0=hsum[:, tile, :], in1=gmnbc[:], op=ALU.mult), [hsum.T[tile]] + gmnbc.T, [hsum.T[tile]])

                          def fz2(tile, j):
                              sg = sigs[j % 3]
                              dve(lambda: nc.vector.tensor_tensor(out=hmtok[:], in0=hsum[:, tile, :], in1=sg[:], op=ALU.mult), [hsum.T[tile]] + sg.T, hmtok.T)
                              pt = nxt('ptr')
                              for c in range(4):
                                  tr(pt[:, c * 128:(c + 1) * 128], hmtok[:, c * 128:(c + 1) * 128], identb[:], hmtok.T + identb.T, pt.T)
                              act(lambda: nc.scalar.copy(out=hmaT[:, 0:4, tile * 128:(tile + 1) * 128], in_=pt[:, 0:512].rearrange("p (c t) -> p c t", c=4)), pt.T, [hmaT.T[tile]])

                          fq = []
                          fcnt = [0]

                          def advance_fq():
                              nq = []
                              for (stg, tile_, j_) in fq:
                                  if stg == 1:
                                      fz1(tile_, j_)
                                      nq.append((2, tile_, j_))
                                  else:
                                      fz2(tile_, j_)
                              fq[:] = nq

                          for ci, ch in enumerate(chains):
                              d = ch['d']
                              Cc, mc = Cst[ci], mrow[ci][0]
                              if isS:
                                  fw.dma('sp', Cc[:, :, 0:128], st_C[d].rearrange("h k v -> k h v"), writes=Cc.T)
                                  fw.dma('sp', Cc[:, :, 128], st_n[d].rearrange("h k -> k h"), writes=Cc.T, allow_slow_non_contiguous=True)
                                  fw.dma('sp', mc[:, 0:1], st_m[d].unsqueeze(1), writes=mc.T)
                              else:
                                  dve(lambda: nc.vector.memset(Cc[:], 0.0), [], Cc.T)
                                  dve(lambda: nc.vector.memset(mc[:], 0.0), [], mc.T)
                              act(lambda: nc.scalar.copy(out=Cbs[ci][:], in_=Cc[:]), Cc.T, Cbs[ci].T)
                          chk(2.2 + (0 if isS else 10))
                          steps = max(len(ch['units']) for ch in chains)
                          curm = [0] * NCH
                          pend_r = [None]
                          for st in range(steps):
                              for ci, ch in enumerate(chains):
                                  if st < len(ch['units']):
                                      tile, full = ch['units'][st]
                                      stt_ = rows_a(ci, tile, full, curm[ci])
                                      curm[ci] = (curm[ci] + 1) % 3
                                      if pend_r[0] is not None:
                                          rows_b(pend_r[0])
                                      pend_r[0] = stt_
                          if pend_r[0] is not None:
                              rows_b(pend_r[0])
                          chk(2.5 + (0 if isS else 10))
                          visited = set()
                          ulist = []
                          pref, rest = [], []
                          for ch in chains:
                              us = ch['units']
                              j_ = 0
                              while j_ < len(us) and not us[j_][1]:
                                  j_ += 1
                              pref.append(us[:j_])
                              rest.append(us[j_:])
                          order = []
                          for ci in range(NCH):
                              order += [(ci, t_, f_) for (t_, f_) in pref[ci]]
                          for st in range(max(len(r_) for r_ in rest)):
                              for ci in range(NCH):
                                  if st < len(rest[ci]):
                                      order.append((ci,) + tuple(rest[ci][st]))
                          for (ci, tile, full) in order:
                              first = tile not in visited
                              if full and first:
                                  visited.add(tile)
                              ulist.append((ci, tile, full, first, len(ulist) % NB))
                          fa1(ulist[0])
                          fa2(ulist[0])
                          for i_, u in enumerate(ulist):
                              if i_ + 1 < len(ulist):
                                  fa1(ulist[i_ + 1])
                              back(u)
                              if i_ + 1 < len(ulist):
                                  fa2(ulist[i_ + 1])
                              advance_fq()
                              if u[2] and not u[3]:
                                  fz0(u[1], fcnt[0])
                                  fq.append((1, u[1], fcnt[0]))
                                  fcnt[0] += 1
                          while fq:
                              advance_fq()
                          if not isS:
                              for ci, ch in enumerate(chains):
                                  d, si = ch['d'], ch['seq']
                                  Cf, mf = Cst[ci], mrow[ci][curm[ci]]
                                  fw.dma('sp', o_C[si, d].rearrange("h k v -> k h v"), Cf[:, :, 0:128], reads=Cf.T)
                                  fw.dma('sp', o_n[si, d].rearrange("h k -> k h"), Cf[:, :, 128], reads=Cf.T, allow_slow_non_contiguous=True)
                                  fw.dma('sp', o_m[si, d].unsqueeze(1), mf[:, 0:1], reads=mf.T)
                      set_psum(2, 2, 2)

                  chk(3 + (0 if isS else 10))
                  xmid = B(esj, "xmid", [128, 8, 1024], F32, 8)
                  h2T = B(esj, "h2T", [128, 8, 1024], BF16, 8)
                  wu = [B(esj, "wu%d" % i, [128, 8, 1024], BF16) for i in range(2)]
                  wd = [B(esj, "wd%d" % i, [128, 8, 1024], BF16) for i in range(2)]
                  wu_v = w_up.rearrange("(c p) n -> p c n", p=128)
                  wd_v = w_dn.rearrange("(f p) n -> p f n", p=128)

                  def load_q(q):
                      fw.dma('pool', wu[q % 2][:], wu_v[:, :, q * 1024:(q + 1) * 1024], writes=wu[q % 2].T)
                      fw.dma('pool', wd[q % 2][:], wd_v[:, q * 8:(q + 1) * 8, :], writes=wd[q % 2].T)
                  with scope() as es7:
                      wo = B(es7, "wo", [128, 8, 1024], BF16)
                      fw.dma('pool', wo[:], w_out.rearrange("(c p) n -> p c n", p=128), writes=wo.T)
                      load_q(0)
                      load_q(1)
                      tmpx = [B(es7, "tmpx%d" % i, [128, 1024], F32) for i in range(2)]
                      ssb2 = [B(es7, "ssc%d" % i, [128, 4], F32) for i in range(2)]
                      xnb2 = [B(es7, "xnc%d" % i, [128, 1024], BF16) for i in range(2)]
                      def stageA(ti):
                          k2 = ti % 2
                          fw.dma('sp', xmid[:, ti, :], xin[ti * 128:(ti + 1) * 128, :], writes=[xmid.T[ti]])
                          pb = nxt('pbig')
                          for nb in range(2):
                              for c in range(8):
                                  mm(pb[:, nb * 512:(nb + 1) * 512], hmaT[:, c, ti * 128:(ti + 1) * 128], wo[:, c, nb * 512:(nb + 1) * 512], c == 0, c == 7, [hmaT.T[ti]] + wo.T, pb.T)
                          dve(lambda: nc.vector.tensor_tensor(out=tmpx[k2][:], in0=pb[:, :], in1=g1bc[:, cond, :], op=ALU.mult), pb.T + g1bc.T, tmpx[k2].T)
                          pool(lambda: nc.gpsimd.tensor_tensor(out=xmid[:, ti, :], in0=xmid[:, ti, :], in1=tmpx[k2][:], op=ALU.add), tmpx[k2].T + [xmid.T[ti]], [xmid.T[ti]])

                      def stageB(ti):
                          k2 = ti % 2
                          norm_to_hT(None, xmid[:, ti, :], [xmid.T[ti]], cond, 1, h2T[:, :, ti * 128:(ti + 1) * 128], [h2T.T[ti]], (ssb2[k2], xnb2[k2]))

                      stageA(0)
                      for ti in range(8):
                          if ti + 1 < 8:
                              stageA(ti + 1)
                          stageB(ti)

                  chk(4 + (0 if isS else 10))
                  with scope() as es8:
                      ub = [B(es8, "ub%d" % i, [128, 8, 512], BF16, 8) for i in range(2)]
                      rl = [B(es8, "rl%d" % i, [128, 512], BF16) for i in range(2)]
                      tmpy = [B(es8, "tmpy%d" % i, [128, 1024], F32) for i in range(2)]
                      ucnt = 0
                      for q in range(4):
                          wuq, wdq = wu[q % 2], wd[q % 2]
                          if 1 <= q and q + 1 < 4:
                              load_q(q + 1)
                          for g in range(2):
                              u = ub[ucnt % 2]
                              ucnt += 1
                              for fb in range(8):
                                  ps = nxt('psm')
                                  for c in range(8):
                                      mm(ps[:, :], wuq[:, c, fb * 128:(fb + 1) * 128], h2T[:, c, g * 512:(g + 1) * 512], c == 0, c == 7, wuq.T + h2T.T[g * 4:(g + 1) * 4], ps.T)
                                  r_ = rl[fb % 2]
                                  act(lambda: nc.scalar.activation(out=r_[:], in_=ps[:, :], func=AF.Relu), ps.T, r_.T)
                                  dve(lambda: nc.vector.tensor_tensor(out=u[:, fb, :], in0=r_[:], in1=r_[:], op=ALU.mult), r_.T, [u.T[fb]])
                              for tt in range(4):
                                  ti = g * 4 + tt
                                  k2 = ti % 2
                                  pb = nxt('pbig')
                                  for nb in range(2):
                                      for fb in range(8):
                                          mm(pb[:, nb * 512:(nb + 1) * 512], u[:, fb, tt * 128:(tt + 1) * 128], wdq[:, fb, nb * 512:(nb + 1) * 512], fb == 0, fb == 7, [u.T[fb]] + wdq.T, pb.T)
                                  dve(lambda: nc.vector.tensor_tensor(out=tmpy[k2][:], in0=pb[:, :], in1=g2bc[:, cond, :], op=ALU.mult), pb.T + g2bc.T, tmpy[k2].T)
                                  pool(lambda: nc.gpsimd.tensor_tensor(out=xmid[:, ti, :], in0=xmid[:, ti, :], in1=tmpy[k2][:], op=ALU.add), tmpy[k2].T + [xmid.T[ti]], [xmid.T[ti]])
                                  if q == 3:
                                      fw.dma('sp', yout[ti * 128:(ti + 1) * 128, :], xmid[:, ti, :], reads=[xmid.T[ti]])

        except _Stop:
            pass
        fw.finish()
        psum_es[0].close()
    return nc


_PROG = [None]
_DEBUG_HOOK = [None]


def _rope_tables():
    rows = 2048 // 64
    row = np.repeat(np.arange(rows, dtype=np.float32), 64)
    col = np.tile(np.arange(64, dtype=np.float32), rows)
    half = 16
    inv = (np.float32(10000.0) ** (-np.arange(0, half, 2, dtype=np.float32) / np.float32(half))).astype(np.float32)
    ang = np.concatenate([row[:, None] * inv, col[:, None] * inv], axis=-1).astype(np.float32)
    cs_, sn_ = np.cos(ang), np.sin(ang)
    return np.concatenate([cs_, cs_, -sn_, sn_], axis=-1).astype(np.float32)


def kernel(x_prompt, x_sample, cache_mla_ckv, cache_mla_krope, state_mlstm_C, state_mlstm_n, state_mlstm_m,
           c, c_ctx, norm1_g, norm2_g, w_ada, b_ada, w_in, mlstm_gate_b, mlstm_norm_g,
           q_lora_g, kv_lora_g, w_q_up, w_kv_up, q_head_g, k_head_g, w_out, w_mlp_up, w_mlp_down):
    f = lambda a: np.ascontiguousarray(np.asarray(a, dtype=np.float32))
    x_prompt, x_sample = f(x_prompt), f(x_sample)
    w_in0 = f(w_in)[0]
    if _PROG[0] is None:
        _PROG[0] = build_program()
    nc = _PROG[0]
    rope = _rope_tables()
    ident = np.eye(128, dtype=np.float32)
    s_idx = np.arange(128)[:, None]
    t_idx = np.arange(128)[None, :]
    mask = np.stack([np.where(s_idx <= t_idx, 0.0, NEG), np.where(s_idx >= t_idx, 0.0, NEG)]).astype(np.float32)
    sel = np.zeros((4, 4, 128), np.float32)
    for k in range(4):
        sel[k, k, :] = 1.0
    sel = sel.reshape(4, 512)
    w_main = np.ascontiguousarray(np.concatenate([w_in0[:, 0:2048], w_in0[:, 2064:2736]], axis=1))
    gcols = w_in0[:, 2048:2064]
    gb = f(mlstm_gate_b)[0]
    in_maps = []
    for core in range(8):
        s, p = core // 2, core % 2
        flip = (p == 1)
        xp_ = x_prompt[4 * core:4 * core + 4]
        xs_ = x_sample[s]
        stC, stn, stm = f(state_mlstm_C)[s, 0], f(state_mlstm_n)[s, 0], f(state_mlstm_m)[s, 0]
        rp = rope
        gc, gbb = gcols, gb
        if flip:
            xp_ = xp_[:, ::-1]
            xs_ = xs_[::-1]
            rp = rope[::-1]
            stC, stn, stm = stC[::-1], stn[::-1], stm[::-1]
            gc = np.concatenate([gcols[:, 8:16], gcols[:, 0:8]], axis=1)
            gbb = np.concatenate([gb[8:16], gb[0:8]])
        gc40 = np.zeros((1024, 40), np.float32)
        gb40 = np.zeros((40, 1), np.float32)
        for d_ in range(2):
            gc40[:, d_ * 4:(d_ + 1) * 4] = gc[:, d_ * 8:d_ * 8 + 4]
            gc40[:, 32 + d_ * 4:32 + (d_ + 1) * 4] = gc[:, d_ * 8 + 4:d_ * 8 + 8]
            gb40[d_ * 4:(d_ + 1) * 4, 0] = gbb[d_ * 8:d_ * 8 + 4]
            gb40[32 + d_ * 4:32 + (d_ + 1) * 4, 0] = gbb[d_ * 8 + 4:d_ * 8 + 8]
        in_maps.append({
            "xp": np.ascontiguousarray(xp_.reshape(1024, 1024)),
            "xs": np.ascontiguousarray(xs_),
            "cvec": np.ascontiguousarray(np.stack([f(c_ctx), f(c)[s]])),
            "w_ada": f(w_ada)[0], "b_ada": f(b_ada)[0][None, :],
            "norm_g": np.ascontiguousarray(np.stack([f(norm1_g)[0], f(norm2_g)[0]])),
            "w_in": w_main, "w_gate": gc40, "gate_b": gb40,
            "rope_cs": np.ascontiguousarray(rp),
            "ctx_ckv": f(cache_mla_ckv)[s, 0], "ctx_kr": f(cache_mla_krope)[s, 0],
            "st_C": np.ascontiguousarray(stC), "st_n": np.ascontiguousarray(stn), "st_m": np.ascontiguousarray(stm),
            "g_q": f(q_lora_g), "g_kv": f(kv_lora_g), "g_qh": f(q_head_g), "g_kh": f(k_head_g), "g_mn": f(mlstm_norm_g),
            "w_q_up": f(w_q_up)[0], "w_kv_up": f(w_kv_up)[0], "w_out": f(w_out)[0],
            "w_up": f(w_mlp_up)[0], "w_dn": f(w_mlp_down)[0],
            "c_ident": ident, "c_mask": mask, "c_sel": sel, "c_nsel": np.ascontiguousarray(-sel),
        })
    if _DEBUG_HOOK[0] is not None:
        return _DEBUG_HOOK[0](in_maps)
    res = run_bass_kernel_spmd(nc, in_maps, core_ids=list(range(8)))
    R = res.results
    y_p = np.zeros((32, 256, 1024), np.float32)
    y_s = np.zeros((4, 2048, 1024), np.float32)
    o_ckv = np.zeros((32, 1, 256, 256), np.float32)
    o_kr = np.zeros((32, 1, 256, 32), np.float32)
    o_C = np.zeros((32, 1, 2, 4, 128, 128), np.float32)
    o_n = np.zeros((32, 1, 2, 4, 128), np.float32)
    o_m = np.zeros((32, 1, 2, 4), np.float32)
    for core in range(8):
        s, p = core // 2, core % 2
        r = R[core]
        yp_ = r["yp"].reshape(4, 256, 1024)
        ck = r["o_ckv"].reshape(4, 256, 256)
        kr_ = r["o_kr"].reshape(4, 256, 32)
        C_, n_, m_ = r["o_C"], r["o_n"], r["o_m"]
        ys_ = r["ys"]
        if p == 1:
            yp_, ck, kr_ = yp_[:, ::-1], ck[:, ::-1], kr_[:, ::-1]
            C_, n_, m_ = C_[:, ::-1], n_[:, ::-1], m_[:, ::-1]
            y_s[s, 1024:2048] = ys_[::-1]
        else:
            y_s[s, 0:1024] = ys_
        y_p[4 * core:4 * core + 4] = yp_
        o_ckv[4 * core:4 * core + 4, 0] = ck
        o_kr[4 * core:4 * core + 4, 0] = kr_
        o_C[4 * core:4 * core + 4, 0] = C_
        o_n[4 * core:4 * core + 4, 0] = n_
        o_m[4 * core:4 * core + 4, 0] = m_
    return (y_p, y_s, o_ckv, o_kr, o_C, o_n, o_m)
```

```python
import numpy as np
from contextlib import ExitStack, contextmanager
import concourse.bass as bass
import concourse.mybir as mybir
from concourse.bass_utils import run_bass_kernel_spmd

F32 = mybir.dt.float32
BF16 = mybir.dt.bfloat16
AF = mybir.ActivationFunctionType
ALU = mybir.AluOpType
AX = mybir.AxisListType
EPS = 1e-6
NEG = -30000.0


class T:
    def __init__(self, name):
        self.name = name
        self.w = []
        self.r = []
        self.dsem = None
        self.dcnt = 0


class FW:
    def __init__(self, nc, es):
        self.nc = nc
        self.es = es
        self.eng = {'pe': nc.tensor, 'act': nc.scalar, 'dve': nc.vector, 'pool': nc.gpsimd, 'sp': nc.sync}
        self.sem = {k: es.enter_context(nc.semaphore("s_" + k)) for k in ('pe', 'act', 'dve', 'pool')}
        self.cnt = {k: 0 for k in self.sem}
        self.waited = {k: {} for k in self.eng}
        self.dma_ts = []
        self.dsem_pool = []
        self.dead = False
        self.limit = None
        self.count2 = None
        self.sem_total = {}

    def _wait(self, e, deps):
        need = {}
        for (s, v) in deps:
            if v > need.get(id(s), (s, 0))[1]:
                need[id(s)] = (s, v)
        for (s, v) in need.values():
            if e == 'pe' and s is self.sem['pe']:
                continue
            if self.waited[e].get(id(s), 0) >= v:
                continue
            self.eng[e].wait_ge(s, v)
            self.waited[e][id(s)] = v

    def op(self, e, fn, reads=(), writes=()):
        if self.count2 is not None and self.limit is not None:
            self.count2 += 1
            if self.count2 > self.limit:
                self.dead = True
        if self.dead:
            return None
        deps = []
        for t in reads:
            deps += t.w
        for t in writes:
            deps += t.w
            deps += t.r
        self._wait(e, deps)
        ins = fn()
        self.cnt[e] += 1
        ins.then_inc(self.sem[e], 1)
        tok = (self.sem[e], self.cnt[e])
        for t in reads:
            t.r.append(tok)
            if len(t.r) > 24:
                t.r = self._compact(t.r)
        for t in writes:
            t.w = [tok]
            t.r = []
        return ins

    @staticmethod
    def _compact(lst):
        m = {}
        for (s, v) in lst:
            if v > m.get(id(s), (s, 0))[1]:
                m[id(s)] = (s, v)
        return list(m.values())

    def dma(self, q, out, in_, reads=(), writes=(), **kw):
        if self.dead:
            return None
        deps = []
        for t in reads:
            deps += t.w
        for t in writes:
            deps += t.w
            deps += t.r
        self._wait(q, deps)
        own = (writes[0] if writes else reads[0])
        if own.dsem is not None and own.dq != ('sw' if q == 'pool' else 'hw'):
            raise RuntimeError('mixed dma queues on tile ' + own.name)
        if own.dsem is None:
            own.dq = 'sw' if q == 'pool' else 'hw'
            pl = [x for x in self.dsem_pool if x[2] == own.dq]
            if pl:
                ent = pl[-1]
                self.dsem_pool.remove(ent)
                own.dsem, own.dcnt = ent[0], ent[1]
            else:
                own.dsem = self.es.enter_context(self.nc.semaphore("d%d" % len(self.dma_ts)))
                own.dcnt = 0
                self.dma_ts.append(own.dsem)
        if own.dcnt > 0:
            self._wait(q, [(own.dsem, own.dcnt)])
        own.dcnt += 16
        self.sem_total[id(own.dsem)] = (own.dsem, own.dcnt)
        ins = self.eng[q].dma_start(out=out, in_=in_, **kw)
        ins.then_inc(own.dsem, 16)
        tok = (own.dsem, own.dcnt)
        for t in reads:
            t.r.append(tok)
        for t in writes:
            t.w = [tok]
            t.r = []
        return ins

    def barrier(self):
        if self.dead:
            return
        toks = [(self.sem[e], self.cnt[e]) for e in self.sem if self.cnt[e] > 0] + list(self.sem_total.values())
        for e in ('pe', 'act', 'dve', 'pool', 'sp'):
            self._wait(e, toks)

    def release(self, t):
        if t.dsem is not None:
            self.dsem_pool.append((t.dsem, t.dcnt, t.dq))
            t.dsem = None

    def finish(self):
        for (sem, cnt) in self.sem_total.values():
            self.eng['sp'].wait_ge(sem, cnt)


class Buf:
    def __init__(self, es, nc, name, shape, dt, ntr=1, psum=False, fw=None):
        self.T = [T(name + str(i)) for i in range(ntr)]
        if fw is not None:
            for t in self.T:
                es.callback(fw.release, t)
        if psum:
            self.t = es.enter_context(nc.psum_tensor(name, shape, dt))
        else:
            self.t = es.enter_context(nc.sbuf_tensor(name, shape, dt))

    def __getitem__(self, k):
        return self.t[k]


class _Stop(Exception):
    pass


def build_program(stage=99):
    nc = bass.Bass("TRN2", target_bir_lowering=False)

    fwbox = [None]

    def chk(k):
        if stage <= k:
            fwbox[0].dead = True

    def din(name, shape):
        return nc.dram_tensor(name, shape, F32, kind="ExternalInput").ap()

    def dout(name, shape):
        return nc.dram_tensor(name, shape, F32, kind="ExternalOutput").ap()

    xp = din("xp", [1024, 1024])
    xs = din("xs", [2048, 1024])
    cvec = din("cvec", [2, 1024])
    w_ada = din("w_ada", [1024, 6144])
    b_ada = din("b_ada", [1, 6144])
    norm_g = din("norm_g", [2, 1024])
    w_in = din("w_in", [1024, 2720])
    w_gate = din("w_gate", [1024, 40])
    gate_b = din("gate_b", [40, 1])
    rope_cs = din("rope_cs", [2048, 64])
    ctx_ckv = din("ctx_ckv", [256, 256])
    ctx_kr = din("ctx_kr", [256, 32])
    st_C = din("st_C", [2, 4, 128, 128])
    st_n = din("st_n", [2, 4, 128])
    st_m = din("st_m", [2, 4])
    g_q = din("g_q", [1, 384])
    g_kv = din("g_kv", [1, 256])
    g_qh = din("g_qh", [1, 96])
    g_kh = din("g_kh", [1, 96])
    g_mn = din("g_mn", [1, 512])
    w_q_up = din("w_q_up", [384, 768])
    w_kv_up = din("w_kv_up", [256, 1024])
    w_out = din("w_out", [1024, 1024])
    w_up = din("w_up", [1024, 4096])
    w_dn = din("w_dn", [4096, 1024])
    c_ident = din("c_ident", [128, 128])
    c_mask = din("c_mask", [2, 128, 128])
    c_sel = din("c_sel", [4, 512])
    c_nsel = din("c_nsel", [4, 512])

    yp = dout("yp", [1024, 1024])
    ys = dout("ys", [1024, 1024])
    o_ckv = dout("o_ckv", [1024, 256])
    o_kr = dout("o_kr", [1024, 32])
    o_C = dout("o_C", [4, 2, 4, 128, 128])
    o_n = dout("o_n", [4, 2, 4, 128])
    o_m = dout("o_m", [4, 2, 4])

    with ExitStack() as es0:
        fw = FW(nc, es0)
        fwbox[0] = fw

        @contextmanager
        def scope():
            with ExitStack() as es_:
                yield es_
            fw.barrier()

        sfx = [""]

        def B(es, name, shape, dt, ntr=1):
            return Buf(es, nc, name + sfx[0], shape, dt, ntr, fw=fw)

        def P(es, name, shape, dt):
            return Buf(es, nc, name, shape, dt, 1, psum=True)

        def act(fn, r, w):
            return fw.op('act', fn, r, w)

        def dve(fn, r, w):
            return fw.op('dve', fn, r, w)

        def pool(fn, r, w):
            return fw.op('pool', fn, r, w)

        def pe(fn, r, w):
            return fw.op('pe', fn, r, w)

        def mm(out, lhsT, rhs, start, stop, r, w):
            return fw.op('pe', lambda: nc.tensor.matmul(out, lhsT=lhsT, rhs=rhs, start=start, stop=stop), r, w)

        def tr(out, in_, ident, r, w):
            return fw.op('pe', lambda: nc.tensor.transpose(out, in_, ident), r, w)

        pbig, ptr, psm = [], [], []
        rot = {'pbig': 0, 'ptr': 0, 'psm': 0}
        psum_es = [None]
        pgen = [0]

        def set_psum(nbig, ntr, nsm):
            assert 2 * nbig + ntr + nsm <= 8
            if psum_es[0] is not None:
                fw.barrier()
                psum_es[0].close()
            psum_es[0] = ExitStack()
            pgen[0] += 1
            g = pgen[0]
            pbig[:] = [P(psum_es[0], "pbig%d_%d" % (i, g), [128, 1024], F32) for i in range(nbig)]
            ptr[:] = [P(psum_es[0], "ptr%d_%d" % (i, g), [128, 1024], BF16) for i in range(ntr)]
            psm[:] = [P(psum_es[0], "psm%d_%d" % (i, g), [128, 512], F32) for i in range(nsm)]

        def nxt(kind):
            lst = {'pbig': pbig, 'ptr': ptr, 'psm': psm}[kind]
            rot[kind] = (rot[kind] + 1) % len(lst)
            return lst[rot[kind]]

        set_psum(2, 2, 2)

        identf = B(es0, "identf", [128, 128], F32)
        identb = B(es0, "identb", [128, 128], BF16)
        maskc = B(es0, "maskc", [128, 2, 128], F32)
        sel4 = B(es0, "sel4", [4, 512], F32)
        nsel4 = B(es0, "nsel4", [4, 512], F32)
        onesf = B(es0, "onesf", [128, 512], F32)
        junk = B(es0, "junk", [128, 1024], BF16)
        gqbc = B(es0, "gqbc", [128, 384], F32)
        gkvbc = B(es0, "gkvbc", [128, 256], F32)
        gqhbc = B(es0, "gqhbc", [128, 96], F32)
        gkhbc = B(es0, "gkhbc", [128, 96], F32)
        gmnbc = B(es0, "gmnbc", [128, 512], F32)
        gbias = B(es0, "gbias", [40, 2], F32)
        g1bc = B(es0, "g1bc", [128, 2, 1024], F32)
        g2bc = B(es0, "g2bc", [128, 2, 1024], F32)
        ab = B(es0, "ab", [128, 2, 4, 8], F32)

        fw.dma('sp', identf[:], c_ident, writes=identf.T)
        fw.dma('pool', identb[:], c_ident, writes=identb.T)
        fw.dma('sp', maskc[:], c_mask.rearrange("d s t -> s d t"), writes=maskc.T)
        fw.dma('sp', sel4[:], c_sel, writes=sel4.T)
        fw.dma('sp', nsel4[:], c_nsel, writes=nsel4.T)
        dve(lambda: nc.vector.memset(onesf[:], 1.0), [], onesf.T)
        fw.dma('sp', gqbc[:], g_q.partition_broadcast(128), writes=gqbc.T)
        fw.dma('sp', gkvbc[:], g_kv.partition_broadcast(128), writes=gkvbc.T)
        fw.dma('sp', gqhbc[:], g_qh.partition_broadcast(128), writes=gqhbc.T)
        fw.dma('sp', gkhbc[:], g_kh.partition_broadcast(128), writes=gkhbc.T)
        fw.dma('sp', gmnbc[:], g_mn.partition_broadcast(128), writes=gmnbc.T)
        fw.dma('sp', gbias[:, 0:1], gate_b, writes=gbias.T)
        dve(lambda: nc.vector.tensor_scalar(out=gbias[:, 1:2], in0=gbias[:, 0:1], scalar1=-1.0, scalar2=None, op0=ALU.mult), gbias.T, gbias.T)
        dve(lambda: nc.vector.tensor_scalar(out=gqhbc[:], in0=gqhbc[:], scalar1=float(96 ** -0.5), scalar2=None, op0=ALU.mult), gqhbc.T, gqhbc.T)

        with scope() as es:
            cT = B(es, "cT", [128, 2, 8], F32)
            ngT = B(es, "ngT", [128, 2, 8], F32)
            ce = B(es, "ce", [128, 2, 8], F32)
            scT = B(es, "scT", [128, 8, 33], BF16)
            bada = B(es, "bada", [33, 6144], F32)
            modrow = B(es, "modrow", [33, 6144], F32)
            wab = [B(es, "wab%d" % i, [128, 8, 512], BF16) for i in range(2)]
            for r in range(2):
                fw.dma('sp', cT[:, r, :], cvec[r].rearrange("(c p) -> p c", p=128), writes=cT.T, allow_slow_non_contiguous=True)
                fw.dma('sp', ngT[:, r, :], norm_g[r].rearrange("(c p) -> p c", p=128), writes=ngT.T, allow_slow_non_contiguous=True)
            dve(lambda: nc.vector.memset(bada[:], 0.0), [], bada.T)
            fw.dma('sp', bada[0:1, :], b_ada, writes=bada.T)
            fw.dma('sp', bada[32:33, :], b_ada, writes=bada.T)
            act(lambda: nc.scalar.activation(out=ce[:], in_=cT[:], func=AF.Exp, scale=-1.0), cT.T, ce.T)
            dve(lambda: nc.vector.tensor_scalar(out=ce[:], in0=ce[:], scalar1=1.0, scalar2=None, op0=ALU.add), ce.T, ce.T)
            dve(lambda: nc.vector.reciprocal(out=ce[:], in_=ce[:]), ce.T, ce.T)
            dve(lambda: nc.vector.memset(scT[:], 0.0), [], scT.T)
            dve(lambda: nc.vector.tensor_tensor(out=scT[:, :, 0], in0=cT[:, 0, :], in1=ce[:, 0, :], op=ALU.mult), cT.T + ce.T, scT.T)
            dve(lambda: nc.vector.tensor_tensor(out=scT[:, :, 32], in0=cT[:, 1, :], in1=ce[:, 1, :], op=ALU.mult), cT.T + ce.T, scT.T)
            wa_v = w_ada.rearrange("(c p) n -> p c n", p=128)
            fw.dma('pool', wab[0][:], wa_v[:, :, 0:512], writes=wab[0].T)
            for blk in range(12):
                wb = wab[blk % 2]
                if blk + 1 < 12:
                    fw.dma('pool', wab[(blk + 1) % 2][:], wa_v[:, :, (blk + 1) * 512:(blk + 2) * 512], writes=wab[(blk + 1) % 2].T)
                ps = nxt('psm')
                for c in range(8):
                    mm(ps[0:33, :], scT[:, c, :], wb[:, c, :], c == 0, c == 7, scT.T + wb.T, ps.T)
                dve(lambda: nc.vector.tensor_tensor(out=modrow[:, blk * 512:(blk + 1) * 512], in0=ps[0:33, :], in1=bada[:, blk * 512:(blk + 1) * 512], op=ALU.add), ps.T + bada.T, modrow.T)
            for cond in range(2):
                r = 32 * cond
                for (dst, off) in ((g1bc, 2048), (g2bc, 5120)):
                    for hh in range(2):
                        ps = nxt('psm')
                        mm(ps[:, :], onesf[r:r + 1, 0:128], modrow[r:r + 1, off + hh * 512: off + (hh + 1) * 512], True, True, onesf.T + modrow.T, ps.T)
                        act(lambda: nc.scalar.copy(out=dst[:, cond, hh * 512:(hh + 1) * 512], in_=ps[:, :]), ps.T, dst.T)
                ps = nxt('psm')
                for vi, off in enumerate((0, 1024, 3072, 4096)):
                    for c in range(8):
                        mm(ps[:, vi * 8 + c: vi * 8 + c + 1], modrow[r:r + 1, off + c * 128: off + (c + 1) * 128], onesf[r:r + 1, 0:1], True, True, onesf.T + modrow.T, ps.T)
                dve(lambda: nc.vector.scalar_tensor_tensor(out=ab[:, cond, 0, :], in0=ps[:, 8:16], scalar=1.0, in1=ngT[:, 0, :], op0=ALU.add, op1=ALU.mult), ps.T + ngT.T, ab.T)
                dve(lambda: nc.vector.tensor_copy(out=ab[:, cond, 1, :], in_=ps[:, 0:8]), ps.T, ab.T)
                dve(lambda: nc.vector.scalar_tensor_tensor(out=ab[:, cond, 2, :], in0=ps[:, 24:32], scalar=1.0, in1=ngT[:, 1, :], op0=ALU.add, op1=ALU.mult), ps.T + ngT.T, ab.T)
                dve(lambda: nc.vector.tensor_copy(out=ab[:, cond, 3, :], in_=ps[:, 16:24]), ps.T, ab.T)

        def rstd_from_mean(ss_ap, ss_T):
            act(lambda: nc.scalar.activation(out=ss_ap, in_=ss_ap, func=AF.Ln, bias=EPS, scale=1.0), ss_T, ss_T)
            act(lambda: nc.scalar.activation(out=ss_ap, in_=ss_ap, func=AF.Exp, scale=-0.5), ss_T, ss_T)

        def norm_to_hT(es_unused, xt_ap, xt_T, cond, which, dst_ap, dst_T, wk):
            ss, xn = wk
            act(lambda: nc.scalar.activation(out=junk[:], in_=xt_ap, func=AF.Square, scale=float(1024 ** -0.5), accum_out=ss[:, 0:1]), xt_T, junk.T + ss.T)
            rstd_from_mean(ss[:, 0:1], ss.T)
            dve(lambda: nc.vector.tensor_scalar(out=xn[:], in0=xt_ap, scalar1=ss[:, 0:1], scalar2=None, op0=ALU.mult), xt_T + ss.T, xn.T)
            pt = nxt('ptr')
            for c in range(8):
                tr(pt[:, c * 128:(c + 1) * 128], xn[:, c * 128:(c + 1) * 128], identb[:], xn.T + identb.T, pt.T)
            ptv = pt[:, :].rearrange("p (c t) -> p c t", c=8)
            a_ap = ab[:, cond, 2 * which, :].unsqueeze(2).to_broadcast([128, 8, 128])
            b_ap = ab[:, cond, 2 * which + 1, :].unsqueeze(2).to_broadcast([128, 8, 128])
            dve(lambda: nc.vector.tensor_tensor(out=dst_ap, in0=ptv, in1=a_ap, op=ALU.mult), pt.T + ab.T, dst_T)
            dve(lambda: nc.vector.tensor_tensor(out=dst_ap, in0=dst_ap, in1=b_ap, op=ALU.add), dst_T + ab.T, dst_T)

        try:
          for sj in ('S', 'P'):
              chk(0 if sj == 'S' else 10)
              isS = sj == 'S'
              sfx[0] = "_" + sj
              cond = 1 if isS else 0
              xin = xs if isS else xp
              yout = ys if isS else yp
              NT = 16 if isS else 8
              NOWN = 8
              NK = 18 if isS else 8
              if isS:
                  seqs = [dict(tiles=list(range(16)), own=list(range(8)), ktiles=list(range(18)))]
              else:
                  seqs = [dict(tiles=[2 * i, 2 * i + 1], own=[2 * i, 2 * i + 1], ktiles=[2 * i, 2 * i + 1]) for i in range(4)]
              with scope() as esj:
                  hmaT = B(esj, "hmaT", [128, 8, 1024], BF16, 8)
                  with scope() as es2:
                      qT = B(es2, "qT", [128, 4, 1024], BF16, 8)
                      kT = B(es2, "kT", [128, 4, 1024], BF16, 8)
                      ktok = B(es2, "ktok", [128, NT, 512], BF16, NT)
                      v1 = B(es2, "v1", [128, NT, 4, 130], BF16, NT)
                      motok = B(es2, "motok", [128, 8, 512], BF16, 8)
                      gall = B(es2, "gall", [40, NT * 128], F32, NT // 4)
                      dve(lambda: nc.vector.memset(v1[:, :, :, 128:129], 1.0), [], v1.T)
                      with scope() as es3:
                          qlnT = B(es3, "qlnT", [128, 3, 1024], BF16, 8)
                          ckvT = B(es3, "ckvT", [128, 2, NK * 128], BF16, NK)
                          kr = B(es3, "kr", [128, NK, 32], F32, NK)
                          with scope() as es4:
                              wi = B(es4, "wi", [128, 8, 2720], BF16, 8)
                              wg = B(es4, "wg", [128, 8, 40], BF16)
                              wi_v = w_in.rearrange("(c p) n -> p c n", p=128)
                              for c in range(8):
                                  fw.dma('pool', wi[:, c, :], wi_v[:, c, :], writes=[wi.T[c]])
                              fw.dma('pool', wg[:], w_gate.rearrange("(c p) n -> p c n", p=128), writes=wg.T)
                              NXB = 1 if isS else 2
                              xt = [B(es4, "xt%d" % i, [128, 1024], F32) for i in range(NXB)]
                              xnb = [B(es4, "xnb%d" % i, [128, 1024], BF16) for i in range(NXB)]
                              ssb = [B(es4, "ssb%d" % i, [128, 4], F32) for i in range(2)]
                              hTg = [B(es4, "hTg%d" % i, [128, 8, 512], BF16, 4) for i in range(2)]
                              latf = [B(es4, "latf%d" % i, [128, 672], F32) for i in range(NXB)]
                              lss = [B(es4, "lss%d" % i, [128, 2], F32) for i in range(2)]
                              qlnb = [B(es4, "qlnb%d" % i, [128, 384], BF16) for i in range(2)]
                              ckvf = [B(es4, "ckvf%d" % i, [128, 256], F32) for i in range(2)]
                              ckvb = [B(es4, "ckvb%d" % i, [128, 256], BF16) for i in range(2)]
                              NHT = 2

                              def norms(grp):
                                  hg_ = hTg[grp % NHT]
                                  for tt in range(4):
                                      ti = grp * 4 + tt
                                      k2 = ti % NXB
                                      fw.dma('sp', xt[k2][:], xin[ti * 128:(ti + 1) * 128, :], writes=xt[k2].T)
                                      norm_to_hT(None, xt[k2][:], xt[k2].T, cond, 0, hg_[:, :, tt * 128:(tt + 1) * 128], [hg_.T[tt]], (ssb[ti % 2], xnb[k2]))

                              def projs(grp):
                                  hg_ = hTg[grp % NHT]
                                  own_grp = (grp * 4) < NOWN
                                  if own_grp:
                                      for (dst, cb) in ((qT, 0), (kT, 512)):
                                          for h in range(4):
                                              ps = nxt('psm')
                                              for c in range(8):
                                                  mm(ps[:, :], wi[:, c, cb + h * 128: cb + (h + 1) * 128], hg_[:, c, :], c == 0, c == 7, [wi.T[c]] + hg_.T, ps.T)
                                              act(lambda: nc.scalar.mul(out=dst[:, h, grp * 512:(grp + 1) * 512], in_=ps[:, :], mul=(float(128 ** -0.5) if cb == 512 else 1.0)), ps.T, dst.T[grp * 4:(grp + 1) * 4])
                                  ps = nxt('psm')
                                  for c in range(8):
                                      mm(ps[0:40, :], wg[:, c, :], hg_[:, c, :], c == 0, c == 7, wg.T + hg_.T, ps.T)
                                  gs = slice(grp * 512, (grp + 1) * 512)
                                  act(lambda: nc.scalar.activation(out=gall[0:8, gs], in_=ps[0:8, :], func=AF.Identity, bias=gbias[0:8, 0:1], scale=1.0), ps.T + gbias.T, [gall.T[grp]])
                                  act(lambda: nc.scalar.activation(out=gall[32:40, gs], in_=ps[32:40, :], func=AF.Exp, bias=gbias[32:40, 1:2], scale=-1.0), ps.T + gbias.T, [gall.T[grp]])
                                  act(lambda: nc.scalar.activation(out=gall[32:40, gs], in_=gall[32:40, gs], func=AF.Ln, bias=1.0, scale=1.0), [gall.T[grp]], [gall.T[grp]])
                                  for tt in range(4):
                                      ti = grp * 4 + tt
                                      k2 = ti % 2
                                      lh = lambda c: hg_[:, c, tt * 128:(tt + 1) * 128]
                                      pb = nxt('pbig')
                                      for nb, cb in enumerate((512, 1024)):
                                          for c in range(8):
                                              mm(pb[:, nb * 512:(nb + 1) * 512], lh(c), wi[:, c, cb:cb + 512], c == 0, c == 7, [wi.T[c], hg_.T[tt]], pb.T)
                                      act(lambda: nc.scalar.mul(out=ktok[:, ti, :], in_=pb[:, 0:512], mul=float(128 ** -0.5)), pb.T, [ktok.T[ti]])
                                      act(lambda: nc.scalar.copy(out=v1[:, ti, :, 0:128], in_=pb[:, 512:1024].rearrange("p (h d) -> p h d", h=4)), pb.T, [v1.T[ti]])
                                      pb = nxt('pbig')
                                      if own_grp:
                                          for c in range(8):
                                              mm(pb[:, 0:512], lh(c), wi[:, c, 1536:2048], c == 0, c == 7, [wi.T[c], hg_.T[tt]], pb.T)
                                          act(lambda: nc.scalar.copy(out=motok[:, ti, :], in_=pb[:, 0:512]), pb.T, [motok.T[ti]])
                                          pb = nxt('pbig')
                                          for (o0, o1, cb) in ((0, 512, 2048), (512, 672, 2560)):
                                              for c in range(8):
                                                  mm(pb[:, o0:o1], lh(c), wi[:, c, cb:cb + (o1 - o0)], c == 0, c == 7, [wi.T[c], hg_.T[tt]], pb.T)
                                          dve(lambda: nc.vector.tensor_copy(out=latf[k2 % NXB][:], in_=pb[:, 0:672]), pb.T, latf[k2 % NXB].T)
                                      else:
                                          for c in range(8):
                                              mm(pb[:, 512:800], lh(c), wi[:, c, 2432:2720], c == 0, c == 7, [wi.T[c], hg_.T[tt]], pb.T)
                                          dve(lambda: nc.vector.tensor_copy(out=latf[k2 % NXB][:, 384:672], in_=pb[:, 512:800]), pb.T, latf[k2 % NXB].T)
                                      lf_ = latf[k2 % NXB]
                                      ls = lss[k2]
                                      if own_grp:
                                          act(lambda: nc.scalar.activation(out=junk[:, 0:384], in_=lf_[:, 0:384], func=AF.Square, scale=float(384 ** -0.5), accum_out=ls[:, 0:1]), lf_.T, junk.T + ls.T)
                                      else:
                                          dve(lambda: nc.vector.memset(ls[:, 0:1], 1.0), [], ls.T)
                                      act(lambda: nc.scalar.activation(out=junk[:, 0:256], in_=lf_[:, 384:640], func=AF.Square, scale=float(256 ** -0.5), accum_out=ls[:, 1:2]), lf_.T, junk.T + ls.T)
                                      rstd_from_mean(ls[:, 0:2], ls.T)
                                      if own_grp:
                                          dve(lambda: nc.vector.scalar_tensor_tensor(out=qlnb[k2][:], in0=lf_[:, 0:384], scalar=ls[:, 0:1], in1=gqbc[:], op0=ALU.mult, op1=ALU.mult), lf_.T + ls.T + gqbc.T, qlnb[k2].T)
                                          pt = nxt('ptr')
                                          for c in range(3):
                                              tr(pt[:, c * 128:(c + 1) * 128], qlnb[k2][:, c * 128:(c + 1) * 128], identb[:], qlnb[k2].T + identb.T, pt.T)
                                          act(lambda: nc.scalar.copy(out=qlnT[:, :, ti * 128:(ti + 1) * 128], in_=pt[:, 0:384].rearrange("p (c t) -> p c t", c=3)), pt.T, [qlnT.T[ti]])
                                      dve(lambda: nc.vector.scalar_tensor_tensor(out=ckvf[k2][:], in0=lf_[:, 384:640], scalar=ls[:, 1:2], in1=gkvbc[:], op0=ALU.mult, op1=ALU.mult), lf_.T + ls.T + gkvbc.T, ckvf[k2].T)
                                      pool(lambda: nc.gpsimd.tensor_copy(out=ckvb[k2][:], in_=ckvf[k2][:]), ckvf[k2].T, ckvb[k2].T)
                                      pool(lambda: nc.gpsimd.tensor_copy(out=kr[:, ti, :], in_=lf_[:, 640:672]), lf_.T, [kr.T[ti]])
                                      if not isS:
                                          fw.dma('sp', o_ckv[ti * 128:(ti + 1) * 128, :], ckvf[k2][:], reads=ckvf[k2].T)
                                          fw.dma('sp', o_kr[ti * 128:(ti + 1) * 128, :], lf_[:, 640:672], reads=lf_.T)
                                      pt = nxt('ptr')
                                      for c in range(2):
                                          tr(pt[:, c * 128:(c + 1) * 128], ckvb[k2][:, c * 128:(c + 1) * 128], identb[:], ckvb[k2].T + identb.T, pt.T)
                                      act(lambda: nc.scalar.copy(out=ckvT[:, :, ti * 128:(ti + 1) * 128], in_=pt[:, 0:256].rearrange("p (c t) -> p c t", c=2)), pt.T, [ckvT.T[ti]])
                              NG = NT // 4
                              if NHT == 2:
                                  norms(0)
                                  for grp in range(NG):
                                      if grp + 1 < NG:
                                          norms(grp + 1)
                                      projs(grp)
                              else:
                                  for grp in range(NG):
                                      norms(grp)
                                      projs(grp)
                              if isS:
                                  for j in range(2):
                                      ti = 16 + j
                                      fw.dma('sp', ckvf[j][:], ctx_ckv[j * 128:(j + 1) * 128, :], writes=ckvf[j].T)
                                      pool(lambda: nc.gpsimd.tensor_copy(out=ckvb[j][:], in_=ckvf[j][:]), ckvf[j].T, ckvb[j].T)
                                      fw.dma('sp', kr[:, ti, :], ctx_kr[j * 128:(j + 1) * 128, :], writes=[kr.T[ti]])
                                      pt = nxt('ptr')
                                      for c in range(2):
                                          tr(pt[:, c * 128:(c + 1) * 128], ckvb[j][:, c * 128:(c + 1) * 128], identb[:], ckvb[j].T + identb.T, pt.T)
                                      act(lambda: nc.scalar.copy(out=ckvT[:, :, ti * 128:(ti + 1) * 128], in_=pt[:, 0:256].rearrange("p (c t) -> p c t", c=2)), pt.T, [ckvT.T[ti]])

                          chk(1 + (0 if isS else 10))
                          set_psum(1, 1, 5)
                          with scope() as es5:
                              wq = B(es5, "wq", [128, 3, 768], BF16)
                              wkv = B(es5, "wkv", [128, 2, 1024], BF16)
                              fw.dma('pool', wq[:], w_q_up.rearrange("(c p) n -> p c n", p=128), writes=wq.T)
                              fw.dma('pool', wkv[:], w_kv_up.rearrange("(c p) n -> p c n", p=128), writes=wkv.T)
                              KT = B(es5, "KT", [128, 4, NK * 128], BF16, NK)
                              V1 = B(es5, "V1", [128, NK, 4, 128], BF16, NK)
                              ob = B(es5, "ob", [128, 2, 128], BF16)
                              rcpb = B(es5, "rcpb", [128, 512], F32)
                              QT = B(es5, "QT", [128, 4, 512], BF16, 4)
                              cs = B(es5, "cs", [128, 16, 64], F32)
                              kf = [B(es5, "kf%d" % i, [128, 2, 4, 96], F32) for i in range(2)]
                              sqf = B(es5, "sqf", [128, 2, 4, 96], F32)
                              knb = [B(es5, "knb%d" % i, [128, 2, 4, 128], BF16) for i in range(2)]
                              for kb_ in knb:
                                  dve(lambda: nc.vector.memset(kb_[:], 0.0), [], kb_.T)
                              hss = [B(es5, "hss%d" % i, [128, 8], F32) for i in range(2)]
                              rt = B(es5, "rt", [128, 2, 2, 4, 32], F32)
                              Eb = [B(es5, "Eb%d" % i, [128, 512], BF16) for i in range(3)]
                              dve(lambda: nc.vector.memset(V1[:], 0.0), [], V1.T)
                              dve(lambda: nc.vector.memset(ob[:], 0.0), [], ob.T)
                              dve(lambda: nc.vector.memset(ob[:, 0, 0:64], 1.0), [], ob.T)
                              dve(lambda: nc.vector.memset(ob[:, 1, 64:128], 1.0), [], ob.T)
                              if isS:
                                  fw.dma('sp', cs[:], rope_cs.rearrange("(j p) d -> p j d", p=128), writes=cs.T)
                              ecnt = [0]
                              stc = [0]
                              gq2 = B(es5, "gq2", [128, 96], F32)
                              dve(lambda: nc.vector.tensor_copy(out=gq2[:], in_=gqhbc[:]), gqhbc.T, gq2.T)
                              dve(lambda: nc.vector.tensor_tensor(out=gq2[:, 0:64], in0=gq2[:, 0:64], in1=gkhbc[:, 0:64], op=ALU.mult), gq2.T + gkhbc.T, gq2.T)

                              def norm_a(psv, src_T, kr_t0):
                                  i2 = ecnt[0] % 2
                                  ecnt[0] += 1
                                  f, s, ss_ = kf[i2], sqf, hss[i2]
                                  if kr_t0 is not None:
                                      dve(lambda: nc.vector.tensor_copy(out=f[:, :, :, 0:64], in_=psv[:, :, :, 0:64]), src_T, f.T)
                                      dve(lambda: nc.vector.tensor_copy(out=f[:, :, :, 64:96], in_=kr[:, kr_t0:kr_t0 + 2, :].unsqueeze(2).to_broadcast([128, 2, 4, 32])), [kr.T[kr_t0], kr.T[kr_t0 + 1]] + f.T, f.T)
                                  else:
                                      dve(lambda: nc.vector.tensor_copy(out=f[:], in_=psv), src_T, f.T)
                                  act(lambda: nc.scalar.activation(out=s[:], in_=f[:], func=AF.Square), f.T, s.T)
                                  dve(lambda: nc.vector.tensor_reduce(out=ss_[:], in_=s[:].rearrange("p t h d -> p (t h) d"), axis=AX.X, op=ALU.add), s.T, ss_.T)
                                  act(lambda: nc.scalar.activation(out=ss_[:], in_=ss_[:], func=AF.Ln, bias=EPS, scale=1.0 / 96), ss_.T, ss_.T)
                                  act(lambda: nc.scalar.activation(out=ss_[:], in_=ss_[:], func=AF.Exp, scale=-0.5), ss_.T, ss_.T)
                                  return i2

                              def norm_b(i2, kind, rope_t0):
                                  f, o, ss_, r_ = kf[i2], knb[i2], hss[i2], rt
                                  f8 = f[:].rearrange("p t h d -> p (t h) d")
                                  o8 = o[:].rearrange("p t h d -> p (t h) d")
                                  tb = lambda a, b: cs[:, rope_t0:rope_t0 + 2, a:b].unsqueeze(2).to_broadcast([128, 2, 4, b - a])
                                  if kind == 'k':
                                      dve(lambda: nc.vector.tensor_tensor(out=o8[:, :, 0:64], in0=f8[:, :, 0:64], in1=ss_[:].unsqueeze(2).to_broadcast([128, 8, 64]), op=ALU.mult), f.T + ss_.T, o.T)
                                      dve(lambda: nc.vector.tensor_tensor(out=f8[:, :, 64:96], in0=f8[:, :, 64:96], in1=ss_[:].unsqueeze(2).to_broadcast([128, 8, 32]), op=ALU.mult), f.T + ss_.T, f.T)
                                      gkr = gkhbc[:, 64:96].unsqueeze(1).to_broadcast([128, 8, 32])
                                      if rope_t0 is None:
                                          dve(lambda: nc.vector.tensor_tensor(out=o8[:, :, 64:96], in0=f8[:, :, 64:96], in1=gkr, op=ALU.mult), f.T + gkhbc.T, o.T)
                                          return o
                                      dve(lambda: nc.vector.tensor_tensor(out=f8[:, :, 64:96], in0=f8[:, :, 64:96], in1=gkr, op=ALU.mult), f.T + gkhbc.T, f.T)
                                  else:
                                      dve(lambda: nc.vector.tensor_tensor(out=f8, in0=f8, in1=ss_[:].unsqueeze(2).to_broadcast([128, 8, 96]), op=ALU.mult), f.T + ss_.T, f.T)
                                      gb = gq2[:].unsqueeze(1).to_broadcast([128, 8, 96])
                                      if rope_t0 is None:
                                          dve(lambda: nc.vector.tensor_tensor(out=o8[:, :, 0:96], in0=f8, in1=gb, op=ALU.mult), f.T + gq2.T, o.T)
                                          return o
                                      dve(lambda: nc.vector.tensor_tensor(out=o8[:, :, 0:64], in0=f8[:, :, 0:64], in1=gq2[:, 0:64].unsqueeze(1).to_broadcast([128, 8, 64]), op=ALU.mult), f.T + gq2.T, o.T)
                                      dve(lambda: nc.vector.tensor_tensor(out=f8[:, :, 64:96], in0=f8[:, :, 64:96], in1=gq2[:, 64:96].unsqueeze(1).to_broadcast([128, 8, 32]), op=ALU.mult), f.T + gq2.T, f.T)
                                  dve(lambda: nc.vector.tensor_tensor(out=r_[:, 0], in0=f[:, :, :, 64:96], in1=tb(0, 32), op=ALU.mult), f.T + cs.T, r_.T)
                                  dve(lambda: nc.vector.tensor_tensor(out=r_[:, 1, :, :, 0:16], in0=f[:, :, :, 80:96], in1=tb(32, 48), op=ALU.mult), f.T + cs.T, r_.T)
                                  dve(lambda: nc.vector.tensor_tensor(out=r_[:, 1, :, :, 16:32], in0=f[:, :, :, 64:80], in1=tb(48, 64), op=ALU.mult), f.T + cs.T, r_.T)
                                  dve(lambda: nc.vector.tensor_tensor(out=o[:, :, :, 64:96], in0=r_[:, 0], in1=r_[:, 1], op=ALU.add), r_.T, o.T)
                                  return o

                              def to_T(o, dst_ap, dst_T):
                                  pt = nxt('ptr')
                                  for t in range(2):
                                      for h in range(4):
                                          j = t * 4 + h
                                          tr(pt[:, j * 128:(j + 1) * 128], o[:, t, h, :], identb[:], o.T + identb.T, pt.T)
                                  act(lambda: nc.scalar.copy(out=dst_ap, in_=pt[:, :].rearrange("p (t h k) -> p t h k", t=2, h=4)), pt.T, dst_T)

                              for hg in range(2):
                                  def k_a(kp):
                                      t0 = 2 * kp
                                      pb = nxt('pbig')
                                      for t in range(2):
                                          for c in range(2):
                                              mm(pb[:, t * 512:(t + 1) * 512], ckvT[:, c, (t0 + t) * 128:(t0 + t + 1) * 128], wkv[:, c, hg * 512:(hg + 1) * 512], c == 0, c == 1, [ckvT.T[t0 + t]] + wkv.T, pb.T)
                                      psv = pb[:, :].rearrange("p (t h d) -> p t h d", t=2, h=4)
                                      dve(lambda: nc.vector.tensor_copy(out=V1[:, t0:t0 + 2, 0::2, 0:64], in_=psv[:, :, 0::2, 64:128]), pb.T, [V1.T[t0], V1.T[t0 + 1]])
                                      dve(lambda: nc.vector.tensor_copy(out=V1[:, t0:t0 + 2, 1::2, 64:128], in_=psv[:, :, 1::2, 64:128]), pb.T, [V1.T[t0], V1.T[t0 + 1]])
                                      return norm_a(psv, pb.T, t0)

                                  def k_b(kp, i2):
                                      t0 = 2 * kp
                                      o = norm_b(i2, 'k', (t0 if (isS and t0 < 16) else None))
                                      to_T(o, KT[:, :, t0 * 128:(t0 + 2) * 128].rearrange("p h (t k) -> p t h k", t=2), [KT.T[t0], KT.T[t0 + 1]])

                                  chk(1.1 + (0 if isS else 10))
                                  st_ = k_a(0)
                                  for kp in range(NK // 2):
                                      nst = k_a(kp + 1) if kp + 1 < NK // 2 else None
                                      k_b(kp, st_)
                                      st_ = nst
                                  for qb in range(2):
                                      chk(1.4 + (0 if isS else 10))
                                      def q_a(tp):
                                          ti0 = qb * 4 + tp * 2
                                          pb = nxt('pbig')
                                          for t in range(2):
                                              for c in range(3):
                                                  mm(pb[:, t * 512:t * 512 + 384], qlnT[:, c, (ti0 + t) * 128:(ti0 + t + 1) * 128], wq[:, c, hg * 384:(hg + 1) * 384], c == 0, c == 2, [qlnT.T[ti0 + t]] + wq.T, pb.T)
                                          psv = pb[:, :].rearrange("p (t x) -> p t x", t=2)[:, :, 0:384].rearrange("p t (h d) -> p t h d", h=4)
                                          return norm_a(psv, pb.T, None)

                                      def q_b(tp, i2):
                                          ti0 = qb * 4 + tp * 2
                                          o = norm_b(i2, 'q', (ti0 if isS else None))
                                          to_T(o, QT[:, :, tp * 256:(tp + 1) * 256].rearrange("p h (t k) -> p t h k", t=2), [QT.T[tp * 2], QT.T[tp * 2 + 1]])

                                      s0 = q_a(0)
                                      s1 = q_a(1)
                                      q_b(0, s0)
                                      q_b(1, s1)
                                      if isS:
                                          sgroups = [(list(range(4)), list(range(18)))]
                                      else:
                                          sgroups = [([0, 1], [4 * qb, 4 * qb + 1]), ([2, 3], [4 * qb + 2, 4 * qb + 3])]
                                      chk(1.6 + (0 if isS else 10))
                                      stb = psm[0:3]
                                      pO, pDn = psm[3], psm[4]
                                      for c2 in range(2):
                                          for (qts, kts) in sgroups:
                                              q0, q1 = qts[0] * 128, (qts[-1] + 1) * 128
                                              items = [(hh, kt) for hh in (2 * c2, 2 * c2 + 1) for kt in kts]

                                              def emit_st(i_):
                                                  hh_, kt_ = items[i_]
                                                  ps_ = stb[stc[0] % 3]
                                                  stc[0] += 1
                                                  mm(ps_[:, q0:q1], KT[:, hh_, kt_ * 128:(kt_ + 1) * 128], QT[:, hh_, q0:q1], True, True, [KT.T[kt_]] + [QT.T[q] for q in qts], ps_.T)
                                                  return ps_
                                              DEPTH = 2
                                              pend = [emit_st(i_) for i_ in range(min(DEPTH, len(items)))]
                                              for i_, (hh, kt) in enumerate(items):
                                                  ps = pend.pop(0)
                                                  E = Eb[ecnt[0] % 3]
                                                  ecnt[0] += 1
                                                  act(lambda: nc.scalar.activation(out=E[:, q0:q1], in_=ps[:, q0:q1], func=AF.Exp), ps.T, E.T)
                                                  if i_ + DEPTH < len(items):
                                                      pend.append(emit_st(i_ + DEPTH))
                                                  fst, lst = (i_ == 0), (i_ == len(items) - 1)
                                                  mm(pO[:, q0:q1], V1[:, kt, hh, :], E[:, q0:q1], fst, lst, E.T + [V1.T[kt]], pO.T)
                                                  mm(pDn[:, q0:q1], ob[:, hh % 2, :], E[:, q0:q1], fst, lst, E.T + ob.T, pDn.T)
                                              act(lambda: nc.scalar.activation(out=rcpb[:, q0:q1], in_=pDn[:, q0:q1], func=AF.Ln), pDn.T, rcpb.T)
                                              act(lambda: nc.scalar.activation(out=rcpb[:, q0:q1], in_=rcpb[:, q0:q1], func=AF.Exp, scale=-1.0), rcpb.T, rcpb.T)
                                              tq0, tq1 = qb * 512 + q0, qb * 512 + q1
                                              dve(lambda: nc.vector.tensor_tensor(out=hmaT[:, 4 + 2 * hg + c2, tq0:tq1], in0=pO[:, q0:q1], in1=rcpb[:, q0:q1], op=ALU.mult), pO.T + rcpb.T, [hmaT.T[qb * 4 + q] for q in qts])
                          set_psum(2, 2, 2)
                      chk(2 + (0 if isS else 10))
                      set_psum(0, 1, 7)
                      with scope() as es6:
                          if isS:
                              chains = [dict(d=0, units=[(t, True) for t in range(8)], seq=0),
                                        dict(d=1, units=[(t, t < 8) for t in range(15, -1, -1)], seq=0)]
                          else:
                              chains = []
                              for i in range(4):
                                  chains.append(dict(d=0, units=[(2 * i, True), (2 * i + 1, True)], seq=i))
                                  chains.append(dict(d=1, units=[(2 * i + 1, True), (2 * i, True)], seq=i))
                          NCH = len(chains)
                          NB, NR = 3, 3
                          Cst = [B(es6, "Cst%d" % i, [128, 4, 129], F32) for i in range(NCH)]
                          Cbs = [B(es6, "Cbs%d" % i, [128, 4, 129], BF16) for i in range(NCH)]
                          mrow = [[B(es6, "mrow%d_%d" % (i, j), [4, 1], F32) for j in range(3)] for i in range(NCH)]
                          rowsM = B(es6, "rowsM", [4, 16, 128], F32, 16)
                          rowsG = B(es6, "rowsG", [4, 16, 128], F32, 16)
                          TSall = B(es6, "TSall", [128, 2 * NT, 16], F32, 2 * NT)
                          gi = [B(es6, "gi%d" % i, [4, 128], F32) for i in range(NR)]
                          gl = [B(es6, "gl%d" % i, [4, 128], F32) for i in range(NR)]
                          Ar = [B(es6, "Ar%d" % i, [4, 128], F32) for i in range(NR)]
                          gsc = [B(es6, "gsc%d" % i, [4, 128], F32) for i in range(NR)]
                          Msc = [B(es6, "Msc%d" % i, [4, 128], F32) for i in range(NR)]
                          R3 = [B(es6, "R3%d" % i, [4, 3, 128], F32) for i in range(NR)]
                          sm = [B(es6, "sm%d" % i, [4, 8], F32) for i in range(NR)]
                          dd = [B(es6, "dd%d" % i, [4, 4], F32) for i in range(NR)]
                          Yd = [B(es6, "Yd%d" % i, [4, 512], F32) for i in range(NB)]
                          Dm = [B(es6, "Dm%d" % i, [128, 512], F32) for i in range(NB)]
                          Pm = [B(es6, "Pm%d" % i, [128, 512], BF16) for i in range(NB)]
                          kw = [B(es6, "kw%d" % i, [128, 512], BF16) for i in range(NB)]
                          tmpn = [B(es6, "tmpn%d" % i, [128, 4, 129], F32) for i in range(NB)]
                          dn = [B(es6, "dn%d" % i, [128, 8], F32) for i in range(NB)]
                          hsum = B(es6, "hsum", [128, 8, 512], F32, 8)
                          htmp = B(es6, "htmp", [128, 512], F32)
                          hmtok = B(es6, "hmtok", [128, 512], BF16)
                          ones4 = B(es6, "ones4", [4, 128], F32)
                          dve(lambda: nc.vector.memset(ones4[:], 1.0), [], ones4.T)
                          rcnt = [0]
                          hcnt = [0]

                          def rows_a(ci, tile, full, cur):
                              d = chains[ci]['d']
                              k = rcnt[0] % NR
                              rcnt[0] += 1
                              mold, mnew = mrow[ci][cur], mrow[ci][(cur + 1) % 3]
                              rev = (d == 1)
                              r0 = d * 4
                              cs_ = slice(tile * 128, (tile + 1) * 128)
                              gT = [gall.T[tile // 4]]
                              fw.dma('sp', gi[k][:], gall[r0:r0 + 4, cs_], reads=gT, writes=gi[k].T)
                              fw.dma('sp', gl[k][:], gall[32 + r0:32 + r0 + 4, cs_], reads=gT, writes=gl[k].T)
                              V = (lambda ap: ap[:, ::-1]) if rev else (lambda ap: ap)
                              last = 0 if rev else 127
                              A_, R_, s_ = Ar[k], R3[k], sm[k]
                              if full:
                                  slot = d * 8 + tile
                                  g_ap, g_T = rowsG[:, slot, :], [rowsG.T[slot]]
                                  M_ap, M_T = rowsM[:, slot, :], [rowsM.T[slot]]
                              else:
                                  g_ap, g_T = gsc[k][:], gsc[k].T
                                  M_ap, M_T = Msc[k][:], Msc[k].T
                              dve(lambda: nc.vector.tensor_tensor_scan(out=V(A_[:]), data0=V(ones4[:]), data1=V(gl[k][:]), initial=0.0, op0=ALU.mult, op1=ALU.add), gl[k].T + ones4.T, A_.T)
                              dve(lambda: nc.vector.tensor_tensor(out=g_ap, in0=gi[k][:], in1=A_[:], op=ALU.add), gi[k].T + A_.T, g_T)
                              dve(lambda: nc.vector.tensor_tensor_scan(out=V(M_ap), data0=V(g_ap), data1=V(g_ap), initial=mold[:, 0:1], op0=ALU.max, op1=ALU.max), g_T + mold.T, M_T)
                              dve(lambda: nc.vector.tensor_tensor(out=mnew[:, 0:1], in0=M_ap[:, last:last + 1], in1=A_[:, last:last + 1], op=ALU.subtract), M_T + A_.T, mnew.T)
                              dve(lambda: nc.vector.tensor_scalar(out=s_[:, 0:1], in0=M_ap[:, last:last + 1], scalar1=-1.0, scalar2=None, op0=ALU.mult), M_T, s_.T)
                              return (ci, tile, full, cur, k, g_ap, g_T, M_ap, M_T)

                          def rows_b(stt):
                              ci, tile, full, cur, k, g_ap, g_T, M_ap, M_T = stt
                              d = chains[ci]['d']
                              mold = mrow[ci][cur]
                              A_, R_, s_ = Ar[k], R3[k], sm[k]
                              act(lambda: nc.scalar.activation(out=s_[:, 1:2], in_=mold[:, 0:1], func=AF.Exp, bias=s_[:, 0:1], scale=1.0), mold.T + s_.T, s_.T)
                              pool(lambda: nc.gpsimd.tensor_tensor(out=R_[:, 0, :], in0=A_[:], in1=M_ap, op=ALU.subtract), A_.T + M_T, R_.T)
                              act(lambda: nc.scalar.activation(out=R_[:, 0, :], in_=R_[:, 0, :], func=AF.Exp), R_.T, R_.T)
                              act(lambda: nc.scalar.activation(out=R_[:, 1, :], in_=M_ap, func=AF.Exp, bias=mold[:, 0:1], scale=-1.0), M_T + mold.T, R_.T)
                              act(lambda: nc.scalar.activation(out=R_[:, 2, :], in_=g_ap, func=AF.Exp, bias=s_[:, 0:1], scale=1.0), g_T + s_.T, R_.T)
                              ps = nxt('psm')
                              for kk in range(3):
                                  tr(ps[:, kk * 4:(kk + 1) * 4], R_[:, kk, :], identf[0:4, 0:4], R_.T + identf.T, ps.T)
                              dve(lambda: nc.vector.tensor_scalar(out=dd[k][:], in0=identf[0:4, 0:4], scalar1=s_[:, 1:2], scalar2=None, op0=ALU.mult), identf.T + s_.T, dd[k].T)
                              mm(ps[:, 12:16], ones4[:, :], dd[k][:], True, True, ones4.T + dd[k].T, ps.T)
                              ti_ = d * NT + tile
                              dve(lambda: nc.vector.tensor_copy(out=TSall[:, ti_, :], in_=ps[:, 0:16]), ps.T, [TSall.T[ti_]])

                          PB = psm

                          def fa1(u):
                              ci, tile, full, seq_first, k = u
                              d = chains[ci]['d']
                              ti_ = d * NT + tile
                              tsT = [TSall.T[ti_]]
                              ts = lambda a, b: TSall[:, ti_, a:b]
                              pool(lambda: nc.gpsimd.tensor_tensor(out=kw[k][:].rearrange("p (h c) -> p h c", h=4), in0=ktok[:, tile, :].rearrange("p (h c) -> p h c", h=4), in1=ts(8, 12).unsqueeze(2).to_broadcast([128, 4, 128]), op=ALU.mult), [ktok.T[tile]] + tsT, kw[k].T)
                              if not full:
                                  return
                              cs_ = slice(tile * 128, (tile + 1) * 128)
                              slot = d * 8 + tile
                              M_ap, M_T = rowsM[:, slot, :], [rowsM.T[slot]]
                              g_ap, g_T = rowsG[:, slot, :], [rowsG.T[slot]]
                              pD, pS = PB[0], PB[1]
                              dve(lambda: nc.vector.tensor_tensor(out=Yd[k][:].rearrange("k (h t) -> k h t", h=4), in0=nsel4[:].rearrange("k (h t) -> k h t", h=4), in1=M_ap.unsqueeze(1).to_broadcast([4, 4, 128]), op=ALU.mult), nsel4.T + M_T, Yd[k].T)
                              mm(pD[:, :], ones4[:, :], Yd[k][:], True, False, ones4.T + Yd[k].T, pD.T)
                              mm(pD[:, :], g_ap, sel4[:], False, True, g_T + sel4.T, pD.T)
                              for h in range(4):
                                  mm(pS[:, h * 128:(h + 1) * 128], kT[:, h, cs_], qT[:, h, cs_], True, True, [kT.T[tile], qT.T[tile]], pS.T)
                              dve(lambda: nc.vector.tensor_tensor(out=Dm[k][:].rearrange("p (h t) -> p h t", h=4), in0=pD[:, :].rearrange("p (h t) -> p h t", h=4), in1=maskc[:, d, :].unsqueeze(1).to_broadcast([128, 4, 128]), op=ALU.add), pD.T + maskc.T, Dm[k].T)
                              act(lambda: nc.scalar.activation(out=Dm[k][:], in_=Dm[k][:], func=AF.Exp), Dm[k].T, Dm[k].T)
                              dve(lambda: nc.vector.tensor_tensor(out=Pm[k][:], in0=pS[:, :], in1=Dm[k][:], op=ALU.mult), pS.T + Dm[k].T, Pm[k].T)

                          def fa2(u):
                              ci, tile, full, seq_first, k = u
                              if not full:
                                  return
                              pI = [PB[0], PB[2]]
                              for h in range(4):
                                  pq = pI[h // 2]
                                  mm(pq[:, (h % 2) * 129:(h % 2) * 129 + 129], Pm[k][:, h * 128:(h + 1) * 128], v1[:, tile, h, 0:129], True, True, Pm[k].T + [v1.T[tile]], pq.T)
                              for j in range(2):
                                  act(lambda: nc.scalar.copy(out=tmpn[k][:, 2 * j:2 * j + 2, :], in_=pI[j][:, 0:258].rearrange("p (h c) -> p h c", h=2)), pI[j].T, tmpn[k].T)

                          def back(u):
                              ci, tile, full, seq_first, k = u
                              d = chains[ci]['d']
                              ti_ = d * NT + tile
                              tsT = [TSall.T[ti_]]
                              ts = lambda a, b: TSall[:, ti_, a:b]
                              C_, Cb_ = Cst[ci], Cbs[ci]
                              cs_ = slice(tile * 128, (tile + 1) * 128)
                              pX = [PB[3], PB[4]]
                              pU = [PB[5], PB[6]]
                              if full:
                                  for h in range(4):
                                      pq = pX[h // 2]
                                      mm(pq[:, (h % 2) * 129:(h % 2) * 129 + 129], qT[:, h, cs_], Cb_[:, h, :], True, True, [qT.T[tile]] + Cb_.T, pq.T)
                                  for h in range(4):
                                      pq = pX[h // 2]
                                      dve(lambda: nc.vector.scalar_tensor_tensor(out=tmpn[k][:, h, :], in0=pq[:, (h % 2) * 129:(h % 2) * 129 + 129], scalar=ts(4 + h, 5 + h), in1=tmpn[k][:, h, :], op0=ALU.mult, op1=ALU.add), pq.T + tsT + tmpn[k].T, tmpn[k].T)
                                  den = tmpn[k][:, :, 128]
                                  dve(lambda: nc.vector.scalar_tensor_tensor(out=dn[k][:, 0:4], in0=den, scalar=-1.0, in1=den, op0=ALU.mult, op1=ALU.max), tmpn[k].T, dn[k].T)
                                  dve(lambda: nc.vector.tensor_tensor(out=dn[k][:, 0:4], in0=dn[k][:, 0:4], in1=ts(0, 4), op=ALU.max), dn[k].T + tsT, dn[k].T)
                                  dve(lambda: nc.vector.reciprocal(out=dn[k][:, 4:8], in_=dn[k][:, 0:4]), dn[k].T, dn[k].T)
                                  rb = dn[k][:, 4:8].unsqueeze(2).to_broadcast([128, 4, 128])
                                  hv = hsum[:, tile, :].rearrange("p (h c) -> p h c", h=4)
                                  if seq_first:
                                      dve(lambda: nc.vector.tensor_tensor(out=hv, in0=tmpn[k][:, :, 0:128], in1=rb, op=ALU.mult), tmpn[k].T + dn[k].T, [hsum.T[tile]])
                                  else:
                                      dve(lambda: nc.vector.tensor_tensor(out=htmp[:].rearrange("p (h c) -> p h c", h=4), in0=tmpn[k][:, :, 0:128], in1=rb, op=ALU.mult), tmpn[k].T + dn[k].T, htmp.T)
                                      dve(lambda: nc.vector.tensor_tensor(out=hsum[:, tile, :], in0=hsum[:, tile, :], in1=htmp[:], op=ALU.add), htmp.T + [hsum.T[tile]], [hsum.T[tile]])
                              for h in range(4):
                                  pq = pU[h // 2]
                                  mm(pq[:, (h % 2) * 129:(h % 2) * 129 + 129], kw[k][:, h * 128:(h + 1) * 128], v1[:, tile, h, 0:129], True, True, kw[k].T + [v1.T[tile]], pq.T)
                              for h in range(4):
                                  pq = pU[h // 2]
                                  dve(lambda: nc.vector.scalar_tensor_tensor(out=C_[:, h, :], in0=C_[:, h, :], scalar=ts(12 + h, 13 + h), in1=pq[:, (h % 2) * 129:(h % 2) * 129 + 129], op0=ALU.mult, op1=ALU.add), C_.T + tsT + pq.T, C_.T)
                              act(lambda: nc.scalar.copy(out=Cb_[:], in_=C_[:]), C_.T, Cb_.T)

                          fsq = [B(es6, "fsq%d" % i, [128, 512], F32) for i in range(2)]
                          sigs = [B(es6, "sigs%d" % i, [128, 512], F32) for i in range(3)]
                          hs2s = [B(es6, "hs2s%d" % i, [128, 4], F32) for i in range(2)]

                          def fz0(tile, j):
                              pool(lambda: nc.gpsimd.tensor_tensor(out=fsq[j % 2][:], in0=hsum[:, tile, :], in1=hsum[:, tile, :], op=ALU.mult), [hsum.T[tile]], fsq[j % 2].T)
                              sg = sigs[j % 3]
                              act(lambda: nc.scalar.activation(out=sg[:], in_=motok[:, tile, :], func=AF.Exp, scale=-1.0), [motok.T[tile]], sg.T)
                              act(lambda: nc.scalar.activation(out=sg[:], in_=sg[:], func=AF.Ln, bias=1.0, scale=1.0), sg.T, sg.T)
                              act(lambda: nc.scalar.activation(out=sg[:], in_=sg[:], func=AF.Exp, scale=-1.0), sg.T, sg.T)

                          def fz1(tile, j):
                              h2 = hs2s[j % 2]
                              hv = hsum[:, tile, :].rearrange("p (h c) -> p h c", h=4)
                              dve(lambda: nc.vector.tensor_reduce(out=h2[:], in_=fsq[j % 2][:].rearrange("p (h c) -> p h c", h=4), axis=AX.X, op=ALU.add), fsq[j % 2].T, h2.T)
                              act(lambda: nc.scalar.activation(out=h2[:], in_=h2[:], func=AF.Ln, bias=EPS, scale=1.0 / 128), h2.T, h2.T)
                              act(lambda: nc.scalar.activation(out=h2[:], in_=h2[:], func=AF.Exp, scale=-0.5), h2.T, h2.T)
                              pool(lambda: nc.gpsimd.tensor_tensor(out=hv, in0=hv, in1=h2[:].unsqueeze(2).to_broadcast([128, 4, 128]), op=ALU.mult), [hsum.T[tile]] + h2.T, [hsum.T[tile]])
                              pool(lambda: nc.gpsimd.tensor_tensor(out=hsum[:, tile, :], in0=hsum[:, tile, :], in1=gmnbc[:], op=ALU.mult), [hsum.T[tile]] + gmnbc.T, [hsum.T[tile]])

                          def fz2(tile, j):
                              sg = sigs[j % 3]
                              dve(lambda: nc.vector.tensor_tensor(out=hmtok[:], in0=hsum[:, tile, :], in1=sg[:], op=ALU.mult), [hsum.T[tile]] + sg.T, hmtok.T)
                              pt = nxt('ptr')
                              for c in range(4):
                                  tr(pt[:, c * 128:(c + 1) * 128], hmtok[:, c * 128:(c + 1) * 128], identb[:], hmtok.T + identb.T, pt.T)
                              act(lambda: nc.scalar.copy(out=hmaT[:, 0:4, tile * 128:(tile + 1) * 128], in_=pt[:, 0:512].rearrange("p (c t) -> p c t", c=4)), pt.T, [hmaT.T[tile]])

                          fq = []
                          fcnt = [0]

                          def advance_fq():
                              nq = []
                              for (stg, tile_, j_) in fq:
                                  if stg == 1:
                                      fz1(tile_, j_)
                                      nq.append((2, tile_, j_))
                                  else:
                                      fz2(tile_, j_)
                              fq[:] = nq

                          for ci, ch in enumerate(chains):
                              d = ch['d']
                              Cc, mc = Cst[ci], mrow[ci][0]
                              if isS:
                                  fw.dma('sp', Cc[:, :, 0:128], st_C[d].rearrange("h k v -> k h v"), writes=Cc.T)
                                  fw.dma('sp', Cc[:, :, 128], st_n[d].rearrange("h k -> k h"), writes=Cc.T, allow_slow_non_contiguous=True)
                                  fw.dma('sp', mc[:, 0:1], st_m[d].unsqueeze(1), writes=mc.T)
                              else:
                                  dve(lambda: nc.vector.memset(Cc[:], 0.0), [], Cc.T)
                                  dve(lambda: nc.vector.memset(mc[:], 0.0), [], mc.T)
                              act(lambda: nc.scalar.copy(out=Cbs[ci][:], in_=Cc[:]), Cc.T, Cbs[ci].T)
                          chk(2.2 + (0 if isS else 10))
                          steps = max(len(ch['units']) for ch in chains)
                          curm = [0] * NCH
                          pend_r = [None]
                          for st in range(steps):
                              for ci, ch in enumerate(chains):
                                  if st < len(ch['units']):
                                      tile, full = ch['units'][st]
                                      stt_ = rows_a(ci, tile, full, curm[ci])
                                      curm[ci] = (curm[ci] + 1) % 3
                                      if pend_r[0] is not None:
                                          rows_b(pend_r[0])
                                      pend_r[0] = stt_
                          if pend_r[0] is not None:
                              rows_b(pend_r[0])
                          chk(2.5 + (0 if isS else 10))
                          visited = set()
                          ulist = []
                          pref, rest = [], []
                          for ch in chains:
                              us = ch['units']
                              j_ = 0
                              while j_ < len(us) and not us[j_][1]:
                                  j_ += 1
                              pref.append(us[:j_])
                              rest.append(us[j_:])
                          order = []
                          for ci in range(NCH):
                              order += [(ci, t_, f_) for (t_, f_) in pref[ci]]
                          for st in range(max(len(r_) for r_ in rest)):
                              for ci in range(NCH):
                                  if st < len(rest[ci]):
                                      order.append((ci,) + tuple(rest[ci][st]))
                          for (ci, tile, full) in order:
                              first = tile not in visited
                              if full and first:
                                  visited.add(tile)
                              ulist.append((ci, tile, full, first, len(ulist) % NB))
                          fa1(ulist[0])
                          fa2(ulist[0])
                          for i_, u in enumerate(ulist):
                              if i_ + 1 < len(ulist):
                                  fa1(ulist[i_ + 1])
                              back(u)
                              if i_ + 1 < len(ulist):
                                  fa2(ulist[i_ + 1])
                              advance_fq()
                              if u[2] and not u[3]:
                                  fz0(u[1], fcnt[0])
                                  fq.append((1, u[1], fcnt[0]))
                                  fcnt[0] += 1
                          while fq:
                              advance_fq()
                          if not isS:
                              for ci, ch in enumerate(chains):
                                  d, si = ch['d'], ch['seq']
                                  Cf, mf = Cst[ci], mrow[ci][curm[ci]]
                                  fw.dma('sp', o_C[si, d].rearrange("h k v -> k h v"), Cf[:, :, 0:128], reads=Cf.T)
                                  fw.dma('sp', o_n[si, d].rearrange("h k -> k h"), Cf[:, :, 128], reads=Cf.T, allow_slow_non_contiguous=True)
                                  fw.dma('sp', o_m[si, d].unsqueeze(1), mf[:, 0:1], reads=mf.T)
                      set_psum(2, 2, 2)

                  chk(3 + (0 if isS else 10))
                  xmid = B(esj, "xmid", [128, 8, 1024], F32, 8)
                  h2T = B(esj, "h2T", [128, 8, 1024], BF16, 8)
                  wu = [B(esj, "wu%d" % i, [128, 8, 1024], BF16) for i in range(2)]
                  wd = [B(esj, "wd%d" % i, [128, 8, 1024], BF16) for i in range(2)]
                  wu_v = w_up.rearrange("(c p) n -> p c n", p=128)
                  wd_v = w_dn.rearrange("(f p) n -> p f n", p=128)

                  def load_q(q):
                      fw.dma('pool', wu[q % 2][:], wu_v[:, :, q * 1024:(q + 1) * 1024], writes=wu[q % 2].T)
                      fw.dma('pool', wd[q % 2][:], wd_v[:, q * 8:(q + 1) * 8, :], writes=wd[q % 2].T)
                  with scope() as es7:
                      wo = B(es7, "wo", [128, 8, 1024], BF16)
                      fw.dma('pool', wo[:], w_out.rearrange("(c p) n -> p c n", p=128), writes=wo.T)
                      load_q(0)
                      load_q(1)
                      tmpx = [B(es7, "tmpx%d" % i, [128, 1024], F32) for i in range(2)]
                      ssb2 = [B(es7, "ssc%d" % i, [128, 4], F32) for i in range(2)]
                      xnb2 = [B(es7, "xnc%d" % i, [128, 1024], BF16) for i in range(2)]
                      def stageA(ti):
                          k2 = ti % 2
                          fw.dma('sp', xmid[:, ti, :], xin[ti * 128:(ti + 1) * 128, :], writes=[xmid.T[ti]])
                          pb = nxt('pbig')
                          for nb in range(2):
                              for c in range(8):
                                  mm(pb[:, nb * 512:(nb + 1) * 512], hmaT[:, c, ti * 128:(ti + 1) * 128], wo[:, c, nb * 512:(nb + 1) * 512], c == 0, c == 7, [hmaT.T[ti]] + wo.T, pb.T)
                          dve(lambda: nc.vector.tensor_tensor(out=tmpx[k2][:], in0=pb[:, :], in1=g1bc[:, cond, :], op=ALU.mult), pb.T + g1bc.T, tmpx[k2].T)
                          pool(lambda: nc.gpsimd.tensor_tensor(out=xmid[:, ti, :], in0=xmid[:, ti, :], in1=tmpx[k2][:], op=ALU.add), tmpx[k2].T + [xmid.T[ti]], [xmid.T[ti]])

                      def stageB(ti):
                          k2 = ti % 2
                          norm_to_hT(None, xmid[:, ti, :], [xmid.T[ti]], cond, 1, h2T[:, :, ti * 128:(ti + 1) * 128], [h2T.T[ti]], (ssb2[k2], xnb2[k2]))

                      stageA(0)
                      for ti in range(8):
                          if ti + 1 < 8:
                              stageA(ti + 1)
                          stageB(ti)

                  chk(4 + (0 if isS else 10))
                  with scope() as es8:
                      ub = [B(es8, "ub%d" % i, [128, 8, 512], BF16, 8) for i in range(2)]
                      rl = [B(es8, "rl%d" % i, [128, 512], BF16) for i in range(2)]
                      tmpy = [B(es8, "tmpy%d" % i, [128, 1024], F32) for i in range(2)]
                      ucnt = 0
                      for q in range(4):
                          wuq, wdq = wu[q % 2], wd[q % 2]
                          if 1 <= q and q + 1 < 4:
                              load_q(q + 1)
                          for g in range(2):
                              u = ub[ucnt % 2]
                              ucnt += 1
                              for fb in range(8):
                                  ps = nxt('psm')
                                  for c in range(8):
                                      mm(ps[:, :], wuq[:, c, fb * 128:(fb + 1) * 128], h2T[:, c, g * 512:(g + 1) * 512], c == 0, c == 7, wuq.T + h2T.T[g * 4:(g + 1) * 4], ps.T)
                                  r_ = rl[fb % 2]
                                  act(lambda: nc.scalar.activation(out=r_[:], in_=ps[:, :], func=AF.Relu), ps.T, r_.T)
                                  dve(lambda: nc.vector.tensor_tensor(out=u[:, fb, :], in0=r_[:], in1=r_[:], op=ALU.mult), r_.T, [u.T[fb]])
                              for tt in range(4):
                                  ti = g * 4 + tt
                                  k2 = ti % 2
                                  pb = nxt('pbig')
                                  for nb in range(2):
                                      for fb in range(8):
                                          mm(pb[:, nb * 512:(nb + 1) * 512], u[:, fb, tt * 128:(tt + 1) * 128], wdq[:, fb, nb * 512:(nb + 1) * 512], fb == 0, fb == 7, [u.T[fb]] + wdq.T, pb.T)
                                  dve(lambda: nc.vector.tensor_tensor(out=tmpy[k2][:], in0=pb[:, :], in1=g2bc[:, cond, :], op=ALU.mult), pb.T + g2bc.T, tmpy[k2].T)
                                  pool(lambda: nc.gpsimd.tensor_tensor(out=xmid[:, ti, :], in0=xmid[:, ti, :], in1=tmpy[k2][:], op=ALU.add), tmpy[k2].T + [xmid.T[ti]], [xmid.T[ti]])
                                  if q == 3:
                                      fw.dma('sp', yout[ti * 128:(ti + 1) * 128, :], xmid[:, ti, :], reads=[xmid.T[ti]])

        except _Stop:
            pass
        fw.finish()
        psum_es[0].close()
    return nc


_PROG = [None]
_DEBUG_HOOK = [None]


def _rope_tables():
    rows = 2048 // 64
    row = np.repeat(np.arange(rows, dtype=np.float32), 64)
    col = np.tile(np.arange(64, dtype=np.float32), rows)
    half = 16
    inv = (np.float32(10000.0) ** (-np.arange(0, half, 2, dtype=np.float32) / np.float32(half))).astype(np.float32)
    ang = np.concatenate([row[:, None] * inv, col[:, None] * inv], axis=-1).astype(np.float32)
    cs_, sn_ = np.cos(ang), np.sin(ang)
    return np.concatenate([cs_, cs_, -sn_, sn_], axis=-1).astype(np.float32)


def kernel(x_prompt, x_sample, cache_mla_ckv, cache_mla_krope, state_mlstm_C, state_mlstm_n, state_mlstm_m,
           c, c_ctx, norm1_g, norm2_g, w_ada, b_ada, w_in, mlstm_gate_b, mlstm_norm_g,
           q_lora_g, kv_lora_g, w_q_up, w_kv_up, q_head_g, k_head_g, w_out, w_mlp_up, w_mlp_down):
    f = lambda a: np.ascontiguousarray(np.asarray(a, dtype=np.float32))
    x_prompt, x_sample = f(x_prompt), f(x_sample)
    w_in0 = f(w_in)[0]
    if _PROG[0] is None:
        _PROG[0] = build_program()
    nc = _PROG[0]
    rope = _rope_tables()
    ident = np.eye(128, dtype=np.float32)
    s_idx = np.arange(128)[:, None]
    t_idx = np.arange(128)[None, :]
    mask = np.stack([np.where(s_idx <= t_idx, 0.0, NEG), np.where(s_idx >= t_idx, 0.0, NEG)]).astype(np.float32)
    sel = np.zeros((4, 4, 128), np.float32)
    for k in range(4):
        sel[k, k, :] = 1.0
    sel = sel.reshape(4, 512)
    w_main = np.ascontiguousarray(np.concatenate([w_in0[:, 0:2048], w_in0[:, 2064:2736]], axis=1))
    gcols = w_in0[:, 2048:2064]
    gb = f(mlstm_gate_b)[0]
    in_maps = []
    for core in range(8):
        s, p = core // 2, core % 2
        flip = (p == 1)
        xp_ = x_prompt[4 * core:4 * core + 4]
        xs_ = x_sample[s]
        stC, stn, stm = f(state_mlstm_C)[s, 0], f(state_mlstm_n)[s, 0], f(state_mlstm_m)[s, 0]
        rp = rope
        gc, gbb = gcols, gb
        if flip:
            xp_ = xp_[:, ::-1]
            xs_ = xs_[::-1]
            rp = rope[::-1]
            stC, stn, stm = stC[::-1], stn[::-1], stm[::-1]
            gc = np.concatenate([gcols[:, 8:16], gcols[:, 0:8]], axis=1)
            gbb = np.concatenate([gb[8:16], gb[0:8]])
        gc40 = np.zeros((1024, 40), np.float32)
        gb40 = np.zeros((40, 1), np.float32)
        for d_ in range(2):
            gc40[:, d_ * 4:(d_ + 1) * 4] = gc[:, d_ * 8:d_ * 8 + 4]
            gc40[:, 32 + d_ * 4:32 + (d_ + 1) * 4] = gc[:, d_ * 8 + 4:d_ * 8 + 8]
            gb40[d_ * 4:(d_ + 1) * 4, 0] = gbb[d_ * 8:d_ * 8 + 4]
            gb40[32 + d_ * 4:32 + (d_ + 1) * 4, 0] = gbb[d_ * 8 + 4:d_ * 8 + 8]
        in_maps.append({
            "xp": np.ascontiguousarray(xp_.reshape(1024, 1024)),
            "xs": np.ascontiguousarray(xs_),
            "cvec": np.ascontiguousarray(np.stack([f(c_ctx), f(c)[s]])),
            "w_ada": f(w_ada)[0], "b_ada": f(b_ada)[0][None, :],
            "norm_g": np.ascontiguousarray(np.stack([f(norm1_g)[0], f(norm2_g)[0]])),
            "w_in": w_main, "w_gate": gc40, "gate_b": gb40,
            "rope_cs": np.ascontiguousarray(rp),
            "ctx_ckv": f(cache_mla_ckv)[s, 0], "ctx_kr": f(cache_mla_krope)[s, 0],
            "st_C": np.ascontiguousarray(stC), "st_n": np.ascontiguousarray(stn), "st_m": np.ascontiguousarray(stm),
            "g_q": f(q_lora_g), "g_kv": f(kv_lora_g), "g_qh": f(q_head_g), "g_kh": f(k_head_g), "g_mn": f(mlstm_norm_g),
            "w_q_up": f(w_q_up)[0], "w_kv_up": f(w_kv_up)[0], "w_out": f(w_out)[0],
            "w_up": f(w_mlp_up)[0], "w_dn": f(w_mlp_down)[0],
            "c_ident": ident, "c_mask": mask, "c_sel": sel, "c_nsel": np.ascontiguousarray(-sel),
        })
    if _DEBUG_HOOK[0] is not None:
        return _DEBUG_HOOK[0](in_maps)
    res = run_bass_kernel_spmd(nc, in_maps, core_ids=list(range(8)))
    R = res.results
    y_p = np.zeros((32, 256, 1024), np.float32)
    y_s = np.zeros((4, 2048, 1024), np.float32)
    o_ckv = np.zeros((32, 1, 256, 256), np.float32)
    o_kr = np.zeros((32, 1, 256, 32), np.float32)
    o_C = np.zeros((32, 1, 2, 4, 128, 128), np.float32)
    o_n = np.zeros((32, 1, 2, 4, 128), np.float32)
    o_m = np.zeros((32, 1, 2, 4), np.float32)
    for core in range(8):
        s, p = core // 2, core % 2
        r = R[core]
        yp_ = r["yp"].reshape(4, 256, 1024)
        ck = r["o_ckv"].reshape(4, 256, 256)
        kr_ = r["o_kr"].reshape(4, 256, 32)
        C_, n_, m_ = r["o_C"], r["o_n"], r["o_m"]
        ys_ = r["ys"]
        if p == 1:
            yp_, ck, kr_ = yp_[:, ::-1], ck[:, ::-1], kr_[:, ::-1]
            C_, n_, m_ = C_[:, ::-1], n_[:, ::-1], m_[:, ::-1]
            y_s[s, 1024:2048] = ys_[::-1]
        else:
            y_s[s, 0:1024] = ys_
        y_p[4 * core:4 * core + 4] = yp_
        o_ckv[4 * core:4 * core + 4, 0] = ck
        o_kr[4 * core:4 * core + 4, 0] = kr_
        o_C[4 * core:4 * core + 4, 0] = C_
        o_n[4 * core:4 * core + 4, 0] = n_
        o_m[4 * core:4 * core + 4, 0] = m_
    return (y_p, y_s, o_ckv, o_kr, o_C, o_n, o_m)
```
